# Optimizing a Trainium2 kernel written in Bass

```python
import jax, jax.numpy as jnp
from jax import lax
import numpy as np

D_MODEL = 1024
BATCH = 4
SEQ = 8192
DEPTH = 1

HEAD_DIM = 64
N_Q_HEADS = 8
N_KV_HEADS = 2
Q_PER_KV = N_Q_HEADS // N_KV_HEADS
ATTN_WIDTH = N_Q_HEADS * HEAD_DIM
KV_WIDTH = N_KV_HEADS * HEAD_DIM
WINDOW = 128
BLOCK = 128
N_BUCKETS = 32
MAX_DISTANCE = 128

SSM_HEAD_DIM = 64
SSM_HEADS = 8
SSM_GROUPS = 2
HEADS_PER_GROUP = SSM_HEADS // SSM_GROUPS
SSM_WIDTH = SSM_HEADS * SSM_HEAD_DIM
D_STATE = 128
CONV_K = 4
CHUNK = 128
XBC_WIDTH = SSM_WIDTH + 2 * SSM_GROUPS * D_STATE

MIX_WIDTH = ATTN_WIDTH + SSM_WIDTH
IN_WIDTH = ATTN_WIDTH + 2 * KV_WIDTH + SSM_WIDTH + XBC_WIDTH + SSM_HEADS
D_FF = -(-8 * D_MODEL // (3 * 256)) * 256
EPS = 1e-6

kernel_name = "hymba_swa_sink_ssd_adaln_block"


def rmsnorm(x, g):
    xf = x.astype(jnp.float32)
    y = xf * lax.rsqrt(jnp.mean(xf * xf, axis=-1, keepdims=True) + EPS)
    return (y * g.astype(jnp.float32)).astype(x.dtype)


def t5_buckets(dist):
    n = np.maximum(dist, 0)
    max_exact = N_BUCKETS // 2
    large = max_exact + (np.log(np.maximum(n, 1) / max_exact) / np.log(MAX_DISTANCE / max_exact)
                         * (N_BUCKETS - max_exact)).astype(np.int32)
    large = np.minimum(large, N_BUCKETS - 1)
    return np.where(n < max_exact, n, large).astype(np.int32)


def sliding_window_attention(q, k, v, sinks, rel_bias):
    b, s, _ = q.shape
    nb = s // BLOCK
    qb = q.reshape(b, nb, BLOCK, N_KV_HEADS, Q_PER_KV, HEAD_DIM)

    def band(t):
        t = t.reshape(b, s, N_KV_HEADS, HEAD_DIM)
        t = jnp.pad(t, ((0, 0), (BLOCK, 0), (0, 0), (0, 0)))
        t = t.reshape(b, nb + 1, BLOCK, N_KV_HEADS, HEAD_DIM)
        return jnp.concatenate([t[:, :-1], t[:, 1:]], axis=2)

    kb, vb = band(k), band(v)
    dist = np.arange(BLOCK)[:, None] + BLOCK - np.arange(2 * BLOCK)[None, :]
    key_pos = np.arange(nb)[:, None] * BLOCK - BLOCK + np.arange(2 * BLOCK)[None, :]
    mask = ((dist >= 0) & (dist < WINDOW))[None] & (key_pos >= 0)[:, None, :]
    mask = mask.reshape(nb, 1, 1, BLOCK, 2 * BLOCK)
    bias = rel_bias.astype(jnp.float32)[t5_buckets(dist)]
    bias = jnp.transpose(bias, (2, 0, 1)).reshape(N_KV_HEADS, Q_PER_KV, BLOCK, 2 * BLOCK)

    scores = jnp.einsum("bnqkgd,bnskd->bnkgqs", qb, kb).astype(jnp.float32)
    scores = scores * (HEAD_DIM ** -0.5) + bias
    scores = jnp.where(mask, scores, -jnp.inf)
    sink = sinks.astype(jnp.float32).reshape(N_KV_HEADS, Q_PER_KV, 1, 1)
    m = jnp.maximum(jnp.max(scores, axis=-1, keepdims=True), sink)
    p = jnp.exp(scores - m)
    denom = jnp.sum(p, axis=-1, keepdims=True) + jnp.exp(sink - m)
    out = jnp.einsum("bnkgqs,bnskd->bnkgqd", p, vb.astype(jnp.float32)) / denom
    out = jnp.transpose(out, (0, 1, 4, 2, 3, 5)).reshape(b, s, ATTN_WIDTH)
    return out.astype(q.dtype)


def ssd_scan(xs, dt, A, Bm, Cm, D_skip):
    b, s = xs.shape[:2]
    nc = s // CHUNK
    xs = xs.astype(jnp.float32)
    xdt = xs * dt[..., None]
    xc = xdt.reshape(b, nc, CHUNK, SSM_GROUPS, HEADS_PER_GROUP, SSM_HEAD_DIM)
    Bc = Bm.astype(jnp.float32).reshape(b, nc, CHUNK, SSM_GROUPS, D_STATE)
    Cc = Cm.astype(jnp.float32).reshape(b, nc, CHUNK, SSM_GROUPS, D_STATE)
    dtA = (dt * A).reshape(b, nc, CHUNK, SSM_GROUPS, HEADS_PER_GROUP)
    Acs = jnp.cumsum(jnp.moveaxis(dtA, 2, -1), axis=-1)

    causal = np.tril(np.ones((CHUNK, CHUNK), dtype=bool))
    seg = Acs[..., :, None] - Acs[..., None, :]
    Lmat = jnp.exp(jnp.where(causal, seg, -jnp.inf))
    CB = jnp.einsum("bclgn,bcsgn->bcgls", Cc, Bc)
    W = CB[:, :, :, None] * Lmat
    y_diag = jnp.einsum("bcgrls,bcsgrp->bclgrp", W, xc)

    decay_states = jnp.exp(Acs[..., -1:] - Acs)
    states = jnp.einsum("bclgn,bcgrl,bclgrp->bcgrpn", Bc, decay_states, xc)
    chunk_decay = jnp.exp(Acs[..., -1])

    def step(h, inp):
        s_c, d_c = inp
        return h * d_c[..., None, None] + s_c, h

    h0 = jnp.zeros((b, SSM_GROUPS, HEADS_PER_GROUP, SSM_HEAD_DIM, D_STATE), jnp.float32)
    _, prev = lax.scan(step, h0, (jnp.moveaxis(states, 1, 0), jnp.moveaxis(chunk_decay, 1, 0)))
    prev = jnp.moveaxis(prev, 0, 1)
    y_off = jnp.einsum("bclgn,bcgrpn,bcgrl->bclgrp", Cc, prev, jnp.exp(Acs))

    y = (y_diag + y_off).reshape(b, s, SSM_GROUPS, HEADS_PER_GROUP, SSM_HEAD_DIM)
    y = y + D_skip.astype(jnp.float32)[:, :, None] * xs
    return y


def hybrid_mixer(h, w_in, conv_w, conv_b, dt_bias, A_log, D_skip, sinks,
                 attn_out_norm, ssm_out_norm, w_o, rel_bias):
    b, s, _ = h.shape
    proj = h @ w_in
    o1 = ATTN_WIDTH
    o2 = o1 + KV_WIDTH
    o3 = o2 + KV_WIDTH
    o4 = o3 + SSM_WIDTH
    o5 = o4 + XBC_WIDTH
    q, k, v = proj[..., :o1], proj[..., o1:o2], proj[..., o2:o3]
    z, xbc, dt_raw = proj[..., o3:o4], proj[..., o4:o5], proj[..., o5:]

    y_attn = sliding_window_attention(q, k, v, sinks, rel_bias)
    y_attn = rmsnorm(y_attn, attn_out_norm)

    xbc = lax.conv_general_dilated(xbc, conv_w[:, None, :], window_strides=(1,),
                                   padding=[(CONV_K - 1, 0)],
                                   dimension_numbers=("NWC", "WIO", "NWC"),
                                   feature_group_count=XBC_WIDTH)
    xbc = jax.nn.silu(xbc + conv_b)
    xs = xbc[..., :SSM_WIDTH].reshape(b, s, SSM_GROUPS, HEADS_PER_GROUP, SSM_HEAD_DIM)
    Bm = xbc[..., SSM_WIDTH:SSM_WIDTH + SSM_GROUPS * D_STATE].reshape(b, s, SSM_GROUPS, D_STATE)
    Cm = xbc[..., SSM_WIDTH + SSM_GROUPS * D_STATE:].reshape(b, s, SSM_GROUPS, D_STATE)
    dt = jax.nn.softplus(dt_raw.astype(jnp.float32) + dt_bias.astype(jnp.float32))
    dt = dt.reshape(b, s, SSM_GROUPS, HEADS_PER_GROUP)
    A = -jnp.exp(A_log.astype(jnp.float32)).reshape(SSM_GROUPS, HEADS_PER_GROUP)
    y_ssm = ssd_scan(xs, dt, A, Bm, Cm, D_skip.reshape(SSM_GROUPS, HEADS_PER_GROUP))
    y_ssm = y_ssm.reshape(b, s, SSM_GROUPS, SSM_WIDTH // SSM_GROUPS)
    gz = jax.nn.silu(z.astype(jnp.float32)).reshape(b, s, SSM_GROUPS, SSM_WIDTH // SSM_GROUPS)
    y_ssm = rmsnorm(y_ssm * gz, ssm_out_norm.reshape(SSM_GROUPS, SSM_WIDTH // SSM_GROUPS))
    y_ssm = y_ssm.reshape(b, s, SSM_WIDTH).astype(h.dtype)

    return jnp.concatenate([y_attn, y_ssm], axis=-1) @ w_o


def swiglu(h, w_gate_up, w_down):
    gu = h @ w_gate_up
    g, u = gu[..., :D_FF], gu[..., D_FF:]
    return (jax.nn.silu(g) * u) @ w_down


def setup_inputs(seed: int = 0) -> dict:
    key = jax.random.key(seed)
    ks = jax.random.split(key, 24)
    f32 = jnp.float32
    nrm = lambda k, shape, sc: jax.random.normal(k, shape, f32) * sc
    dt = jnp.exp(jax.random.uniform(ks[8], (DEPTH, SSM_HEADS), f32)
                 * (jnp.log(0.1) - jnp.log(0.001)) + jnp.log(0.001))
    return {
        "x": nrm(ks[0], (BATCH, SEQ, D_MODEL), 1.0),
        "c": nrm(ks[1], (BATCH, D_MODEL), 1.0),
        "ada_w": nrm(ks[2], (DEPTH, D_MODEL, 6 * D_MODEL), D_MODEL ** -0.5),
        "ada_b": nrm(ks[3], (DEPTH, 6 * D_MODEL), 0.01),
        "norm1": 1.0 + nrm(ks[4], (DEPTH, D_MODEL), 0.01),
        "w_in": nrm(ks[5], (DEPTH, D_MODEL, IN_WIDTH), D_MODEL ** -0.5),
        "conv_w": nrm(ks[6], (DEPTH, CONV_K, XBC_WIDTH), CONV_K ** -0.5),
        "conv_b": nrm(ks[7], (DEPTH, XBC_WIDTH), 0.01),
        "dt_bias": dt + jnp.log(-jnp.expm1(-dt)),
        "A_log": jnp.log(jax.random.uniform(ks[9], (DEPTH, SSM_HEADS), f32, 1.0, 16.0)),
        "D_skip": 1.0 + nrm(ks[10], (DEPTH, SSM_HEADS), 0.1),
        "sinks": nrm(ks[11], (DEPTH, N_Q_HEADS), 0.5),
        "attn_out_norm": 1.0 + nrm(ks[12], (DEPTH, ATTN_WIDTH), 0.01),
        "ssm_out_norm": 1.0 + nrm(ks[13], (DEPTH, SSM_WIDTH), 0.01),
        "w_o": nrm(ks[14], (DEPTH, MIX_WIDTH, D_MODEL), MIX_WIDTH ** -0.5),
        "norm2": 1.0 + nrm(ks[15], (DEPTH, D_MODEL), 0.01),
        "w_gate_up": nrm(ks[16], (DEPTH, D_MODEL, 2 * D_FF), D_MODEL ** -0.5),
        "w_down": nrm(ks[17], (DEPTH, D_FF, D_MODEL), D_FF ** -0.5),
        "rel_bias": nrm(ks[18], (N_BUCKETS, N_Q_HEADS), 0.5),
        "final_norm": 1.0 + nrm(ks[19], (D_MODEL,), 0.01),
    }


def reference(x, c, ada_w, ada_b, norm1, w_in, conv_w, conv_b, dt_bias, A_log, D_skip,
              sinks, attn_out_norm, ssm_out_norm, w_o, norm2, w_gate_up, w_down,
              rel_bias, final_norm):
    cond = jax.nn.silu(c)
    for l in range(DEPTH):
        mod = (cond @ ada_w[l] + ada_b[l])[:, None, :]
        shift1, scale1, gate1, shift2, scale2, gate2 = jnp.split(mod, 6, axis=-1)
        h = rmsnorm(x, norm1[l]) * (1.0 + scale1) + shift1
        x = x + gate1 * hybrid_mixer(h, w_in[l], conv_w[l], conv_b[l], dt_bias[l], A_log[l],
                                     D_skip[l], sinks[l], attn_out_norm[l], ssm_out_norm[l],
                                     w_o[l], rel_bias)
        h = rmsnorm(x, norm2[l]) * (1.0 + scale2) + shift2
        x = x + gate2 * swiglu(h, w_gate_up[l], w_down[l])
    return rmsnorm(x, final_norm)
```

```python
import contextlib
import numpy as np
import concourse.bass as bass
import concourse.mybir as mybir
from concourse.bass_utils import run_bass_kernel_spmd

F32 = mybir.dt.float32
BF16 = mybir.dt.bfloat16
AF = mybir.ActivationFunctionType
ALU = mybir.AluOpType
AX = mybir.AxisListType

ENGS = ("pe", "act", "dve", "pool", "sp")

D_MODEL = 1024
IN_WIDTH = 2312
D_FF = 2816
NEG = -30000.0


class DmaSem:
    def __init__(self, handle):
        self.h = handle
        self.count = 0


class Prog:
    def __init__(self, nc, stack):
        self.nc = nc
        self.stack = stack
        self.q = {e: [] for e in ENGS}
        self.cnt = {e: 0 for e in ENGS}
        self.esem = {e: stack.enter_context(nc.semaphore("es_" + e)) for e in ENGS if e != "sp"}
        self.waited = {e: {} for e in ENGS}
        self.lastw = {}
        self.readers = {}
        self.nsem = 0
        self.dma_sems = []

    def dma_sem(self, name=None):
        self.nsem += 1
        s = DmaSem(self.stack.enter_context(self.nc.semaphore(name or ("ds%d" % self.nsem))))
        self.dma_sems.append(s)
        return s

    def _collect(self, eng, reads, writes):
        waits = {}

        def need(ev, raw):
            if ev is None:
                return
            sem, val, src = ev
            if src == eng:
                if eng == "pe":
                    return
                if not raw:
                    return
            key = id(sem)
            if self.waited[eng].get(key, 0) >= val:
                return
            if key not in waits or waits[key][1] < val:
                waits[key] = (sem, val)

        for r in reads:
            need(self.lastw.get(r), True)
        for w in writes:
            need(self.lastw.get(w), False)
            for ev in self.readers.get(w, {}).values():
                need(ev, False)
        for key, (sem, val) in waits.items():
            self.waited[eng][key] = val
        return list(waits.values())

    def _record(self, ev, reads, writes):
        for r in reads:
            d = self.readers.setdefault(r, {})
            k = id(ev[0])
            if k not in d or d[k][1] < ev[1]:
                d[k] = ev
        for w in writes:
            self.lastw[w] = ev
            self.readers[w] = {}

    def op(self, eng, fn, reads=(), writes=()):
        waits = self._collect(eng, reads, writes)
        self.cnt[eng] += 1
        val = self.cnt[eng]
        sem = self.esem[eng]

        def emit(e, waits=waits, fn=fn, sem=sem):
            for (s, v) in waits:
                e.wait_ge(s, v)
            ins = fn(e)
            ins.then_inc(sem, 1)

        self.q[eng].append(emit)
        self._record((sem, val, eng), reads, writes)

    def dma(self, eng, fns, dsem, reads=(), writes=()):
        waits = self._collect(eng, reads, writes)
        dsem.count += 16 * len(fns)
        val = dsem.count

        def emit(e, waits=waits, fns=fns, h=dsem.h):
            for (s, v) in waits:
                e.wait_ge(s, v)
            for fn in fns:
                fn(e).then_inc(h, 16)

        self.q[eng].append(emit)
        self._record((dsem.h, val, "dma"), reads, writes)

    def finish(self, eng="sp"):
        waits = []
        for e in ENGS:
            if e != "sp" and self.cnt[e] > 0:
                waits.append((self.esem[e], self.cnt[e]))
        for s in self.dma_sems:
            if s.count > 0:
                waits.append((s.h, s.count))

        def emit(e, waits=waits):
            for (s, v) in waits:
                e.wait_ge(s, v)

        self.q[eng].append(emit)

    def emit_all(self):
        with self.nc.Block() as block:
            @block.tensor
            def _(e):
                for f in self.q["pe"]:
                    f(e)

            @block.scalar
            def _(e):
                for f in self.q["act"]:
                    f(e)

            @block.vector
            def _(e):
                for f in self.q["dve"]:
                    f(e)

            @block.gpsimd
            def _(e):
                for f in self.q["pool"]:
                    f(e)

            @block.sync
            def _(e):
                for f in self.q["sp"]:
                    f(e)


def build_nc(ntok, dbg=False):
    NG = ntok // 256
    DBG = {}
    nc = bass.Bass("TRN2", target_bir_lowering=False)

    def din(name, shape, dt=F32):
        return nc.dram_tensor(name, list(shape), dt, kind="ExternalInput").ap()

    x_main = din("x_main", [ntok, 1024])
    x_pre = din("x_pre", [ntok, 1024])
    c_col = din("c_col", [128, 8])
    ada_w = din("ada_w", [1024, 6144])
    adab_col = din("adab_col", [128, 48])
    adab_g1 = din("adab_g1", [128, 1024])
    adab_g2 = din("adab_g2", [128, 1024])
    norm1_col = din("norm1_col", [128, 8])
    norm2_col = din("norm2_col", [128, 8])
    mixnorm_col = din("mixnorm_col", [128, 8])
    w_in = din("w_in", [1024, IN_WIDTH])
    convw_col = din("convw_col", [128, 8, 4])
    convb_col = din("convb_col", [128, 8])
    dtb_b = din("dtb_b", [128, 8])
    alog_b = din("alog_b", [128, 8])
    dskip_b = din("dskip_b", [128, 8])
    sinks_b = din("sinks_b", [128, 8])
    w_o = din("w_o", [1024, 1024])
    w_gu = din("w_gu", [1024, 2 * D_FF])
    w_dn = din("w_dn", [D_FF, 1024])
    biasf_in = din("biasf", [128, 8, 256])
    fnorm_in = din("fnorm_b", [128, 1024])
    flag_in = din("flag", [128, 1])
    out = nc.dram_tensor("out", [ntok, 1024], F32, kind="ExternalOutput").ap()
    wgu_s = nc.dram_tensor("wgu_s", [1024, 2 * D_FF], BF16).ap()
    wdn_s = nc.dram_tensor("wdn_s", [D_FF, 1024], BF16).ap()

    with contextlib.ExitStack() as st:
        P = Prog(nc, st)

        def T(name, shape, dt=F32):
            return st.enter_context(nc.sbuf_tensor(name, list(shape), dt))

        def PSUM(name, shape, dt=F32):
            return st.enter_context(nc.psum_tensor(name, list(shape), dt))

        w_in_bf = T("w_in_bf", [128, 8, IN_WIDTH], BF16)
        w_o_bf = T("w_o_bf", [128, 8, 1024], BF16)
        wgu = [T("wgu%d" % i, [128, 8, 512], BF16) for i in range(2)]
        wdn = [T("wdn%d" % i, [128, 2, 1024], BF16) for i in range(2)]
        biasf = T("biasf_sb", [128, 8, 256])
        fnorm = T("fnorm_sb", [128, 1024])
        gate1_b = T("gate1_b", [128, 1024])
        gate2h_b = T("gate2h_b", [128, 1024])
        Dg = T("Dg", [128, 8, 128], BF16)
        S = T("S", [128, 512])
        Sbf = T("Sbf", [128, 512], BF16)
        ident = T("ident", [128, 128], BF16)
        identf = T("identf", [128, 128])
        tri = T("tri", [128, 128])
        ones = T("ones", [128, 128])
        maskTf = T("maskTf", [128, 128])
        maskT = T("maskT", [128, 128], BF16)
        g1 = T("g1", [128, 8]); sh1 = T("sh1", [128, 8]); g2 = T("g2", [128, 8]); sh2 = T("sh2", [128, 8])
        modc = T("modc", [128, 48])
        adabc = T("adabc", [128, 48])
        n1c = T("n1c", [128, 8]); n2c = T("n2c", [128, 8]); mnc = T("mnc", [128, 8])
        ccol = T("ccol", [128, 8]); cth = T("cth", [128, 8]); condf = T("condf", [128, 8]); condb = T("condb", [128, 8], BF16)
        convw = T("convw", [128, 8, 4]); convb = T("convb", [128, 8])
        dtb = T("dtb", [128, 8]); A_b = T("A_b", [128, 8]); dsk = T("dsk", [128, 8])
        sink = T("sink", [128, 8]); nsink = T("nsink", [128, 8])
        flag = T("flag_sb", [128, 1]); maskb = T("maskb", [128, 1])
        eps1 = T("eps1", [128, 8]); eps4 = T("eps4", [128, 8]); mhalf = T("mhalf", [128, 8])
        kT = T("kT", [128, 384], BF16)
        vx = T("vx", [128, 3, 128], BF16)
        raw = T("raw", [128, 8, 259])
        xg = [T("xg%d" % i, [128, 2, 1024]) for i in range(2)]
        hT = T("hT", [128, 8, 256], BF16)
        mixT = T("mixT", [128, 8, 256], BF16)
        qT = T("qT", [128, 4, 256], BF16)
        cvo = T("cvo", [128, 8, 256], BF16)
        gz = T("gz", [128, 2, 512], BF16)
        dtu = T("dtu", [128, 16]); dtv = T("dtv", [128, 16])
        sp_t = [T("sp_t%d" % i, [128, 16]) for i in range(8)]
        xn = T("xn", [128, 1024], BF16)
        mixb = T("mixb", [128, 1024], BF16)
        sc = T("sc", [128, 4, 256])
        pb = T("pb", [128, 4, 256], BF16)
        pTs = T("pTs", [128, 8, 128], BF16)
        LT = T("LT", [128, 8, 128], BF16)
        WT = T("WT", [128, 8, 128], BF16)
        xdt = T("xdt", [128, 512], BF16); xdd = T("xdd", [128, 512], BF16); xsb = T("xsb", [128, 512], BF16)
        Btok = T("Btok", [128, 256], BF16)
        ya = T("ya", [128, 512]); yt = T("yt", [128, 512]); yg = T("yg", [128, 512])
        acc = [T("acc%d" % i, [128, 256]) for i in range(2)]
        cth2 = [T("cth2_%d" % i, [128, 256]) for i in range(2)]
        thz = T("thz", [128, 512])
        thg = [T("thg%d" % i, [128, 256]) for i in range(2)]
        t2 = [T("t2_%d" % i, [128, 256]) for i in range(2)]
        act2 = [T("act2_%d" % i, [128, 2, 256], BF16) for i in range(2)]
        ttmp = [T("ttmp%d" % i, [128, 512]) for i in range(2)]
        ms = T("ms", [128, 8]); mse = T("mse", [128, 8]); rstd = T("rstd", [128, 8])
        a_t = T("a_t", [128, 8]); e_in = T("e_in", [128, 24]); e24 = T("e24", [128, 24])
        w2 = T("w2", [128, 8]); nAcs = T("nAcs", [128, 8])
        rmax = T("rmax", [128, 4]); negm = T("negm", [128, 4]); rsum = T("rsum", [128, 4])
        stmp = T("stmp", [128, 4]); es = T("es", [128, 4]); den = T("den", [128, 4]); rden = T("rden", [128, 4])
        psT = [PSUM("psT%d" % i, [128, 1024], BF16) for i in range(2)]
        psM = [PSUM("psM%d" % i, [128, 512], F32) for i in range(6)]
        rr = {"m": 0, "t": 0}

        def nextM():
            i = rr["m"] % 6
            rr["m"] += 1
            return psM[i], "psM%d" % i

        def nextT():
            i = rr["t"] % 2
            rr["t"] += 1
            return psT[i], "psT%d" % i

        dsem_dbg = P.dma_sem("dbg") if dbg else None

        def dump(name, ap, res, cond=True):
            if not (dbg and cond) or name in DBG:
                return
            dt_ = ap.dtype
            d = nc.dram_tensor("dbg_" + name, list(ap.shape), dt_, kind="ExternalOutput").ap()
            DBG[name] = d
            P.dma("sp", [lambda e, d=d, ap=ap: e.dma_start(out=d, in_=ap)], dsem_dbg, reads=res)

        s_small = P.dma_sem("s_small")
        small = [(ccol, c_col, "ccol"), (adabc, adab_col, "adabc"), (n1c, norm1_col, "n1c"), (n2c, norm2_col, "n2c"),
                 (mnc, mixnorm_col, "mnc"), (convw, convw_col, "convw"), (convb, convb_col, "convb"),
                 (dtb, dtb_b, "dtb"), (A_b, alog_b, "A_b"), (dsk, dskip_b, "dsk"), (sink, sinks_b, "sink"),
                 (flag, flag_in, "flag"), (biasf, biasf_in, "biasf"), (fnorm, fnorm_in, "fnorm")]
        P.dma("sp", [(lambda e, d=d, s=s: e.dma_start(out=d[:], in_=s)) for d, s, _ in small], s_small,
              writes=[r for _, _, r in small])
        s_g = P.dma_sem("s_g")
        P.dma("sp", [lambda e: e.dma_start(out=xg[0][:, 0, :], in_=adab_g1),
                     lambda e: e.dma_start(out=xg[0][:, 1, :], in_=adab_g2)], s_g, writes=["x0"])

        P.op("pool", lambda e: e.memset(ones[:], 1.0), writes=["ones"])
        P.op("pool", lambda e: e.affine_select(out=tri[:], in_=ones[:], pattern=[[1, 128]], compare_op=ALU.is_ge,
                                               fill=0.0, base=0, channel_multiplier=-1), reads=["ones"], writes=["tri"])
        P.op("pool", lambda e: e.affine_select(out=identf[:], in_=tri[:], pattern=[[-1, 128]], compare_op=ALU.is_ge,
                                               fill=0.0, base=0, channel_multiplier=1), reads=["tri"], writes=["identf"])
        P.op("pool", lambda e: e.memset(maskTf[:], NEG), writes=["maskTf"])
        P.op("pool", lambda e: e.affine_select(out=maskTf[:], in_=maskTf[:], pattern=[[-1, 128]], compare_op=ALU.is_gt,
                                               fill=0.0, base=0, channel_multiplier=1), reads=["maskTf"], writes=["maskTf"])
        P.op("dve", lambda e: e.tensor_copy(out=ident[:], in_=identf[:]), reads=["identf"], writes=["ident"])
        P.op("dve", lambda e: e.tensor_copy(out=maskT[:], in_=maskTf[:]), reads=["maskTf"], writes=["maskT"])
        P.op("pool", lambda e: e.memset(eps1[:], 1e-6), writes=["eps1"])
        P.op("pool", lambda e: e.memset(eps4[:], 4e-6), writes=["eps4"])
        P.op("pool", lambda e: e.memset(mhalf[:], -0.5), writes=["mhalf"])
        P.op("pool", lambda e: e.memset(S[:], 0.0), writes=["S"])
        P.op("pool", lambda e: e.memset(Sbf[:], 0.0), writes=["Sbf"])
        P.op("pool", lambda e: e.memset(raw[:], 0.0), writes=["raw"])
        P.op("pool", lambda e: e.memset(kT[:], 0.0), writes=["kT"])
        P.op("pool", lambda e: e.memset(vx[:], 0.0), writes=["vx"])

        P.op("act", lambda e: e.activation(out=cth[:], in_=ccol[:], func=AF.Tanh, scale=0.5), reads=["ccol"], writes=["cth"])
        P.op("dve", lambda e: e.scalar_tensor_tensor(out=condf[:], in0=cth[:], scalar=1.0, in1=ccol[:], op0=ALU.add, op1=ALU.mult),
             reads=["cth", "ccol"], writes=["condf"])
        P.op("dve", lambda e: e.tensor_scalar(out=condb[:], in0=condf[:], scalar1=0.5, scalar2=None, op0=ALU.mult),
             reads=["condf"], writes=["condb"])

        s_ada = [P.dma_sem("s_ada%d" % i) for i in range(2)]
        modps, modps_r = psM[5], "psM5"
        for pc in range(12):
            sl = pc % 2
            P.dma("pool", [lambda e, pc=pc, sl=sl: e.dma_start(
                out=wgu[sl][:], in_=ada_w[:, pc * 512:(pc + 1) * 512].rearrange("(k p) n -> p k n", p=128))],
                s_ada[sl], writes=["wgu%d" % sl])
            vec = pc // 2
            if vec in (2, 5):
                pm, pr = nextM()

                def f(e, sl=sl, pm=pm):
                    for k in range(8):
                        ins = e.matmul(pm[:, 0:512], lhsT=condb[:, k:k + 1].to_broadcast([128, 128]), rhs=wgu[sl][:, k, :],
                                       start=(k == 0), stop=(k == 7))
                    return ins
                P.op("pe", f, reads=["wgu%d" % sl, "condb"], writes=[pr])
                half = pc % 2
                if vec == 2:
                    P.op("dve", lambda e, pm=pm, half=half: e.tensor_tensor(
                        out=gate1_b[:, half * 512:(half + 1) * 512], in0=pm[:, 0:512], in1=xg[0][:, 0, half * 512:(half + 1) * 512], op=ALU.add),
                        reads=[pr, "x0"], writes=["gate1_b"])
                else:
                    P.op("dve", lambda e, pm=pm, half=half: e.tensor_tensor(
                        out=gate2h_b[:, half * 512:(half + 1) * 512], in0=pm[:, 0:512], in1=xg[0][:, 1, half * 512:(half + 1) * 512], op=ALU.add),
                        reads=[pr, "x0"], writes=["gate2h_b"])
            else:
                def f(e, sl=sl, pc=pc):
                    for jj in range(4):
                        j = pc * 4 + jj
                        for k in range(8):
                            ins = e.matmul(modps[:, j:j + 1], lhsT=wgu[sl][:, k, jj * 128:(jj + 1) * 128], rhs=condb[:, k:k + 1],
                                           start=(k == 0), stop=(k == 7))
                    return ins
                P.op("pe", f, reads=["wgu%d" % sl, "condb"], writes=[modps_r])
        P.op("dve", lambda e: e.tensor_scalar(out=gate2h_b[:], in0=gate2h_b[:], scalar1=0.5, scalar2=None, op0=ALU.mult),
             reads=["gate2h_b"], writes=["gate2h_b"])
        P.op("dve", lambda e: e.tensor_tensor(out=modc[:], in0=modps[:, 0:48], in1=adabc[:], op=ALU.add),
             reads=[modps_r, "adabc"], writes=["modc"])
        P.op("dve", lambda e: e.scalar_tensor_tensor(out=g1[:], in0=modc[:, 8:16], scalar=1.0, in1=n1c[:], op0=ALU.add, op1=ALU.mult),
             reads=["modc", "n1c"], writes=["g1"])
        P.op("dve", lambda e: e.tensor_copy(out=sh1[:], in_=modc[:, 0:8]), reads=["modc"], writes=["sh1"])
        P.op("dve", lambda e: e.scalar_tensor_tensor(out=g2[:], in0=modc[:, 32:40], scalar=1.0, in1=n2c[:], op0=ALU.add, op1=ALU.mult),
             reads=["modc", "n2c"], writes=["g2"])
        P.op("dve", lambda e: e.tensor_copy(out=sh2[:], in_=modc[:, 24:32]), reads=["modc"], writes=["sh2"])

        P.op("dve", lambda e: e.tensor_scalar(out=convw[:], in0=convw[:], scalar1=0.5, scalar2=None, op0=ALU.mult), reads=["convw"], writes=["convw"])
        P.op("dve", lambda e: e.tensor_scalar(out=convb[:], in0=convb[:], scalar1=0.5, scalar2=None, op0=ALU.mult), reads=["convb"], writes=["convb"])
        P.op("act", lambda e: e.activation(out=A_b[:], in_=A_b[:], func=AF.Exp), reads=["A_b"], writes=["A_b"])
        P.op("dve", lambda e: e.tensor_scalar(out=A_b[:], in0=A_b[:], scalar1=-1.0, scalar2=None, op0=ALU.mult), reads=["A_b"], writes=["A_b"])
        P.op("dve", lambda e: e.tensor_scalar(out=nsink[:], in0=sink[:], scalar1=-1.0, scalar2=None, op0=ALU.mult), reads=["sink"], writes=["nsink"])
        P.op("dve", lambda e: e.tensor_scalar(out=maskb[:], in0=flag[:], scalar1=-NEG, scalar2=NEG, op0=ALU.mult, op1=ALU.add),
             reads=["flag"], writes=["maskb"])

        def fdg(e):
            for h in range(8):
                ins = e.tensor_scalar(out=Dg[:, h, :], in0=identf[:], scalar1=dsk[:, h:h + 1], scalar2=None, op0=ALU.mult)
            return ins
        P.op("dve", fdg, reads=["identf", "dsk"], writes=["Dg"])

        s_win = P.dma_sem("s_win")
        P.dma("pool", [(lambda e, kk=kk: e.dma_start(out=w_in_bf[:, 2 * kk:2 * kk + 2, :],
                                                     in_=w_in[kk * 256:(kk + 1) * 256, :].rearrange("(k p) n -> p k n", p=128)))
                       for kk in range(4)], s_win, writes=["w_in"])
        s_wo = P.dma_sem("s_wo")
        P.dma("pool", [(lambda e, kk=kk: e.dma_start(out=w_o_bf[:, 4 * kk:4 * kk + 4, :],
                                                     in_=w_o[kk * 512:(kk + 1) * 512, :].rearrange("(k p) n -> p k n", p=128)))
                       for kk in range(2)], s_wo, writes=["w_o_raw"])

        def fwo(e):
            for k in range(8):
                ins = e.tensor_scalar(out=w_o_bf[:, k, :], in0=w_o_bf[:, k, :], scalar1=mnc[:, k:k + 1], scalar2=None, op0=ALU.mult)
            return ins
        P.op("dve", fwo, reads=["w_o_raw", "mnc"], writes=["w_o"])
        s_scr = P.dma_sem("s_scr")
        P.dma("pool", [(lambda e, kk=kk: e.dma_start(out=wgu_s[kk * 128:(kk + 1) * 128, :], in_=w_gu[kk * 128:(kk + 1) * 128, :]))
                       for kk in range(8)] +
              [(lambda e, kk=kk: e.dma_start(out=wdn_s[kk * 704:(kk + 1) * 704, :], in_=w_dn[kk * 704:(kk + 1) * 704, :]))
               for kk in range(4)], s_scr, writes=["scr"])

        xsem = [P.dma_sem("xs%d" % i) for i in range(2)]
        osem = [P.dma_sem("os%d" % i) for i in range(2)]
        wsem = [P.dma_sem("ws%d" % i) for i in range(2)]

        def load_x(gi):
            pre = gi < NG
            src = x_pre if pre else x_main
            g = gi if pre else gi - NG
            sl = gi % 2
            P.dma("sp", [lambda e, src=src, g=g, sl=sl: e.dma_start(
                out=xg[sl][:], in_=src[g * 256:(g + 1) * 256, :].rearrange("(t p) d -> p t d", p=128))],
                xsem[sl], writes=["x%d" % sl])

        def load_w(i):
            sl = i % 2
            P.dma("sp", [lambda e, i=i, sl=sl: e.dma_start(out=wgu[sl][:, :, 0:256],
                                                          in_=wgu_s[:, i * 256:(i + 1) * 256].rearrange("(k p) n -> p k n", p=128)),
                         lambda e, i=i, sl=sl: e.dma_start(out=wgu[sl][:, :, 256:512],
                                                          in_=wgu_s[:, D_FF + i * 256:D_FF + (i + 1) * 256].rearrange("(k p) n -> p k n", p=128)),
                         lambda e, i=i, sl=sl: e.dma_start(out=wdn[sl][:],
                                                          in_=wdn_s[i * 256:(i + 1) * 256, :].rearrange("(a p) n -> p a n", p=128))],
                  wsem[sl], reads=["scr"], writes=["wgu%d" % sl, "wdn%d" % sl])

        def rms_stats(src_ap, src_res, scale, col, eps_t):
            P.op("act", lambda e: e.activation(out=xn[:, 0:src_ap.shape[-1]], in_=src_ap, func=AF.Square, scale=scale,
                                               accum_out=ms[:, col:col + 1]),
                 reads=list(src_res) if isinstance(src_res, (list, tuple)) else [src_res], writes=["xn", "ms%d" % col])
            P.op("pool", lambda e: e.tensor_tensor(out=mse[:, col:col + 1], in0=ms[:, col:col + 1], in1=eps_t[:, 0:1], op=ALU.add),
                 reads=["ms%d" % col], writes=["mse%d" % col])
            P.op("pool", lambda e: e.tensor_tensor(out=rstd[:, col:col + 1], in0=mse[:, col:col + 1], in1=mhalf[:, 0:1], op=ALU.pow),
                 reads=["mse%d" % col], writes=["rstd%d" % col])

        def norm_transpose(xs_ap, xres, gvec, svec, dstT, dres, t):
            rms_stats(xs_ap, xres, 1.0 / 32.0, 0, eps1)
            P.op("dve", lambda e: e.tensor_scalar(out=xn[:], in0=xs_ap, scalar1=rstd[:, 0:1], scalar2=None, op0=ALU.mult),
                 reads=[xres, "rstd0"], writes=["xn"])
            pt, ptr = nextT()

            def ftr(e):
                for k in range(8):
                    ins = e.transpose(out=pt[:, k * 128:(k + 1) * 128], in_=xn[:, k * 128:(k + 1) * 128], identity=ident[:])
                return ins
            P.op("pe", ftr, reads=["xn", "ident"], writes=[ptr])

            def fev(e):
                for k in range(8):
                    ins = e.activation(out=dstT[:, k, t * 128:(t + 1) * 128], in_=pt[:, k * 128:(k + 1) * 128], func=AF.Identity,
                                       scale=gvec[:, k:k + 1], bias=svec[:, k:k + 1])
                return ins
            P.op("act", fev, reads=[ptr, "g1", "sh1", "g2", "sh2"], writes=[dres])

        import os
        STAGE = int(os.environ.get("KSTAGE", "9"))
        if STAGE >= 2:
            load_x(0)
        for gi in range(2 * NG):
            pre = gi < NG
            if STAGE < 2 or (STAGE == 2 and not pre):
                break
            g_local = gi if pre else gi - NG
            sl = gi % 2
            xr = "x%d" % sl
            xs = xg[sl]
            if gi + 1 < 2 * NG:
                load_x(gi + 1)
            if gi == NG:
                P.op("dve", lambda e: e.tensor_scalar(out=S[:], in0=S[:], scalar1=flag[:, 0:1], scalar2=None, op0=ALU.mult),
                     reads=["S", "flag"], writes=["S"])
                P.op("act", lambda e: e.activation(out=Sbf[:], in_=S[:], func=AF.Copy), reads=["S"], writes=["Sbf"])
                P.op("dve", lambda e: e.tensor_scalar(out=raw[:, :, 0:3], in0=raw[:, :, 0:3], scalar1=flag[:, 0:1], scalar2=None, op0=ALU.mult),
                     reads=["raw", "flag"], writes=["raw"])
            if not pre:
                load_w(0)
                load_w(1)

            for t in range(2):
                norm_transpose(xs[:, t, :], xr, g1, sh1, hT, "hT", t)

            dump("hT", hT[:], ["hT"], (not pre) and g_local == 0)
            dump("g1", g1[:], ["g1"], (not pre) and g_local == 0); dump("sh1", sh1[:], ["sh1"], (not pre) and g_local == 0)
            dump("gate1", gate1_b[:], ["gate1_b"], (not pre) and g_local == 0); dump("gate2h", gate2h_b[:], ["gate2h_b"], (not pre) and g_local == 0)
            dump("S0", S[:], ["S"], (not pre) and g_local == 0)
            def fm_chunks(cols_list, pm, pr):
                def f(e):
                    for ci, c0 in enumerate(cols_list):
                        for k in range(8):
                            ins = e.matmul(pm[:, ci * 256:(ci + 1) * 256], lhsT=w_in_bf[:, k, c0:c0 + 128], rhs=hT[:, k, :],
                                           start=(k == 0), stop=(k == 7))
                    return ins
                P.op("pe", f, reads=["w_in", "hT"], writes=[pr])

            if not pre:
                for i in range(2):
                    pm, pr = nextM()
                    fm_chunks([(2 * i) * 128, (2 * i + 1) * 128], pm, pr)
                    P.op("act", lambda e, pm=pm, i=i: e.activation(out=qT[:, 2 * i:2 * i + 2, :],
                                                                    in_=pm[:, 0:512].rearrange("p (c n) -> p c n", c=2), func=AF.Copy),
                         reads=[pr], writes=["qT"])
            pm, pr = nextM()
            fm_chunks([512], pm, pr)
            P.op("act", lambda e, pm=pm: e.activation(out=kT[:, 128:384], in_=pm[:, 0:256], func=AF.Copy), reads=[pr], writes=["kT"])
            for i in range(4):
                pm, pr = nextM()
                fm_chunks([1280 + (2 * i) * 128, 1280 + (2 * i + 1) * 128], pm, pr)
                P.op("act", lambda e, pm=pm, i=i: e.activation(out=raw[:, 2 * i:2 * i + 2, 3:259],
                                                                in_=pm[:, 0:512].rearrange("p (c n) -> p c n", c=2), func=AF.Copy),
                     reads=[pr], writes=["raw"])
            for c in range(8):
                b = c % 2
                ar = "acc%d" % b

                def fconv(e, c=c, b=b):
                    e.tensor_scalar(out=acc[b][:], in0=raw[:, c, 0:256], scalar1=convw[:, c, 0:1], scalar2=convb[:, c:c + 1],
                                    op0=ALU.mult, op1=ALU.add)
                    ins = None
                    for k in range(1, 4):
                        ins = e.scalar_tensor_tensor(out=acc[b][:], in0=raw[:, c, k:k + 256], scalar=convw[:, c, k:k + 1], in1=acc[b][:],
                                                     op0=ALU.mult, op1=ALU.add)
                    return ins
                P.op("dve", lambda e, c=c, b=b: e.tensor_scalar(out=acc[b][:], in0=raw[:, c, 0:256], scalar1=convw[:, c, 0:1],
                                                                 scalar2=convb[:, c:c + 1], op0=ALU.mult, op1=ALU.add),
                     reads=["raw", "convw", "convb"], writes=[ar])
                for k in range(1, 4):
                    P.op("dve", lambda e, c=c, b=b, k=k: e.scalar_tensor_tensor(out=acc[b][:], in0=raw[:, c, k:k + 256],
                                                                               scalar=convw[:, c, k:k + 1], in1=acc[b][:],
                                                                               op0=ALU.mult, op1=ALU.add),
                         reads=["raw", "convw", ar], writes=[ar])
                P.op("act", lambda e, b=b: e.activation(out=cth2[b][:], in_=acc[b][:], func=AF.Tanh), reads=[ar], writes=["cth2_%d" % b])
                P.op("dve", lambda e, c=c, b=b: e.scalar_tensor_tensor(out=cvo[:, c, :], in0=cth2[b][:], scalar=1.0, in1=acc[b][:],
                                                                       op0=ALU.add, op1=ALU.mult),
                     reads=["cth2_%d" % b, ar], writes=["cvo"])
            P.op("dve", lambda e: e.tensor_copy(out=raw[:, :, 0:3], in_=raw[:, :, 256:259]), reads=["raw"], writes=["raw"])

            dump("qT", qT[:], ["qT"], (not pre) and g_local == 0); dump("kT", kT[:], ["kT"], (not pre) and g_local == 0); dump("cvo", cvo[:], ["cvo"], (not pre) and g_local == 0)
            for t in range(2):
                tsl = slice(t * 128, (t + 1) * 128)
                pa, par = nextM()
                if pre:
                    def f(e, pa=pa, tsl=tsl):
                        for k in range(8):
                            e.matmul(pa[:, 0:128], lhsT=hT[:, k, tsl], rhs=w_in_bf[:, k, 640:768], start=(k == 0), stop=(k == 7))
                        for k in range(8):
                            ins = e.matmul(pa[:, 128:136], lhsT=hT[:, k, tsl], rhs=w_in_bf[:, k, 2304:2312], start=(k == 0), stop=(k == 7))
                        return ins
                    P.op("pe", f, reads=["hT", "w_in"], writes=[par])
                    P.op("act", lambda e, pa=pa, t=t: e.activation(out=vx[:, 1 + t, :], in_=pa[:, 0:128], func=AF.Copy), reads=[par], writes=["vx"])
                    P.op("dve", lambda e, pa=pa, t=t: e.tensor_tensor(out=dtu[:, t * 8:(t + 1) * 8], in0=pa[:, 128:136], in1=dtb[:], op=ALU.add),
                         reads=[par, "dtb"], writes=["dtu"])
                else:
                    pb2, pbr = nextM()

                    def f(e, pa=pa, pb2=pb2, tsl=tsl):
                        for k in range(8):
                            e.matmul(pa[:, 0:512], lhsT=hT[:, k, tsl], rhs=w_in_bf[:, k, 640:1152], start=(k == 0), stop=(k == 7))
                        for k in range(8):
                            e.matmul(pb2[:, 0:128], lhsT=hT[:, k, tsl], rhs=w_in_bf[:, k, 1152:1280], start=(k == 0), stop=(k == 7))
                        for k in range(8):
                            ins = e.matmul(pb2[:, 128:136], lhsT=hT[:, k, tsl], rhs=w_in_bf[:, k, 2304:2312], start=(k == 0), stop=(k == 7))
                        return ins
                    P.op("pe", f, reads=["hT", "w_in"], writes=[par, pbr])
                    P.op("act", lambda e, pa=pa, t=t: e.activation(out=vx[:, 1 + t, :], in_=pa[:, 0:128], func=AF.Copy), reads=[par], writes=["vx"])

                    def fth(e, pa=pa, pb2=pb2):
                        e.activation(out=thz[:, 0:384], in_=pa[:, 128:512], func=AF.Tanh, scale=0.5)
                        return e.activation(out=thz[:, 384:512], in_=pb2[:, 0:128], func=AF.Tanh, scale=0.5)
                    P.op("act", fth, reads=[par, pbr], writes=["thz"])

                    def fgz(e, pa=pa, pb2=pb2, t=t):
                        e.scalar_tensor_tensor(out=gz[:, t, 0:384], in0=thz[:, 0:384], scalar=1.0, in1=pa[:, 128:512], op0=ALU.add, op1=ALU.mult)
                        return e.scalar_tensor_tensor(out=gz[:, t, 384:512], in0=thz[:, 384:512], scalar=1.0, in1=pb2[:, 0:128],
                                                      op0=ALU.add, op1=ALU.mult)
                    P.op("dve", fgz, reads=["thz", par, pbr], writes=["gz"])
                    P.op("dve", lambda e, pb2=pb2, t=t: e.tensor_tensor(out=dtu[:, t * 8:(t + 1) * 8], in0=pb2[:, 128:136], in1=dtb[:], op=ALU.add),
                         reads=[pbr, "dtb"], writes=["dtu"])

            au, tt, dd, ww, w2s, rr_, lnp, relu = sp_t
            P.op("act", lambda e: e.activation(out=au[:], in_=dtu[:], func=AF.Abs), reads=["dtu"], writes=["sp_au"])
            P.op("act", lambda e: e.activation(out=tt[:], in_=au[:], func=AF.Exp, scale=-1.0), reads=["sp_au"], writes=["sp_tt"])
            P.op("dve", lambda e: e.tensor_scalar(out=dd[:], in0=tt[:], scalar1=2.0, scalar2=None, op0=ALU.add), reads=["sp_tt"], writes=["sp_dd"])
            P.op("dve", lambda e: e.reciprocal(out=dd[:], in_=dd[:]), reads=["sp_dd"], writes=["sp_dd"])
            P.op("dve", lambda e: e.tensor_tensor(out=ww[:], in0=tt[:], in1=dd[:], op=ALU.mult), reads=["sp_tt", "sp_dd"], writes=["sp_ww"])
            P.op("dve", lambda e: e.tensor_tensor(out=w2s[:], in0=ww[:], in1=ww[:], op=ALU.mult), reads=["sp_ww"], writes=["sp_w2"])
            P.op("dve", lambda e: e.tensor_scalar(out=rr_[:], in0=w2s[:], scalar1=1.0 / 13.0, scalar2=None, op0=ALU.mult), reads=["sp_w2"], writes=["sp_rr"])
            for cst in (1.0 / 11.0, 1.0 / 9.0, 1.0 / 7.0, 1.0 / 5.0, 1.0 / 3.0):
                P.op("dve", lambda e, cst=cst: e.scalar_tensor_tensor(out=rr_[:], in0=rr_[:], scalar=cst, in1=w2s[:], op0=ALU.add, op1=ALU.mult),
                     reads=["sp_rr", "sp_w2"], writes=["sp_rr"])
            P.op("dve", lambda e: e.scalar_tensor_tensor(out=lnp[:], in0=rr_[:], scalar=1.0, in1=ww[:], op0=ALU.add, op1=ALU.mult),
                 reads=["sp_rr", "sp_ww"], writes=["sp_ln"])
            P.op("dve", lambda e: e.tensor_scalar(out=relu[:], in0=dtu[:], scalar1=0.0, scalar2=None, op0=ALU.max), reads=["dtu"], writes=["sp_relu"])
            P.op("dve", lambda e: e.scalar_tensor_tensor(out=dtv[:], in0=lnp[:], scalar=2.0, in1=relu[:], op0=ALU.mult, op1=ALU.add),
                 reads=["sp_ln", "sp_relu"], writes=["dtv"])

            dump("vx", vx[:], ["vx"], (not pre) and g_local == 0); dump("gz", gz[:], ["gz"], (not pre) and g_local == 0); dump("dtv", dtv[:], ["dtv"], (not pre) and g_local == 0)
            for t in range(2):
                tsl = slice(t * 128, (t + 1) * 128)
                gt_first = (not pre) and g_local == 0 and t == 0
                dt_ap = dtv[:, t * 8:(t + 1) * 8]
                P.op("dve", lambda e, dt_ap=dt_ap: e.tensor_tensor(out=a_t[:], in0=dt_ap, in1=A_b[:], op=ALU.mult), reads=["dtv", "A_b"], writes=["a_t"])
                pc_, pcr = nextM()

                def fcs(e, pc_=pc_):
                    e.matmul(pc_[:, 0:8], lhsT=tri[:], rhs=a_t[:], start=True, stop=True)
                    e.matmul(pc_[:, 8:16], lhsT=ones[:], rhs=a_t[:], start=True, stop=True)
                    return e.matmul(pc_[0:64, 16:24], lhsT=ident[:, 0:64], rhs=ident[:, 0:8], start=True, stop=True)
                P.op("pe", fcs, reads=["tri", "ones", "a_t"], writes=[pcr])
                P.op("dve", lambda e, pc_=pc_: e.tensor_copy(out=e_in[:, 0:16], in_=pc_[:, 0:16]), reads=[pcr], writes=["e_in"])
                P.op("dve", lambda e: e.tensor_tensor(out=e_in[:, 16:24], in0=e_in[:, 8:16], in1=e_in[:, 0:8], op=ALU.subtract),
                     reads=["e_in"], writes=["e_in2"])
                P.op("act", lambda e: e.activation(out=e24[:], in_=e_in[:], func=AF.Exp), reads=["e_in", "e_in2"], writes=["e24"])
                P.op("dve", lambda e, dt_ap=dt_ap: e.tensor_tensor(out=w2[:], in0=dt_ap, in1=e24[:, 16:24], op=ALU.mult), reads=["dtv", "e24"], writes=["w2"])
                pt, ptr = nextT()

                def ftx(e, pt=pt, tsl=tsl):
                    for c in range(6):
                        ins = e.transpose(out=pt[:, c * 128:(c + 1) * 128], in_=cvo[:, c, tsl], identity=ident[:])
                    return ins
                P.op("pe", ftx, reads=["cvo", "ident"], writes=[ptr])
                P.op("dve", lambda e, pt=pt: e.tensor_tensor(out=xdd[:].rearrange("p (h d) -> p h d", h=8),
                                                            in0=pt[:, 0:512].rearrange("p (h d) -> p h d", h=8),
                                                            in1=w2[:].unsqueeze(2).to_broadcast([128, 8, 64]), op=ALU.mult),
                     reads=[ptr, "w2"], writes=["xdd"])
                P.op("act", lambda e, pt=pt: e.activation(out=Btok[:], in_=pt[:, 512:768], func=AF.Copy), reads=[ptr], writes=["Btok"])
                if (not pre) and STAGE not in (31, 33):
                    P.op("dve", lambda e, pt=pt, dt_ap=dt_ap: e.tensor_tensor(out=xdt[:].rearrange("p (h d) -> p h d", h=8),
                                                                              in0=pt[:, 0:512].rearrange("p (h d) -> p h d", h=8),
                                                                              in1=dt_ap.unsqueeze(2).to_broadcast([128, 8, 64]), op=ALU.mult),
                         reads=[ptr, "dtv"], writes=["xdt"])
                    P.op("act", lambda e, pt=pt: e.activation(out=xsb[:], in_=pt[:, 0:512], func=AF.Copy), reads=[ptr, "xdt", "xdd"], writes=["xsb"])
                    P.op("dve", lambda e: e.tensor_scalar(out=nAcs[:], in0=e_in[:, 0:8], scalar1=-1.0, scalar2=None, op0=ALU.mult),
                         reads=["e_in"], writes=["nAcs"])
                    pcb, pcbr = nextM()

                    def fcb(e, pcb=pcb, tsl=tsl):
                        for g in range(2):
                            ins = e.matmul(pcb[:, g * 128:(g + 1) * 128], lhsT=cvo[:, 4 + g, tsl], rhs=cvo[:, 6 + g, tsl], start=True, stop=True)
                        return ins
                    P.op("pe", fcb, reads=["cvo"], writes=[pcbr])
                    pl = [nextM(), nextM()]

                    def fl(e, pl=pl, pcb=pcb):
                        for h in range(8):
                            o = pl[h // 4][0][:, (h % 4) * 128:(h % 4 + 1) * 128]
                            ins = e.matmul(o, lhsT=a_t[:, h:h + 1].to_broadcast([128, 128]), rhs=tri[:], start=True, stop=True)
                        return e.matmul(pcb[0:64, 256:264], lhsT=ident[:, 0:64], rhs=ident[:, 0:8], start=True, stop=True)
                    P.op("pe", fl, reads=["a_t", "tri", "ident"], writes=[pl[0][1], pl[1][1]])
                    scv = sc[:].rearrange("p a b -> p (a b)").rearrange("p (h l) -> p h l", h=8)

                    def fl2(e, pl=pl, scv=scv):
                        for h in range(8):
                            ins = e.scalar_tensor_tensor(out=scv[:, h, :], in0=pl[h // 4][0][:, (h % 4) * 128:(h % 4 + 1) * 128],
                                                         scalar=nAcs[:, h:h + 1], in1=maskTf[:], op0=ALU.add, op1=ALU.add)
                        return ins
                    P.op("dve", fl2, reads=[pl[0][1], pl[1][1], "nAcs", "maskTf"], writes=["sc"])
                    P.op("act", lambda e, scv=scv: e.activation(out=LT[:], in_=scv, func=AF.Exp), reads=["sc"], writes=["LT"])

                    def fwt(e, pcb=pcb):
                        for h in range(8):
                            g = h // 4
                            ins = e.tensor_tensor(out=WT[:, h, :], in0=pcb[:, g * 128:(g + 1) * 128], in1=LT[:, h, :], op=ALU.mult)
                        return ins
                    P.op("dve", fwt, reads=[pcbr, "LT"], writes=["WT"])
                    py, pyr = nextM()

                    SKIPB = (STAGE == 34)
                    if SKIPB:
                        P.op = lambda *a, **k: None

                    def fy(e, py=py):
                        for h in range(8):
                            hs = slice(h * 64, (h + 1) * 64)
                            e.matmul(py[:, hs], lhsT=WT[:, h, :], rhs=xdt[:, hs], start=True, stop=False)
                            ins = e.matmul(py[:, hs], lhsT=Dg[:, h, :], rhs=xsb[:, hs], start=False, stop=True)
                        return ins
                    P.op("pe", fy, reads=["WT", "xdt", "Dg", "xsb"], writes=[pyr])
                    pyo, pyor = nextM()

                    def fyo(e, pyo=pyo, tsl=tsl):
                        for g in range(2):
                            ins = e.matmul(pyo[:, g * 256:(g + 1) * 256], lhsT=cvo[:, 6 + g, tsl], rhs=Sbf[:, g * 256:(g + 1) * 256], start=True, stop=True)
                        return ins
                    P.op("pe", fyo, reads=["cvo", "Sbf"], writes=[pyor])
                    P.op("dve", lambda e, pyo=pyo: e.tensor_tensor(out=yt[:].rearrange("p (h d) -> p h d", h=8),
                                                                  in0=pyo[:, 0:512].rearrange("p (h d) -> p h d", h=8),
                                                                  in1=e24[:, 0:8].unsqueeze(2).to_broadcast([128, 8, 64]), op=ALU.mult),
                         reads=[pyor, "e24"], writes=["yt"])
                    P.op("dve", lambda e, py=py: e.tensor_tensor(out=yt[:], in0=py[:, 0:512], in1=yt[:], op=ALU.add), reads=[pyr, "yt"], writes=["yt"])
                    P.op("dve", lambda e, t=t: e.tensor_tensor(out=yg[:], in0=yt[:], in1=gz[:, t, :], op=ALU.mult), reads=["yt", "gz"], writes=["yg"])
                    for g in range(2):
                        rms_stats(yg[:, g * 256:(g + 1) * 256], "yg", 1.0 / 16.0, 1 + g, eps4)

                    def fmx(e):
                        for g in range(2):
                            ins = e.tensor_scalar(out=mixb[:, 512 + g * 256:512 + (g + 1) * 256], in0=yg[:, g * 256:(g + 1) * 256],
                                                  scalar1=rstd[:, 1 + g:2 + g], scalar2=None, op0=ALU.mult)
                        return ins
                    P.op("dve", fmx, reads=["yg", "rstd1", "rstd2"], writes=["mixb_s"])
                    if SKIPB:
                        del P.op
                if (not pre) and STAGE not in (31, 33):
                    dump("e24", e24[:], ["e24"], (not pre) and g_local == 0 and t == 0); dump("LT", LT[:], ["LT"], (not pre) and g_local == 0 and t == 0)
                    dump("WT", WT[:], ["WT"], (not pre) and g_local == 0 and t == 0); dump("yt", yt[:], ["yt"], (not pre) and g_local == 0 and t == 0)
                    dump("yg", yg[:], ["yg"], (not pre) and g_local == 0 and t == 0)
                pst, pstr = nextM()

                def fst(e, pst=pst):
                    for g in range(2):
                        ins = e.matmul(pst[:, g * 256:(g + 1) * 256], lhsT=Btok[:, g * 128:(g + 1) * 128], rhs=xdd[:, g * 256:(g + 1) * 256], start=True, stop=True)
                    return ins
                P.op("pe", fst, reads=["Btok", "xdd"], writes=[pstr])
                P.op("pool", lambda e: e.tensor_tensor(out=S[:].rearrange("p (h d) -> p h d", h=8), in0=S[:].rearrange("p (h d) -> p h d", h=8),
                                                       in1=e24[:, 8:16].unsqueeze(2).to_broadcast([128, 8, 64]), op=ALU.mult),
                     reads=["S", "e24"], writes=["S"])
                P.op("dve", lambda e, pst=pst: e.tensor_tensor(out=S[:], in0=pst[:, 0:512], in1=S[:], op=ALU.add), reads=[pstr, "S"], writes=["S"])
                P.op("act", lambda e: e.activation(out=Sbf[:], in_=S[:], func=AF.Copy), reads=["S"], writes=["Sbf"])

                if pre or STAGE in (31, 32, 34):
                    continue
                for hp in range(2):
                    ksl = slice(hp * 64, (hp + 1) * 64)
                    pS = [nextM(), nextM()]

                    def fsc(e, pS=pS, ksl=ksl, t=t, tsl=tsl):
                        for c in range(4):
                            ins = e.matmul(pS[c // 2][0][:, (c % 2) * 256:(c % 2 + 1) * 256], lhsT=qT[ksl, c, tsl],
                                           rhs=kT[ksl, t * 128:t * 128 + 256], start=True, stop=True)
                        return ins
                    P.op("pe", fsc, reads=["qT", "kT"], writes=[pS[0][1], pS[1][1]])

                    def fsb(e, pS=pS, hp=hp):
                        for i in range(2):
                            ins = e.scalar_tensor_tensor(out=sc[:, 2 * i:2 * i + 2, :], in0=pS[i][0][:, 0:512].rearrange("p (c n) -> p c n", c=2),
                                                         scalar=0.125, in1=biasf[:, 4 * hp + 2 * i:4 * hp + 2 * i + 2, :], op0=ALU.mult, op1=ALU.add)
                        return ins
                    P.op("dve", fsb, reads=[pS[0][1], pS[1][1], "biasf"], writes=["sc"])
                    if gt_first:
                        P.op("dve", lambda e: e.tensor_scalar(out=sc[:, :, 0:128], in0=sc[:, :, 0:128], scalar1=maskb[:, 0:1], scalar2=None, op0=ALU.add),
                             reads=["sc", "maskb"], writes=["sc"])
                    P.op("dve", lambda e: e.tensor_reduce(out=rmax[:], in_=sc[:], axis=AX.X, op=ALU.max), reads=["sc"], writes=["rmax"])
                    P.op("dve", lambda e, hp=hp: e.scalar_tensor_tensor(out=negm[:], in0=rmax[:], scalar=-1.0, in1=nsink[:, 4 * hp:4 * hp + 4],
                                                                       op0=ALU.mult, op1=ALU.min),
                         reads=["rmax", "nsink"], writes=["negm"])

                    def fex(e):
                        for c in range(4):
                            ins = e.activation(out=pb[:, c, :], in_=sc[:, c, :], func=AF.Exp, bias=negm[:, c:c + 1], scale=1.0,
                                               accum_out=rsum[:, c:c + 1])
                        return ins
                    P.op("act", fex, reads=["sc", "negm"], writes=["pb", "rsum"])
                    P.op("dve", lambda e, hp=hp: e.tensor_tensor(out=stmp[:], in0=sink[:, 4 * hp:4 * hp + 4], in1=negm[:], op=ALU.add),
                         reads=["sink", "negm"], writes=["stmp"])
                    P.op("act", lambda e: e.activation(out=es[:], in_=stmp[:], func=AF.Exp), reads=["stmp"], writes=["es"])
                    P.op("dve", lambda e: e.tensor_tensor(out=den[:], in0=rsum[:], in1=es[:], op=ALU.add), reads=["rsum", "es"], writes=["den"])
                    P.op("dve", lambda e: e.reciprocal(out=rden[:], in_=den[:]), reads=["den"], writes=["rden"])
                    pt, ptr = nextT()

                    def ftp(e, pt=pt):
                        for c in range(4):
                            for j in range(2):
                                ins = e.transpose(out=pt[:, (2 * c + j) * 128:(2 * c + j + 1) * 128], in_=pb[:, c, j * 128:(j + 1) * 128], identity=ident[:])
                        return ins
                    P.op("pe", ftp, reads=["pb", "ident"], writes=[ptr])
                    P.op("act", lambda e, pt=pt: e.activation(out=pTs[:], in_=pt[:, 0:1024].rearrange("p (c n) -> p c n", c=8), func=AF.Copy),
                         reads=[ptr], writes=["pTs"])
                    po, por = nextM()

                    def fpv(e, po=po, hp=hp, t=t):
                        for c in range(4):
                            e.matmul(po[:, c * 64:(c + 1) * 64], lhsT=pTs[:, 2 * c, :], rhs=vx[:, t, hp * 64:(hp + 1) * 64], start=True, stop=False)
                            ins = e.matmul(po[:, c * 64:(c + 1) * 64], lhsT=pTs[:, 2 * c + 1, :], rhs=vx[:, t + 1, hp * 64:(hp + 1) * 64], start=False, stop=True)
                        return ins
                    P.op("pe", fpv, reads=["pTs", "vx"], writes=[por])
                    P.op("dve", lambda e, po=po, hp=hp: e.tensor_tensor(out=ya[:, hp * 256:(hp + 1) * 256].rearrange("p (h d) -> p h d", h=4),
                                                                       in0=po[:, 0:256].rearrange("p (h d) -> p h d", h=4),
                                                                       in1=rden[:].unsqueeze(2).to_broadcast([128, 4, 64]), op=ALU.mult),
                         reads=[por, "rden"], writes=["ya%d" % hp])
                rms_stats(ya[:], ["ya0", "ya1"], 512.0 ** -0.5, 3, eps1)
                P.op("dve", lambda e: e.tensor_scalar(out=mixb[:, 0:512], in0=ya[:], scalar1=rstd[:, 3:4], scalar2=None, op0=ALU.mult),
                     reads=["ya0", "ya1", "rstd3", "xn"], writes=["mixb_a"])
                dump("ya", ya[:], ["ya0", "ya1"], (not pre) and g_local == 0 and t == 0); dump("mixb", mixb[:], ["mixb_a", "mixb_s"], (not pre) and g_local == 0 and t == 0)
                pt, ptr = nextT()

                def ftm(e, pt=pt):
                    for k in range(8):
                        ins = e.transpose(out=pt[:, k * 128:(k + 1) * 128], in_=mixb[:, k * 128:(k + 1) * 128], identity=ident[:])
                    return ins
                P.op("pe", ftm, reads=["mixb_a", "mixb_s", "ident"], writes=[ptr])
                P.op("act", lambda e, pt=pt, tsl=tsl: e.activation(out=mixT[:, :, tsl], in_=pt[:, 0:1024].rearrange("p (c n) -> p c n", c=8), func=AF.Copy),
                     reads=[ptr], writes=["mixT"])

            P.op("dve", lambda e: e.tensor_copy(out=kT[:, 0:128], in_=kT[:, 256:384]), reads=["kT"], writes=["kT"])
            P.op("dve", lambda e: e.tensor_copy(out=vx[:, 0, :], in_=vx[:, 2, :]), reads=["vx"], writes=["vx"])
            if pre or STAGE in (3, 31, 32, 33, 34):
                continue

            for t in range(2):
                tsl = slice(t * 128, (t + 1) * 128)
                for nh in range(2):
                    pm, pr = nextM()

                    def fo(e, pm=pm, nh=nh, tsl=tsl):
                        for k in range(8):
                            ins = e.matmul(pm[:, 0:512], lhsT=mixT[:, k, tsl], rhs=w_o_bf[:, k, nh * 512:(nh + 1) * 512], start=(k == 0), stop=(k == 7))
                        return ins
                    P.op("pe", fo, reads=["mixT", "w_o"], writes=[pr])
                    tb = nh
                    P.op("dve", lambda e, pm=pm, nh=nh, tb=tb: e.tensor_tensor(out=ttmp[tb][:], in0=pm[:, 0:512], in1=gate1_b[:, nh * 512:(nh + 1) * 512], op=ALU.mult),
                         reads=[pr, "gate1_b"], writes=["ttmp%d" % tb])
                    P.op("pool", lambda e, t=t, nh=nh, tb=tb, xs=xs: e.tensor_tensor(out=xs[:, t, nh * 512:(nh + 1) * 512], in0=xs[:, t, nh * 512:(nh + 1) * 512],
                                                                             in1=ttmp[tb][:], op=ALU.add),
                         reads=["ttmp%d" % tb, xr], writes=[xr])
                norm_transpose(xs[:, t, :], xr, g2, sh2, hT, "hT", t)

            if STAGE == 4:
                continue
            dump("x1", xs[:], [xr], (not pre) and g_local == 0); dump("h2T", hT[:], ["hT"], (not pre) and g_local == 0); dump("mixT", mixT[:], ["mixT"], (not pre) and g_local == 0)
            psD = [(psM[2 + i], "psM%d" % (2 + i)) for i in range(4)]
            for i in range(11):
                sl_w = i % 2
                for fc in range(2):
                    pgi = (2 * i + fc) % 2
                    pg, pgr = psM[pgi], "psM%d" % pgi

                    def fgu(e, pg=pg, sl_w=sl_w, fc=fc):
                        for k in range(8):
                            e.matmul(pg[:, 0:256], lhsT=wgu[sl_w][:, k, fc * 128:(fc + 1) * 128], rhs=hT[:, k, :], start=(k == 0), stop=(k == 7))
                        for k in range(8):
                            ins = e.matmul(pg[:, 256:512], lhsT=wgu[sl_w][:, k, 256 + fc * 128:256 + (fc + 1) * 128], rhs=hT[:, k, :], start=(k == 0), stop=(k == 7))
                        return ins
                    P.op("pe", fgu, reads=["wgu%d" % sl_w, "hT"], writes=[pgr])
                    b = pgi
                    P.op("act", lambda e, pg=pg, b=b: e.activation(out=thg[b][:], in_=pg[:, 0:256], func=AF.Tanh, scale=0.5), reads=[pgr], writes=["thg%d" % b])
                    P.op("dve", lambda e, pg=pg, b=b: e.scalar_tensor_tensor(out=t2[b][:], in0=thg[b][:], scalar=1.0, in1=pg[:, 0:256], op0=ALU.add, op1=ALU.mult),
                         reads=["thg%d" % b, pgr], writes=["t2_%d" % b])
                    P.op("dve", lambda e, pg=pg, b=b, sl_w=sl_w, fc=fc: e.tensor_tensor(out=act2[sl_w][:, fc, :], in0=t2[b][:], in1=pg[:, 256:512], op=ALU.mult),
                         reads=["t2_%d" % b, pgr], writes=["act2_%d" % sl_w])

                def fdn(e, sl_w=sl_w, i=i):
                    for t in range(2):
                        for nh in range(2):
                            for fc in range(2):
                                ins = e.matmul(psD[t * 2 + nh][0][:, 0:512], lhsT=act2[sl_w][:, fc, t * 128:(t + 1) * 128],
                                               rhs=wdn[sl_w][:, fc, nh * 512:(nh + 1) * 512],
                                               start=(i == 0 and fc == 0), stop=(i == 10 and fc == 1))
                    return ins
                P.op("pe", fdn, reads=["act2_%d" % sl_w, "wdn%d" % sl_w], writes=[r for _, r in psD])
                if i + 2 < 11:
                    load_w(i + 2)
            rr["m"] = 0
            for t in range(2):
                for nh in range(2):
                    pm, pr = psD[t * 2 + nh]
                    tb = nh
                    P.op("dve", lambda e, pm=pm, nh=nh, tb=tb: e.tensor_tensor(out=ttmp[tb][:], in0=pm[:, 0:512], in1=gate2h_b[:, nh * 512:(nh + 1) * 512], op=ALU.mult),
                         reads=[pr, "gate2h_b"], writes=["ttmp%d" % tb])
                    P.op("pool", lambda e, t=t, nh=nh, tb=tb, xs=xs: e.tensor_tensor(out=xs[:, t, nh * 512:(nh + 1) * 512], in0=xs[:, t, nh * 512:(nh + 1) * 512],
                                                                             in1=ttmp[tb][:], op=ALU.add),
                         reads=["ttmp%d" % tb, xr], writes=[xr])
                rms_stats(xs[:, t, :], xr, 1.0 / 32.0, 4, eps1)
                P.op("dve", lambda e, t=t, xs=xs: e.scalar_tensor_tensor(out=xs[:, t, :], in0=xs[:, t, :], scalar=rstd[:, 4:5], in1=fnorm[:], op0=ALU.mult, op1=ALU.mult),
                     reads=[xr, "rstd4", "fnorm"], writes=[xr])
            P.dma("sp", [lambda e, g_local=g_local, xs=xs: e.dma_start(out=out[g_local * 256:(g_local + 1) * 256, :].rearrange("(t p) d -> p t d", p=128), in_=xs[:])],
                  osem[sl], reads=[xr])

        P.finish("sp")
        P.emit_all()
    return nc


def _t5_buckets(dist):
    n = np.maximum(dist, 0)
    max_exact = 16
    large = max_exact + (np.log(np.maximum(n, 1) / max_exact) / np.log(128 / max_exact) * (32 - max_exact)).astype(np.int32)
    large = np.minimum(large, 31)
    return np.where(n < max_exact, n, large).astype(np.int32)


def _col(v, nchunk):
    return np.ascontiguousarray(np.asarray(v, np.float32).reshape(nchunk, 128).T)


def _bc(v):
    return np.ascontiguousarray(np.broadcast_to(np.asarray(v, np.float32)[None, :], (128, len(v))))


_NC_CACHE = {}
_DBG = False
_LAST = None


def kernel(x, c, ada_w, ada_b, norm1, w_in, conv_w, conv_b, dt_bias, A_log, D_skip, sinks,
           attn_out_norm, ssm_out_norm, w_o, norm2, w_gate_up, w_down, rel_bias, final_norm):
    x = np.asarray(x, np.float32)
    B, S_, D = x.shape
    ntok = S_ // 2
    f = lambda a: np.asarray(a, np.float32)
    ada_b0 = f(ada_b)[0]
    dist = np.arange(128)[:, None] + 128 - np.arange(256)[None, :]
    valid = (dist >= 0) & (dist < 128)
    gathered = f(rel_bias)[_t5_buckets(dist)]
    biasf = np.where(valid[:, :, None], gathered, np.float32(NEG)).astype(np.float32)
    biasf = np.ascontiguousarray(np.transpose(biasf, (0, 2, 1)))
    perm = []
    for cch in range(4):
        perm += list(range(cch * 64, cch * 64 + 64)) + list(range((cch + 4) * 64, (cch + 4) * 64 + 64))
    w_in0 = f(w_in)[0]
    w_in_p = np.ascontiguousarray(np.concatenate([w_in0[:, perm], w_in0[:, 512:]], axis=1))
    shared = {
        "ada_w": np.ascontiguousarray(f(ada_w)[0]),
        "adab_col": _col(ada_b0, 48),
        "adab_g1": _bc(ada_b0[2048:3072]),
        "adab_g2": _bc(ada_b0[5120:6144]),
        "norm1_col": _col(f(norm1)[0], 8),
        "norm2_col": _col(f(norm2)[0], 8),
        "mixnorm_col": _col(np.concatenate([f(attn_out_norm)[0], f(ssm_out_norm)[0]]), 8),
        "w_in": w_in_p,
        "convw_col": np.ascontiguousarray(np.transpose(f(conv_w)[0].reshape(4, 8, 128), (2, 1, 0))),
        "convb_col": _col(f(conv_b)[0], 8),
        "dtb_b": _bc(f(dt_bias)[0]),
        "alog_b": _bc(f(A_log)[0]),
        "dskip_b": _bc(f(D_skip)[0]),
        "sinks_b": _bc(f(sinks)[0]),
        "w_o": np.ascontiguousarray(f(w_o)[0]),
        "w_gu": np.ascontiguousarray(f(w_gate_up)[0]),
        "w_dn": np.ascontiguousarray(f(w_down)[0]),
        "biasf": biasf,
        "fnorm_b": _bc(f(final_norm)),
    }
    in_maps = []
    for core in range(8):
        b, half = core // 2, core % 2
        m = dict(shared)
        m["x_main"] = np.ascontiguousarray(x[b, half * ntok:(half + 1) * ntok])
        m["x_pre"] = np.ascontiguousarray(x[b, 0:ntok]) if half == 1 else np.zeros((ntok, D), np.float32)
        m["c_col"] = _col(f(c)[b], 8)
        m["flag"] = np.full((128, 1), float(half), np.float32)
        in_maps.append(m)
    if ntok not in _NC_CACHE:
        _NC_CACHE[ntok] = build_nc(ntok, dbg=_DBG)
    res = run_bass_kernel_spmd(_NC_CACHE[ntok], in_maps, core_ids=list(range(8)))
    if _DBG:
        global _LAST
        _LAST = res.results
    outp = np.empty((B, S_, D), np.float32)
    for core in range(8):
        b, half = core // 2, core % 2
        outp[b, half * ntok:(half + 1) * ntok] = res.results[core]["out"]
    return outp
```

```python
import contextlib
import numpy as np
import concourse.bass as bass
import concourse.mybir as mybir
from concourse.bass_utils import run_bass_kernel_spmd

F32 = mybir.dt.float32
BF16 = mybir.dt.bfloat16
AF = mybir.ActivationFunctionType
ALU = mybir.AluOpType
AX = mybir.AxisListType

ENGS = ("pe", "act", "dve", "pool", "sp")

D_MODEL = 1024
IN_WIDTH = 2312
D_FF = 2816
NEG = -30000.0
NSLOT = 4
DMA_RATE = 250e3


class DmaSem:
    def __init__(self, handle):
        self.h = handle
        self.count = 0


class _FakeIns:
    def then_inc(self, *a, **k):
        return self


def _numel(ap):
    n = 1
    for d in ap.shape[1:]:
        n *= int(d)
    return n


class _FakeEng:
    def __init__(self, eng):
        self.eng = eng
        self.cost = 0.0
        self.bytes = 0

    def __getattr__(self, name):
        def call(*args, **kw):
            out = kw.get("out", args[0] if args else None)
            n = _numel(out) if out is not None and hasattr(out, "shape") else 1
            if name == "matmul":
                lhsT = kw.get("lhsT", args[1] if len(args) > 1 else None)
                mult = 4.0 if (lhsT is not None and lhsT.dtype == F32) else 1.0
                self.cost += mult * (n / 2400.0 + 0.035)
            elif name == "transpose":
                self.cost += n / 2400.0 + 0.035
            elif name == "dma_start":
                self.cost += 0.06
                self.bytes += n * int(out.shape[0]) * (2 if out.dtype == BF16 else 4)
            elif name == "wait_ge":
                pass
            elif self.eng == "act":
                self.cost += n / 1200.0 + 0.22
            elif self.eng == "dve":
                self.cost += n / 960.0 + 0.17
            else:
                self.cost += n / 500.0 + 0.3
            return _FakeIns()
        return call


class _Op:
    __slots__ = ("i", "eng", "fn", "fns", "dsem", "preds", "cost", "lat", "start", "done", "idx", "val", "succs", "npred", "lab", "crit")


class Prog:
    LOOKAHEAD = 700

    def __init__(self, nc, stack):
        self.nc = nc
        self.stack = stack
        self.esem = {e: stack.enter_context(nc.semaphore("es_" + e)) for e in ENGS if e != "sp"}
        self.ops = []
        self.lastw = {}
        self.readers = {}
        self.nsem = 0
        self.dma_sems = []
        self.last_on_dsem = {}
        self.q = {e: [] for e in ENGS}

    def dma_sem(self, name=None):
        self.nsem += 1
        s = DmaSem(self.stack.enter_context(self.nc.semaphore(name or ("ds%d" % self.nsem))))
        self.dma_sems.append(s)
        return s

    def _deps(self, reads, writes, eng=None):
        preds = {}

        def need(i, raw):
            if i is None:
                return
            preds[i] = preds.get(i, False) or raw
        for r in reads:
            need(self.lastw.get(r), True)
            if r.startswith("ps"):
                for i in self.readers.get(r, ()):
                    if self.ops[i].eng != eng:
                        need(i, True)
        for w in writes:
            need(self.lastw.get(w), False)
            for i in self.readers.get(w, ()):
                need(i, False)
        return preds

    def _record(self, i, reads, writes):
        for r in reads:
            self.readers.setdefault(r, []).append(i)
        for w in writes:
            self.lastw[w] = i
            self.readers[w] = []

    def _new(self, eng, reads, writes):
        o = _Op()
        o.i = len(self.ops)
        o.eng = eng
        o.preds = self._deps(reads, writes, eng)
        o.fn = None
        o.fns = None
        o.dsem = None
        o.lab = (tuple(reads), tuple(writes))
        self.ops.append(o)
        self._record(o.i, reads, writes)
        return o

    def op(self, eng, fn, reads=(), writes=()):
        o = self._new(eng, reads, writes)
        o.fn = fn
        fk = _FakeEng(eng)
        fn(fk)
        o.cost = fk.cost
        o.lat = 0.0

    def dma(self, eng, fns, dsem, reads=(), writes=()):
        o = self._new(eng, reads, writes)
        o.fns = fns
        o.dsem = dsem
        prev = self.last_on_dsem.get(id(dsem))
        if prev is not None and prev not in o.preds:
            o.preds[prev] = False
        self.last_on_dsem[id(dsem)] = o.i
        fk = _FakeEng(eng)
        for f in fns:
            f(fk)
        o.cost = fk.cost
        o.lat = 2.2 + fk.bytes / DMA_RATE

    def finish(self, eng="sp"):
        ops = self.ops
        n = len(ops)
        for o in ops:
            o.succs = []
            o.npred = len(o.preds)
            o.start = None
        for o in ops:
            for p in o.preds:
                ops[p].succs.append(o.i)
        ready = [o.i for o in ops if o.npred == 0]
        eng_free = {e: 0.0 for e in ENGS}
        order = {e: [] for e in ENGS}
        oldest = 0
        scheduled = 0
        while scheduled < n:
            while oldest < n and ops[oldest].start is not None:
                oldest += 1
            best = None
            for i in ready:
                if i > oldest + self.LOOKAHEAD:
                    continue
                o = ops[i]
                t = eng_free[o.eng]
                o.crit = -1
                for p in o.preds:
                    po = ops[p]
                    d = po.done + (0.0 if po.eng == o.eng and po.dsem is None else 0.08)
                    if d > t:
                        t = d
                        o.crit = p
                key = (t, i)
                if best is None or key < best[0]:
                    best = (key, i)
            (t, _), i = best
            o = ops[i]
            o.start = t
            eng_free[o.eng] = t + o.cost
            o.done = t + o.cost + o.lat
            if o.crit == -1 and order[o.eng]:
                o.crit = -2 - order[o.eng][-1]
            order[o.eng].append(i)
            ready.remove(i)
            scheduled += 1
            for sidx in o.succs:
                so = ops[sidx]
                so.npred -= 1
                if so.npred == 0:
                    ready.append(sidx)
        self.makespan = max(o.done for o in ops)
        cnt = {e: 0 for e in ENGS}
        for e in ENGS:
            for i in order[e]:
                o = ops[i]
                if o.dsem is None:
                    cnt[e] += 1
                    o.val = (self.esem[e], cnt[e])
                else:
                    o.dsem.count += 16 * len(o.fns)
                    o.val = (o.dsem.h, o.dsem.count)
        for e in ENGS:
            waited = {}
            for i in order[e]:
                o = ops[i]
                waits = {}
                for p, raw in o.preds.items():
                    po = ops[p]
                    if po.dsem is None and po.eng == e:
                        if e == "pe" or not raw:
                            continue
                    sem, val = po.val
                    k = id(sem)
                    if waited.get(k, 0) >= val:
                        continue
                    if k not in waits or waits[k][1] < val:
                        waits[k] = (sem, val)
                for k, (sem, val) in waits.items():
                    waited[k] = val
                wl = list(waits.values())
                if o.dsem is None:
                    def emit(en, wl=wl, fn=o.fn, sem=o.val[0]):
                        for (s, v) in wl:
                            en.wait_ge(s, v)
                        fn(en).then_inc(sem, 1)
                else:
                    def emit(en, wl=wl, fns=o.fns, h=o.dsem.h):
                        for (s, v) in wl:
                            en.wait_ge(s, v)
                        for f in fns:
                            f(en).then_inc(h, 16)
                self.q[e].append(emit)
        fw = []
        for e in ENGS:
            if e != "sp" and cnt[e] > 0:
                fw.append((self.esem[e], cnt[e]))
        for s in self.dma_sems:
            if s.count > 0:
                fw.append((s.h, s.count))

        def emit_fin(en, fw=fw):
            for (s, v) in fw:
                en.wait_ge(s, v)
        self.q[eng].append(emit_fin)

    def emit_all(self):
        with self.nc.Block() as block:
            @block.tensor
            def _(e):
                for f in self.q["pe"]:
                    f(e)

            @block.scalar
            def _(e):
                for f in self.q["act"]:
                    f(e)

            @block.vector
            def _(e):
                for f in self.q["dve"]:
                    f(e)

            @block.gpsimd
            def _(e):
                for f in self.q["pool"]:
                    f(e)

            @block.sync
            def _(e):
                for f in self.q["sp"]:
                    f(e)


def build_nc(ntok, dbg=False):
    NG = ntok // 256
    DBG = {}
    nc = bass.Bass("TRN2", target_bir_lowering=False)

    def din(name, shape, dt=F32):
        return nc.dram_tensor(name, list(shape), dt, kind="ExternalInput").ap()

    x_main = din("x_main", [ntok, 1024])
    x_pre = din("x_pre", [ntok, 1024])
    c_col = din("c_col", [128, 8])
    ada_w = din("ada_w", [1024, 6144])
    adab_col = din("adab_col", [128, 48])
    adab_g1 = din("adab_g1", [128, 1024])
    adab_g2 = din("adab_g2", [128, 1024])
    norm1_col = din("norm1_col", [128, 8])
    norm2_col = din("norm2_col", [128, 8])
    mixnorm_col = din("mixnorm_col", [128, 8])
    w_in = din("w_in", [1024, IN_WIDTH])
    convw_col = din("convw_col", [128, 8, 4])
    convb_col = din("convb_col", [128, 8])
    dtb_b = din("dtb_b", [128, 8])
    alog_b = din("alog_b", [128, 8])
    dskip_b = din("dskip_b", [128, 8])
    sinks_b = din("sinks_b", [128, 8])
    w_o = din("w_o", [1024, 1024])
    w_gu = din("w_gu", [1024, 2 * D_FF])
    w_dn = din("w_dn", [D_FF, 1024])
    biasf_in = din("biasf", [128, 8, 256])
    fnorm_in = din("fnorm_b", [128, 1024])
    flag_in = din("flag", [128, 1])
    out = nc.dram_tensor("out", [ntok, 1024], F32, kind="ExternalOutput").ap()
    wgu_s = nc.dram_tensor("wgu_s", [22, 128, 2048], BF16).ap()
    wdn_s = nc.dram_tensor("wdn_s", [22, 128, 1024], BF16).ap()

    with contextlib.ExitStack() as st:
        P = Prog(nc, st)

        def T(name, shape, dt=F32):
            return st.enter_context(nc.sbuf_tensor(name, list(shape), dt))

        def PSUM(name, shape, dt=F32):
            return st.enter_context(nc.psum_tensor(name, list(shape), dt))

        w_in_bf = T("w_in_bf", [128, 8, IN_WIDTH], BF16)
        w_o_bf = T("w_o_bf", [128, 8, 1024], BF16)
        wgu = [T("wgu%d" % i, [128, 8, 256], BF16) for i in range(NSLOT)]
        wdn = [T("wdn%d" % i, [128, 1024], BF16) for i in range(NSLOT)]
        biasf = T("biasf_sb", [128, 8, 256])
        fnorm = T("fnorm_sb", [128, 1024])
        Dg = T("Dg", [128, 8, 128], BF16)
        S = T("S", [128, 512])
        Sbf = T("Sbf", [128, 512], BF16)
        ident = T("ident", [128, 128], BF16)
        identf = T("identf", [128, 128])
        tri = T("tri", [128, 128])
        ones = T("ones", [128, 128])
        maskTf = T("maskTf", [128, 128])
        maskT = T("maskT", [128, 128], BF16)
        g1 = T("g1", [128, 8]); sh1 = T("sh1", [128, 8]); g2 = T("g2", [128, 8]); sh2 = T("sh2", [128, 8])
        modc = T("modc", [128, 48])
        adabc = T("adabc", [128, 48])
        n1c = T("n1c", [128, 8]); n2c = T("n2c", [128, 8]); mnc = T("mnc", [128, 8])
        ccol = T("ccol", [128, 8]); cth = T("cth", [128, 8]); condf = T("condf", [128, 8]); condb = T("condb", [128, 8], BF16)
        convw = T("convw", [128, 8, 4]); convb = T("convb", [128, 8])
        dtb = T("dtb", [128, 8]); A_b = T("A_b", [128, 8]); dsk = T("dsk", [128, 8])
        sink = T("sink", [128, 8]); nsink = T("nsink", [128, 8])
        flag = T("flag_sb", [128, 1]); maskb = T("maskb", [128, 1])
        eps1 = T("eps1", [128, 8]); eps4 = T("eps4", [128, 8]); mhalf = T("mhalf", [128, 8])
        kT = T("kT", [128, 384], BF16)
        vx = T("vx", [128, 3, 128], BF16)
        raw = T("raw", [128, 8, 259])
        xg = [T("xg%d" % i, [128, 2, 1024]) for i in range(2)]
        hT = T("hT", [128, 8, 256], BF16)
        mixT = T("mixT", [128, 8, 256], BF16)
        qT = T("qT", [128, 4, 256], BF16)
        cvo = T("cvo", [128, 8, 256], BF16)
        gz = T("gz", [128, 2, 512], BF16)
        dtu = T("dtu", [128, 16]); dtv = T("dtv", [128, 16])
        sp_t = [T("sp_t%d" % i, [128, 16]) for i in range(8)]
        xn = T("xn", [128, 1024], BF16)
        mixb = T("mixb", [128, 1024], BF16)
        sc = T("sc", [128, 4, 256])
        pb = T("pb", [128, 4, 256], BF16)
        pTs = T("pTs", [128, 8, 128], BF16)
        LT = T("LT", [128, 8, 128], BF16)
        WT = T("WT", [128, 8, 128], BF16)
        xdt = T("xdt", [128, 512], BF16); xdd = T("xdd", [128, 512], BF16); xsb = T("xsb", [128, 512], BF16)
        Btok = T("Btok", [128, 256], BF16)
        ya = T("ya", [128, 512]); yt = T("yt", [128, 512]); yg = T("yg", [128, 512])
        acc = [T("acc%d" % i, [128, 256]) for i in range(2)]
        cth2 = [T("cth2_%d" % i, [128, 256]) for i in range(2)]
        thz = T("thz", [128, 512])
        thg = [T("thg%d" % i, [128, 256]) for i in range(2)]
        t2 = [T("t2_%d" % i, [128, 256]) for i in range(2)]
        act2 = [T("act2_%d" % i, [128, 256], BF16) for i in range(2)]
        ms = T("ms", [128, 8]); mse = T("mse", [128, 8]); rstd = T("rstd", [128, 8])
        a_t = T("a_t", [128, 8]); e_in = T("e_in", [128, 24]); e24 = T("e24", [128, 24])
        w2 = T("w2", [128, 8]); nAcs = T("nAcs", [128, 8])
        rmax = T("rmax", [128, 4]); negm = T("negm", [128, 4]); rsum = T("rsum", [128, 4])
        stmp = T("stmp", [128, 4]); es = T("es", [128, 4]); den = T("den", [128, 4]); rden = T("rden", [128, 4])
        psT = [PSUM("psT%d" % i, [128, 1024], BF16) for i in range(2)]
        psM = [PSUM("psM%d" % i, [128, 512], F32) for i in range(6)]
        rr = {"m": 0, "t": 0}

        def nextM():
            i = rr["m"] % 6
            rr["m"] += 1
            return psM[i], "psM%d" % i

        def nextT():
            i = rr["t"] % 2
            rr["t"] += 1
            return psT[i], "psT%d" % i

        dsem_dbg = P.dma_sem("dbg") if dbg else None

        def dump(name, ap, res, cond=True):
            if not (dbg and cond) or name in DBG:
                return
            dt_ = ap.dtype
            d = nc.dram_tensor("dbg_" + name, list(ap.shape), dt_, kind="ExternalOutput").ap()
            DBG[name] = d
            P.dma("sp", [lambda e, d=d, ap=ap: e.dma_start(out=d, in_=ap)], dsem_dbg, reads=res)

        gate1_b = xg[1][:, 0, :]
        gate2h_b = xg[1][:, 1, :]
        s_small = P.dma_sem("s_small")
        small = [(ccol, c_col, "ccol"), (adabc, adab_col, "adabc"), (n1c, norm1_col, "n1c"), (n2c, norm2_col, "n2c"),
                 (mnc, mixnorm_col, "mnc"), (convw, convw_col, "convw"), (convb, convb_col, "convb"),
                 (dtb, dtb_b, "dtb"), (A_b, alog_b, "A_b"), (dsk, dskip_b, "dsk"), (sink, sinks_b, "sink"),
                 (flag, flag_in, "flag"), (biasf, biasf_in, "biasf"), (fnorm, fnorm_in, "fnorm")]
        P.dma("sp", [(lambda e, d=d, s=s: e.dma_start(out=d[:], in_=s)) for d, s, _ in small], s_small,
              writes=[r for _, _, r in small])
        s_g = P.dma_sem("s_g")
        P.dma("sp", [lambda e: e.dma_start(out=xg[0][:, 0, :], in_=adab_g1),
                     lambda e: e.dma_start(out=xg[0][:, 1, :], in_=adab_g2)], s_g, writes=["x0"])

        P.op("pool", lambda e: e.memset(ones[:], 1.0), writes=["ones"])
        P.op("pool", lambda e: e.affine_select(out=tri[:], in_=ones[:], pattern=[[1, 128]], compare_op=ALU.is_ge,
                                               fill=0.0, base=0, channel_multiplier=-1), reads=["ones"], writes=["tri"])
        P.op("pool", lambda e: e.affine_select(out=identf[:], in_=tri[:], pattern=[[-1, 128]], compare_op=ALU.is_ge,
                                               fill=0.0, base=0, channel_multiplier=1), reads=["tri"], writes=["identf"])
        P.op("pool", lambda e: e.memset(maskTf[:], NEG), writes=["maskTf"])
        P.op("pool", lambda e: e.affine_select(out=maskTf[:], in_=maskTf[:], pattern=[[-1, 128]], compare_op=ALU.is_gt,
                                               fill=0.0, base=0, channel_multiplier=1), reads=["maskTf"], writes=["maskTf"])
        P.op("dve", lambda e: e.tensor_copy(out=ident[:], in_=identf[:]), reads=["identf"], writes=["ident"])
        P.op("dve", lambda e: e.tensor_copy(out=maskT[:], in_=maskTf[:]), reads=["maskTf"], writes=["maskT"])
        P.op("pool", lambda e: e.memset(eps1[:], 1e-6), writes=["eps1"])
        P.op("pool", lambda e: e.memset(eps4[:], 4e-6), writes=["eps4"])
        P.op("pool", lambda e: e.memset(mhalf[:], -0.5), writes=["mhalf"])
        P.op("pool", lambda e: e.memset(S[:], 0.0), writes=["S"])
        P.op("pool", lambda e: e.memset(Sbf[:], 0.0), writes=["Sbf"])
        P.op("pool", lambda e: e.memset(raw[:], 0.0), writes=["raw"])
        P.op("pool", lambda e: e.memset(kT[:], 0.0), writes=["kT"])
        P.op("pool", lambda e: e.memset(vx[:], 0.0), writes=["vx"])

        P.op("act", lambda e: e.activation(out=cth[:], in_=ccol[:], func=AF.Tanh, scale=0.5), reads=["ccol"], writes=["cth"])
        P.op("dve", lambda e: e.scalar_tensor_tensor(out=condf[:], in0=cth[:], scalar=1.0, in1=ccol[:], op0=ALU.add, op1=ALU.mult),
             reads=["cth", "ccol"], writes=["condf"])
        P.op("dve", lambda e: e.tensor_scalar(out=condb[:], in0=condf[:], scalar1=0.5, scalar2=None, op0=ALU.mult),
             reads=["condf"], writes=["condb"])

        s_ada = [P.dma_sem("s_ada%d" % i) for i in range(NSLOT)]
        modps, modps_r = psM[5], "psM5"
        for pc in range(24):
            sl = pc % NSLOT
            P.dma("pool", [lambda e, pc=pc, sl=sl: e.dma_start(
                out=wgu[sl][:], in_=ada_w[:, pc * 256:(pc + 1) * 256].rearrange("(k p) n -> p k n", p=128))],
                s_ada[sl], writes=["wgu%d" % sl])
            vec = pc // 4
            if vec in (2, 5):
                pi = (pc % 4) + (0 if vec == 2 else 4)
                pm, pr = psM[pi % 4], "psM%d" % (pi % 4)

                def f(e, sl=sl, pm=pm):
                    for k in range(8):
                        ins = e.matmul(pm[:, 0:256], lhsT=condb[:, k:k + 1].to_broadcast([128, 128]), rhs=wgu[sl][:, k, :],
                                       start=(k == 0), stop=(k == 7))
                    return ins
                P.op("pe", f, reads=["wgu%d" % sl, "condb"], writes=[pr])
                qd = pc % 4
                dst = gate1_b if vec == 2 else gate2h_b
                srcb = xg[0][:, 0, :] if vec == 2 else xg[0][:, 1, :]
                P.op("dve", lambda e, pm=pm, qd=qd, dst=dst, srcb=srcb: e.tensor_tensor(
                    out=dst[:, qd * 256:(qd + 1) * 256], in0=pm[:, 0:256], in1=srcb[:, qd * 256:(qd + 1) * 256], op=ALU.add),
                    reads=[pr, "x0"], writes=[("gate1_b%d" if vec == 2 else "gate2h_b%d") % qd])
            else:
                def f(e, sl=sl, pc=pc):
                    for jj in range(2):
                        j = pc * 2 + jj
                        for k in range(8):
                            ins = e.matmul(modps[:, j:j + 1], lhsT=wgu[sl][:, k, jj * 128:(jj + 1) * 128], rhs=condb[:, k:k + 1],
                                           start=(k == 0), stop=(k == 7))
                    return ins
                P.op("pe", f, reads=["wgu%d" % sl, "condb"], writes=[modps_r])
        P.op("dve", lambda e: e.tensor_scalar(out=gate2h_b, in0=gate2h_b, scalar1=0.5, scalar2=None, op0=ALU.mult),
             reads=["gate2h_b%d" % i for i in range(4)], writes=["gate2h_b"])
        def fmodc(e):
            e.tensor_tensor(out=modc[:, 0:16], in0=modps[:, 0:16], in1=adabc[:, 0:16], op=ALU.add)
            return e.tensor_tensor(out=modc[:, 24:40], in0=modps[:, 24:40], in1=adabc[:, 24:40], op=ALU.add)
        P.op("dve", fmodc, reads=[modps_r, "adabc"], writes=["modc"])
        P.op("dve", lambda e: e.scalar_tensor_tensor(out=g1[:], in0=modc[:, 8:16], scalar=1.0, in1=n1c[:], op0=ALU.add, op1=ALU.mult),
             reads=["modc", "n1c"], writes=["g1"])
        P.op("dve", lambda e: e.tensor_copy(out=sh1[:], in_=modc[:, 0:8]), reads=["modc"], writes=["sh1"])
        P.op("dve", lambda e: e.scalar_tensor_tensor(out=g2[:], in0=modc[:, 32:40], scalar=1.0, in1=n2c[:], op0=ALU.add, op1=ALU.mult),
             reads=["modc", "n2c"], writes=["g2"])
        P.op("dve", lambda e: e.tensor_copy(out=sh2[:], in_=modc[:, 24:32]), reads=["modc"], writes=["sh2"])

        P.op("dve", lambda e: e.tensor_scalar(out=convw[:], in0=convw[:], scalar1=0.5, scalar2=None, op0=ALU.mult), reads=["convw"], writes=["convw"])
        P.op("dve", lambda e: e.tensor_scalar(out=convb[:], in0=convb[:], scalar1=0.5, scalar2=None, op0=ALU.mult), reads=["convb"], writes=["convb"])
        P.op("act", lambda e: e.activation(out=A_b[:], in_=A_b[:], func=AF.Exp), reads=["A_b"], writes=["A_b"])
        P.op("dve", lambda e: e.tensor_scalar(out=A_b[:], in0=A_b[:], scalar1=-1.0, scalar2=None, op0=ALU.mult), reads=["A_b"], writes=["A_b"])
        P.op("dve", lambda e: e.tensor_scalar(out=nsink[:], in0=sink[:], scalar1=-1.0, scalar2=None, op0=ALU.mult), reads=["sink"], writes=["nsink"])
        P.op("dve", lambda e: e.tensor_scalar(out=maskb[:], in0=flag[:], scalar1=-NEG, scalar2=NEG, op0=ALU.mult, op1=ALU.add),
             reads=["flag"], writes=["maskb"])

        def fdg(e):
            for h in range(8):
                ins = e.tensor_scalar(out=Dg[:, h, :], in0=identf[:], scalar1=dsk[:, h:h + 1], scalar2=None, op0=ALU.mult)
            return ins
        P.op("dve", fdg, reads=["identf", "dsk"], writes=["Dg"])

        s_win = P.dma_sem("s_win")
        P.dma("pool", [(lambda e, kk=kk: e.dma_start(out=w_in_bf[:, 2 * kk:2 * kk + 2, :],
                                                     in_=w_in[kk * 256:(kk + 1) * 256, :].rearrange("(k p) n -> p k n", p=128)))
                       for kk in range(4)], s_win, writes=["w_in"])
        s_wo = P.dma_sem("s_wo")
        P.dma("pool", [(lambda e, kk=kk: e.dma_start(out=w_o_bf[:, 4 * kk:4 * kk + 4, :],
                                                     in_=w_o[kk * 512:(kk + 1) * 512, :].rearrange("(k p) n -> p k n", p=128)))
                       for kk in range(2)], s_wo, writes=["w_o_raw"])

        def fwo(e):
            for k in range(8):
                e.tensor_scalar(out=w_o_bf[:, k, :], in0=w_o_bf[:, k, :], scalar1=mnc[:, k:k + 1], scalar2=None, op0=ALU.mult)
            for k in range(8):
                ins = e.tensor_tensor(out=w_o_bf[:, k, :], in0=w_o_bf[:, k, :], in1=gate1_b, op=ALU.mult)
            return ins
        P.op("dve", fwo, reads=["w_o_raw", "mnc"] + ["gate1_b%d" % i for i in range(4)], writes=["w_o", "x1"])
        s_scr = P.dma_sem("s_scr")
        P.dma("pool", [(lambda e, j=j, u=u: e.dma_start(out=wgu_s[j].rearrange("p (k n) -> p k n", k=8)[:, :, u * 128:(u + 1) * 128],
                                                        in_=w_gu[:, u * D_FF + j * 128:u * D_FF + (j + 1) * 128].rearrange("(k p) n -> p k n", p=128)))
                       for j in range(22) for u in range(2)], s_scr, writes=["scr_gu"])
        s_dnl = [P.dma_sem("s_dnl%d" % i) for i in range(NSLOT)]
        s_dns = P.dma_sem("s_dns")
        for j in range(22):
            sl = j % NSLOT
            P.dma("pool", [lambda e, j=j, sl=sl: e.dma_start(out=wdn[sl][:], in_=w_dn[j * 128:(j + 1) * 128, :])],
                  s_dnl[sl], writes=["wdn%d" % sl])
            P.op("dve", lambda e, sl=sl: e.tensor_tensor(out=wdn[sl][:], in0=wdn[sl][:], in1=gate2h_b, op=ALU.mult),
                 reads=["wdn%d" % sl, "gate2h_b"], writes=["wdn%d" % sl, "x1"])
            P.dma("sp", [lambda e, j=j, sl=sl: e.dma_start(out=wdn_s[j], in_=wdn[sl][:])],
                  s_dns, reads=["wdn%d" % sl], writes=["scr_dn%d" % j])

        xsem = [P.dma_sem("xs%d" % i) for i in range(2)]
        osem = [P.dma_sem("os%d" % i) for i in range(2)]
        wsem = [P.dma_sem("ws%d" % i) for i in range(NSLOT)]

        def load_x(gi):
            pre = gi < NG
            src = x_pre if pre else x_main
            g = gi if pre else gi - NG
            sl = gi % 2
            P.dma("sp", [lambda e, src=src, g=g, sl=sl: e.dma_start(
                out=xg[sl][:], in_=src[g * 256:(g + 1) * 256, :].rearrange("(t p) d -> p t d", p=128))],
                xsem[sl], writes=["x%d" % sl])

        def load_w(j):
            sl = j % NSLOT
            P.dma("sp", [lambda e, j=j, sl=sl: e.dma_start(out=wgu[sl][:], in_=wgu_s[j].rearrange("p (k n) -> p k n", k=8)),
                         lambda e, j=j, sl=sl: e.dma_start(out=wdn[sl][:], in_=wdn_s[j])],
                  wsem[sl], reads=["scr_gu", "scr_dn%d" % j], writes=["wgu%d" % sl, "wdn%d" % sl])

        def rms_stats(src_ap, src_res, scale, col, eps_t):
            P.op("act", lambda e: e.activation(out=xn[:, 0:src_ap.shape[-1]], in_=src_ap, func=AF.Square, scale=scale,
                                               accum_out=ms[:, col:col + 1]),
                 reads=list(src_res) if isinstance(src_res, (list, tuple)) else [src_res], writes=["xn", "ms%d" % col])
            P.op("pool", lambda e: e.tensor_tensor(out=mse[:, col:col + 1], in0=ms[:, col:col + 1], in1=eps_t[:, 0:1], op=ALU.add),
                 reads=["ms%d" % col], writes=["mse%d" % col])
            P.op("pool", lambda e: e.tensor_tensor(out=rstd[:, col:col + 1], in0=mse[:, col:col + 1], in1=mhalf[:, 0:1], op=ALU.pow),
                 reads=["mse%d" % col], writes=["rstd%d" % col])

        def norm_transpose(xs_ap, xres, gvec, svec, dstT, dres, t):
            rms_stats(xs_ap, xres, 1.0 / 32.0, 0, eps1)
            P.op("dve", lambda e: e.tensor_scalar(out=xn[:], in0=xs_ap, scalar1=rstd[:, 0:1], scalar2=None, op0=ALU.mult),
                 reads=[xres, "rstd0"], writes=["xn"])
            pt, ptr = nextT()

            def ftr(e):
                for k in range(8):
                    ins = e.transpose(out=pt[:, k * 128:(k + 1) * 128], in_=xn[:, k * 128:(k + 1) * 128], identity=ident[:])
                return ins
            P.op("pe", ftr, reads=["xn", "ident"], writes=[ptr])

            def fev(e):
                for k in range(8):
                    ins = e.activation(out=dstT[:, k, t * 128:(t + 1) * 128], in_=pt[:, k * 128:(k + 1) * 128], func=AF.Identity,
                                       scale=gvec[:, k:k + 1], bias=svec[:, k:k + 1])
                return ins
            P.op("act", fev, reads=[ptr, "g1", "sh1", "g2", "sh2"], writes=[dres])

        import os
        STAGE = int(os.environ.get("KSTAGE", "9"))
        if STAGE >= 2:
            load_x(0)
        for gi in range(2 * NG):
            pre = gi < NG
            if STAGE < 2 or (STAGE == 2 and not pre):
                break
            g_local = gi if pre else gi - NG
            sl = gi % 2
            xr = "x%d" % sl
            xs = xg[sl]
            if gi + 1 < 2 * NG:
                load_x(gi + 1)
            if gi == NG:
                P.op("dve", lambda e: e.tensor_scalar(out=S[:], in0=S[:], scalar1=flag[:, 0:1], scalar2=None, op0=ALU.mult),
                     reads=["S", "flag"], writes=["S"])
                P.op("act", lambda e: e.activation(out=Sbf[:], in_=S[:], func=AF.Copy), reads=["S"], writes=["Sbf"])
                P.op("dve", lambda e: e.tensor_scalar(out=raw[:, :, 0:3], in0=raw[:, :, 0:3], scalar1=flag[:, 0:1], scalar2=None, op0=ALU.mult),
                     reads=["raw", "flag"], writes=["raw"])
            if not pre:
                for j in range(NSLOT):
                    load_w(j)

            for t in range(2):
                norm_transpose(xs[:, t, :], xr, g1, sh1, hT, "hT", t)

            dump("hT", hT[:], ["hT"], (not pre) and g_local == 0)
            dump("g1", g1[:], ["g1"], (not pre) and g_local == 0); dump("sh1", sh1[:], ["sh1"], (not pre) and g_local == 0)
            dump("S0", S[:], ["S"], (not pre) and g_local == 0)
            def fm_chunks(cols_list, pm, pr):
                def f(e):
                    for ci, c0 in enumerate(cols_list):
                        for k in range(8):
                            ins = e.matmul(pm[:, ci * 256:(ci + 1) * 256], lhsT=w_in_bf[:, k, c0:c0 + 128], rhs=hT[:, k, :],
                                           start=(k == 0), stop=(k == 7))
                    return ins
                P.op("pe", f, reads=["w_in", "hT"], writes=[pr])

            if not pre:
                for i in range(2):
                    pm, pr = nextM()
                    fm_chunks([(2 * i) * 128, (2 * i + 1) * 128], pm, pr)
                    P.op("act", lambda e, pm=pm, i=i: e.activation(out=qT[:, 2 * i:2 * i + 2, :],
                                                                    in_=pm[:, 0:512].rearrange("p (c n) -> p c n", c=2), func=AF.Copy),
                         reads=[pr], writes=["qT"])
            pm, pr = nextM()
            fm_chunks([512], pm, pr)
            P.op("act", lambda e, pm=pm: e.activation(out=kT[:, 128:384], in_=pm[:, 0:256], func=AF.Copy), reads=[pr], writes=["kT"])
            for i in range(4):
                pm, pr = nextM()
                fm_chunks([1280 + (2 * i) * 128, 1280 + (2 * i + 1) * 128], pm, pr)
                P.op("act", lambda e, pm=pm, i=i: e.activation(out=raw[:, 2 * i:2 * i + 2, 3:259],
                                                                in_=pm[:, 0:512].rearrange("p (c n) -> p c n", c=2), func=AF.Copy),
                     reads=[pr], writes=["raw"])
            for c in range(8):
                b = c % 2
                ar = "acc%d" % b

                def fconv(e, c=c, b=b):
                    e.tensor_scalar(out=acc[b][:], in0=raw[:, c, 0:256], scalar1=convw[:, c, 0:1], scalar2=convb[:, c:c + 1],
                                    op0=ALU.mult, op1=ALU.add)
                    ins = None
                    for k in range(1, 4):
                        ins = e.scalar_tensor_tensor(out=acc[b][:], in0=raw[:, c, k:k + 256], scalar=convw[:, c, k:k + 1], in1=acc[b][:],
                                                     op0=ALU.mult, op1=ALU.add)
                    return ins
                P.op("dve", lambda e, c=c, b=b: e.tensor_scalar(out=acc[b][:], in0=raw[:, c, 0:256], scalar1=convw[:, c, 0:1],
                                                                 scalar2=convb[:, c:c + 1], op0=ALU.mult, op1=ALU.add),
                     reads=["raw", "convw", "convb"], writes=[ar])
                for k in range(1, 4):
                    P.op("dve", lambda e, c=c, b=b, k=k: e.scalar_tensor_tensor(out=acc[b][:], in0=raw[:, c, k:k + 256],
                                                                               scalar=convw[:, c, k:k + 1], in1=acc[b][:],
                                                                               op0=ALU.mult, op1=ALU.add),
                         reads=["raw", "convw", ar], writes=[ar])
                P.op("act", lambda e, b=b: e.activation(out=cth2[b][:], in_=acc[b][:], func=AF.Tanh), reads=[ar], writes=["cth2_%d" % b])
                P.op("dve", lambda e, c=c, b=b: e.scalar_tensor_tensor(out=cvo[:, c, :], in0=cth2[b][:], scalar=1.0, in1=acc[b][:],
                                                                       op0=ALU.add, op1=ALU.mult),
                     reads=["cth2_%d" % b, ar], writes=["cvo"])
            P.op("dve", lambda e: e.tensor_copy(out=raw[:, :, 0:3], in_=raw[:, :, 256:259]), reads=["raw"], writes=["raw"])

            dump("qT", qT[:], ["qT"], (not pre) and g_local == 0); dump("kT", kT[:], ["kT"], (not pre) and g_local == 0); dump("cvo", cvo[:], ["cvo"], (not pre) and g_local == 0)
            for t in range(2):
                tsl = slice(t * 128, (t + 1) * 128)
                pa, par = nextM()
                if pre:
                    def f(e, pa=pa, tsl=tsl):
                        for k in range(8):
                            e.matmul(pa[:, 0:128], lhsT=hT[:, k, tsl], rhs=w_in_bf[:, k, 640:768], start=(k == 0), stop=(k == 7))
                        for k in range(8):
                            ins = e.matmul(pa[:, 128:136], lhsT=hT[:, k, tsl], rhs=w_in_bf[:, k, 2304:2312], start=(k == 0), stop=(k == 7))
                        return ins
                    P.op("pe", f, reads=["hT", "w_in"], writes=[par])
                    P.op("act", lambda e, pa=pa, t=t: e.activation(out=vx[:, 1 + t, :], in_=pa[:, 0:128], func=AF.Copy), reads=[par], writes=["vx"])
                    P.op("dve", lambda e, pa=pa, t=t: e.tensor_tensor(out=dtu[:, t * 8:(t + 1) * 8], in0=pa[:, 128:136], in1=dtb[:], op=ALU.add),
                         reads=[par, "dtb"], writes=["dtu"])
                else:
                    pb2, pbr = nextM()

                    def f(e, pa=pa, pb2=pb2, tsl=tsl):
                        for k in range(8):
                            e.matmul(pa[:, 0:512], lhsT=hT[:, k, tsl], rhs=w_in_bf[:, k, 640:1152], start=(k == 0), stop=(k == 7))
                        for k in range(8):
                            e.matmul(pb2[:, 0:128], lhsT=hT[:, k, tsl], rhs=w_in_bf[:, k, 1152:1280], start=(k == 0), stop=(k == 7))
                        for k in range(8):
                            ins = e.matmul(pb2[:, 128:136], lhsT=hT[:, k, tsl], rhs=w_in_bf[:, k, 2304:2312], start=(k == 0), stop=(k == 7))
                        return ins
                    P.op("pe", f, reads=["hT", "w_in"], writes=[par, pbr])
                    P.op("act", lambda e, pa=pa, t=t: e.activation(out=vx[:, 1 + t, :], in_=pa[:, 0:128], func=AF.Copy), reads=[par], writes=["vx"])

                    def fth(e, pa=pa, pb2=pb2):
                        e.activation(out=thz[:, 0:384], in_=pa[:, 128:512], func=AF.Tanh, scale=0.5)
                        return e.activation(out=thz[:, 384:512], in_=pb2[:, 0:128], func=AF.Tanh, scale=0.5)
                    P.op("act", fth, reads=[par, pbr], writes=["thz"])

                    def fgz(e, pa=pa, pb2=pb2, t=t):
                        e.scalar_tensor_tensor(out=gz[:, t, 0:384], in0=thz[:, 0:384], scalar=1.0, in1=pa[:, 128:512], op0=ALU.add, op1=ALU.mult)
                        return e.scalar_tensor_tensor(out=gz[:, t, 384:512], in0=thz[:, 384:512], scalar=1.0, in1=pb2[:, 0:128],
                                                      op0=ALU.add, op1=ALU.mult)
                    P.op("dve", fgz, reads=["thz", par, pbr], writes=["gz"])
                    P.op("dve", lambda e, pb2=pb2, t=t: e.tensor_tensor(out=dtu[:, t * 8:(t + 1) * 8], in0=pb2[:, 128:136], in1=dtb[:], op=ALU.add),
                         reads=[pbr, "dtb"], writes=["dtu"])

            au, tt, dd, ww, w2s, rr_, lnp, relu = sp_t
            P.op("act", lambda e: e.activation(out=au[:], in_=dtu[:], func=AF.Abs), reads=["dtu"], writes=["sp_au"])
            P.op("act", lambda e: e.activation(out=tt[:], in_=au[:], func=AF.Exp, scale=-1.0), reads=["sp_au"], writes=["sp_tt"])
            P.op("dve", lambda e: e.tensor_scalar(out=dd[:], in0=tt[:], scalar1=2.0, scalar2=None, op0=ALU.add), reads=["sp_tt"], writes=["sp_dd"])
            P.op("dve", lambda e: e.reciprocal(out=dd[:], in_=dd[:]), reads=["sp_dd"], writes=["sp_dd"])
            P.op("dve", lambda e: e.tensor_tensor(out=ww[:], in0=tt[:], in1=dd[:], op=ALU.mult), reads=["sp_tt", "sp_dd"], writes=["sp_ww"])
            P.op("dve", lambda e: e.tensor_tensor(out=w2s[:], in0=ww[:], in1=ww[:], op=ALU.mult), reads=["sp_ww"], writes=["sp_w2"])
            P.op("dve", lambda e: e.tensor_scalar(out=rr_[:], in0=w2s[:], scalar1=1.0 / 13.0, scalar2=None, op0=ALU.mult), reads=["sp_w2"], writes=["sp_rr"])
            for cst in (1.0 / 11.0, 1.0 / 9.0, 1.0 / 7.0, 1.0 / 5.0, 1.0 / 3.0):
                P.op("dve", lambda e, cst=cst: e.scalar_tensor_tensor(out=rr_[:], in0=rr_[:], scalar=cst, in1=w2s[:], op0=ALU.add, op1=ALU.mult),
                     reads=["sp_rr", "sp_w2"], writes=["sp_rr"])
            P.op("dve", lambda e: e.scalar_tensor_tensor(out=lnp[:], in0=rr_[:], scalar=1.0, in1=ww[:], op0=ALU.add, op1=ALU.mult),
                 reads=["sp_rr", "sp_ww"], writes=["sp_ln"])
            P.op("dve", lambda e: e.tensor_scalar(out=relu[:], in0=dtu[:], scalar1=0.0, scalar2=None, op0=ALU.max), reads=["dtu"], writes=["sp_relu"])
            P.op("dve", lambda e: e.scalar_tensor_tensor(out=dtv[:], in0=lnp[:], scalar=2.0, in1=relu[:], op0=ALU.mult, op1=ALU.add),
                 reads=["sp_ln", "sp_relu"], writes=["dtv"])

            dump("vx", vx[:], ["vx"], (not pre) and g_local == 0); dump("gz", gz[:], ["gz"], (not pre) and g_local == 0); dump("dtv", dtv[:], ["dtv"], (not pre) and g_local == 0)
            for t in range(2):
                tsl = slice(t * 128, (t + 1) * 128)
                gt_first = (not pre) and g_local == 0 and t == 0
                dt_ap = dtv[:, t * 8:(t + 1) * 8]
                P.op("dve", lambda e, dt_ap=dt_ap: e.tensor_tensor(out=a_t[:], in0=dt_ap, in1=A_b[:], op=ALU.mult), reads=["dtv", "A_b"], writes=["a_t"])
                pc_, pcr = nextM()

                def fcs(e, pc_=pc_):
                    e.matmul(pc_[:, 0:8], lhsT=tri[:], rhs=a_t[:], start=True, stop=True)
                    e.matmul(pc_[:, 8:16], lhsT=ones[:], rhs=a_t[:], start=True, stop=True)
                    return e.matmul(pc_[0:64, 16:24], lhsT=ident[:, 0:64], rhs=ident[:, 0:8], start=True, stop=True)
                P.op("pe", fcs, reads=["tri", "ones", "a_t"], writes=[pcr])
                P.op("dve", lambda e, pc_=pc_: e.tensor_copy(out=e_in[:, 0:16], in_=pc_[:, 0:16]), reads=[pcr], writes=["e_in"])
                P.op("dve", lambda e: e.tensor_tensor(out=e_in[:, 16:24], in0=e_in[:, 8:16], in1=e_in[:, 0:8], op=ALU.subtract),
                     reads=["e_in"], writes=["e_in2"])
                P.op("act", lambda e: e.activation(out=e24[:], in_=e_in[:], func=AF.Exp), reads=["e_in", "e_in2"], writes=["e24"])
                P.op("dve", lambda e, dt_ap=dt_ap: e.tensor_tensor(out=w2[:], in0=dt_ap, in1=e24[:, 16:24], op=ALU.mult), reads=["dtv", "e24"], writes=["w2"])
                pt, ptr = nextT()

                def ftx(e, pt=pt, tsl=tsl):
                    for c in range(6):
                        ins = e.transpose(out=pt[:, c * 128:(c + 1) * 128], in_=cvo[:, c, tsl], identity=ident[:])
                    return ins
                P.op("pe", ftx, reads=["cvo", "ident"], writes=[ptr])
                P.op("dve", lambda e, pt=pt: e.tensor_tensor(out=xdd[:].rearrange("p (h d) -> p h d", h=8),
                                                            in0=pt[:, 0:512].rearrange("p (h d) -> p h d", h=8),
                                                            in1=w2[:].unsqueeze(2).to_broadcast([128, 8, 64]), op=ALU.mult),
                     reads=[ptr, "w2"], writes=["xdd"])
                P.op("act", lambda e, pt=pt: e.activation(out=Btok[:], in_=pt[:, 512:768], func=AF.Copy), reads=[ptr], writes=["Btok"])
                if (not pre) and STAGE not in (31, 33):
                    P.op("dve", lambda e, pt=pt, dt_ap=dt_ap: e.tensor_tensor(out=xdt[:].rearrange("p (h d) -> p h d", h=8),
                                                                              in0=pt[:, 0:512].rearrange("p (h d) -> p h d", h=8),
                                                                              in1=dt_ap.unsqueeze(2).to_broadcast([128, 8, 64]), op=ALU.mult),
                         reads=[ptr, "dtv"], writes=["xdt"])
                    P.op("act", lambda e, pt=pt: e.activation(out=xsb[:], in_=pt[:, 0:512], func=AF.Copy), reads=[ptr, "xdt", "xdd"], writes=["xsb"])
                    P.op("dve", lambda e: e.tensor_scalar(out=nAcs[:], in0=e_in[:, 0:8], scalar1=-1.0, scalar2=None, op0=ALU.mult),
                         reads=["e_in"], writes=["nAcs"])
                    pcb, pcbr = nextM()

                    def fcb(e, pcb=pcb, tsl=tsl):
                        for g in range(2):
                            ins = e.matmul(pcb[:, g * 128:(g + 1) * 128], lhsT=cvo[:, 4 + g, tsl], rhs=cvo[:, 6 + g, tsl], start=True, stop=True)
                        return ins
                    P.op("pe", fcb, reads=["cvo"], writes=[pcbr])
                    pl = [nextM(), nextM()]

                    def fl(e, pl=pl, pcb=pcb):
                        for h in range(8):
                            o = pl[h // 4][0][:, (h % 4) * 128:(h % 4 + 1) * 128]
                            ins = e.matmul(o, lhsT=a_t[:, h:h + 1].to_broadcast([128, 128]), rhs=tri[:], start=True, stop=True)
                        return e.matmul(pcb[0:64, 256:264], lhsT=ident[:, 0:64], rhs=ident[:, 0:8], start=True, stop=True)
                    P.op("pe", fl, reads=["a_t", "tri", "ident"], writes=[pl[0][1], pl[1][1]])
                    scv = sc[:].rearrange("p a b -> p (a b)").rearrange("p (h l) -> p h l", h=8)

                    def fl2(e, pl=pl, scv=scv):
                        for h in range(8):
                            ins = e.scalar_tensor_tensor(out=scv[:, h, :], in0=pl[h // 4][0][:, (h % 4) * 128:(h % 4 + 1) * 128],
                                                         scalar=nAcs[:, h:h + 1], in1=maskTf[:], op0=ALU.add, op1=ALU.add)
                        return ins
                    P.op("dve", fl2, reads=[pl[0][1], pl[1][1], "nAcs", "maskTf"], writes=["sc"])
                    P.op("act", lambda e, scv=scv: e.activation(out=LT[:], in_=scv, func=AF.Exp), reads=["sc"], writes=["LT"])

                    def fwt(e, pcb=pcb):
                        for h in range(8):
                            g = h // 4
                            ins = e.tensor_tensor(out=WT[:, h, :], in0=pcb[:, g * 128:(g + 1) * 128], in1=LT[:, h, :], op=ALU.mult)
                        return ins
                    P.op("dve", fwt, reads=[pcbr, "LT"], writes=["WT"])
                    py, pyr = nextM()

                    SKIPB = (STAGE == 34)
                    if SKIPB:
                        P.op = lambda *a, **k: None

                    def fy(e, py=py):
                        for h in range(8):
                            hs = slice(h * 64, (h + 1) * 64)
                            e.matmul(py[:, hs], lhsT=WT[:, h, :], rhs=xdt[:, hs], start=True, stop=False)
                            ins = e.matmul(py[:, hs], lhsT=Dg[:, h, :], rhs=xsb[:, hs], start=False, stop=True)
                        return ins
                    P.op("pe", fy, reads=["WT", "xdt", "Dg", "xsb"], writes=[pyr])
                    pyo, pyor = nextM()

                    def fyo(e, pyo=pyo, tsl=tsl):
                        for g in range(2):
                            ins = e.matmul(pyo[:, g * 256:(g + 1) * 256], lhsT=cvo[:, 6 + g, tsl], rhs=Sbf[:, g * 256:(g + 1) * 256], start=True, stop=True)
                        return ins
                    P.op("pe", fyo, reads=["cvo", "Sbf"], writes=[pyor])
                    P.op("dve", lambda e, pyo=pyo: e.tensor_tensor(out=yt[:].rearrange("p (h d) -> p h d", h=8),
                                                                  in0=pyo[:, 0:512].rearrange("p (h d) -> p h d", h=8),
                                                                  in1=e24[:, 0:8].unsqueeze(2).to_broadcast([128, 8, 64]), op=ALU.mult),
                         reads=[pyor, "e24"], writes=["yt"])
                    P.op("dve", lambda e, py=py: e.tensor_tensor(out=yt[:], in0=py[:, 0:512], in1=yt[:], op=ALU.add), reads=[pyr, "yt"], writes=["yt"])
                    P.op("dve", lambda e, t=t: e.tensor_tensor(out=yg[:], in0=yt[:], in1=gz[:, t, :], op=ALU.mult), reads=["yt", "gz"], writes=["yg"])
                    for g in range(2):
                        rms_stats(yg[:, g * 256:(g + 1) * 256], "yg", 1.0 / 16.0, 1 + g, eps4)

                    def fmx(e):
                        for g in range(2):
                            ins = e.tensor_scalar(out=mixb[:, 512 + g * 256:512 + (g + 1) * 256], in0=yg[:, g * 256:(g + 1) * 256],
                                                  scalar1=rstd[:, 1 + g:2 + g], scalar2=None, op0=ALU.mult)
                        return ins
                    P.op("dve", fmx, reads=["yg", "rstd1", "rstd2"], writes=["mixb_s"])
                    if SKIPB:
                        del P.op
                if (not pre) and STAGE not in (31, 33):
                    dump("e24", e24[:], ["e24"], (not pre) and g_local == 0 and t == 0); dump("LT", LT[:], ["LT"], (not pre) and g_local == 0 and t == 0)
                    dump("WT", WT[:], ["WT"], (not pre) and g_local == 0 and t == 0); dump("yt", yt[:], ["yt"], (not pre) and g_local == 0 and t == 0)
                    dump("yg", yg[:], ["yg"], (not pre) and g_local == 0 and t == 0)
                pst, pstr = nextM()

                def fst(e, pst=pst):
                    for g in range(2):
                        ins = e.matmul(pst[:, g * 256:(g + 1) * 256], lhsT=Btok[:, g * 128:(g + 1) * 128], rhs=xdd[:, g * 256:(g + 1) * 256], start=True, stop=True)
                    return ins
                P.op("pe", fst, reads=["Btok", "xdd"], writes=[pstr])
                P.op("pool", lambda e: e.tensor_tensor(out=S[:].rearrange("p (h d) -> p h d", h=8), in0=S[:].rearrange("p (h d) -> p h d", h=8),
                                                       in1=e24[:, 8:16].unsqueeze(2).to_broadcast([128, 8, 64]), op=ALU.mult),
                     reads=["S", "e24"], writes=["S"])
                P.op("dve", lambda e, pst=pst: e.tensor_tensor(out=S[:], in0=pst[:, 0:512], in1=S[:], op=ALU.add), reads=[pstr, "S"], writes=["S"])
                P.op("act", lambda e: e.activation(out=Sbf[:], in_=S[:], func=AF.Copy), reads=["S"], writes=["Sbf"])

                if pre or STAGE in (31, 32, 34):
                    continue
                for hp in range(2):
                    ksl = slice(hp * 64, (hp + 1) * 64)
                    pS = [nextM(), nextM()]

                    def fsc(e, pS=pS, ksl=ksl, t=t, tsl=tsl):
                        for c in range(4):
                            ins = e.matmul(pS[c // 2][0][:, (c % 2) * 256:(c % 2 + 1) * 256], lhsT=qT[ksl, c, tsl],
                                           rhs=kT[ksl, t * 128:t * 128 + 256], start=True, stop=True)
                        return ins
                    P.op("pe", fsc, reads=["qT", "kT"], writes=[pS[0][1], pS[1][1]])

                    def fsb(e, pS=pS, hp=hp):
                        for i in range(2):
                            ins = e.scalar_tensor_tensor(out=sc[:, 2 * i:2 * i + 2, :], in0=pS[i][0][:, 0:512].rearrange("p (c n) -> p c n", c=2),
                                                         scalar=0.125, in1=biasf[:, 4 * hp + 2 * i:4 * hp + 2 * i + 2, :], op0=ALU.mult, op1=ALU.add)
                        return ins
                    P.op("dve", fsb, reads=[pS[0][1], pS[1][1], "biasf"], writes=["sc"])
                    if gt_first:
                        P.op("dve", lambda e: e.tensor_scalar(out=sc[:, :, 0:128], in0=sc[:, :, 0:128], scalar1=maskb[:, 0:1], scalar2=None, op0=ALU.add),
                             reads=["sc", "maskb"], writes=["sc"])
                    P.op("dve", lambda e: e.tensor_reduce(out=rmax[:], in_=sc[:], axis=AX.X, op=ALU.max), reads=["sc"], writes=["rmax"])
                    P.op("dve", lambda e, hp=hp: e.scalar_tensor_tensor(out=negm[:], in0=rmax[:], scalar=-1.0, in1=nsink[:, 4 * hp:4 * hp + 4],
                                                                       op0=ALU.mult, op1=ALU.min),
                         reads=["rmax", "nsink"], writes=["negm"])

                    def fex(e):
                        for c in range(4):
                            ins = e.activation(out=pb[:, c, :], in_=sc[:, c, :], func=AF.Exp, bias=negm[:, c:c + 1], scale=1.0,
                                               accum_out=rsum[:, c:c + 1])
                        return ins
                    P.op("act", fex, reads=["sc", "negm"], writes=["pb", "rsum"])
                    P.op("dve", lambda e, hp=hp: e.tensor_tensor(out=stmp[:], in0=sink[:, 4 * hp:4 * hp + 4], in1=negm[:], op=ALU.add),
                         reads=["sink", "negm"], writes=["stmp"])
                    P.op("act", lambda e: e.activation(out=es[:], in_=stmp[:], func=AF.Exp), reads=["stmp"], writes=["es"])
                    P.op("dve", lambda e: e.tensor_tensor(out=den[:], in0=rsum[:], in1=es[:], op=ALU.add), reads=["rsum", "es"], writes=["den"])
                    P.op("dve", lambda e: e.reciprocal(out=rden[:], in_=den[:]), reads=["den"], writes=["rden"])
                    pt, ptr = nextT()

                    def ftp(e, pt=pt):
                        for c in range(4):
                            for j in range(2):
                                ins = e.transpose(out=pt[:, (2 * c + j) * 128:(2 * c + j + 1) * 128], in_=pb[:, c, j * 128:(j + 1) * 128], identity=ident[:])
                        return ins
                    P.op("pe", ftp, reads=["pb", "ident"], writes=[ptr])
                    P.op("act", lambda e, pt=pt: e.activation(out=pTs[:], in_=pt[:, 0:1024].rearrange("p (c n) -> p c n", c=8), func=AF.Copy),
                         reads=[ptr], writes=["pTs"])
                    po, por = nextM()

                    def fpv(e, po=po, hp=hp, t=t):
                        for c in range(4):
                            e.matmul(po[:, c * 64:(c + 1) * 64], lhsT=pTs[:, 2 * c, :], rhs=vx[:, t, hp * 64:(hp + 1) * 64], start=True, stop=False)
                            ins = e.matmul(po[:, c * 64:(c + 1) * 64], lhsT=pTs[:, 2 * c + 1, :], rhs=vx[:, t + 1, hp * 64:(hp + 1) * 64], start=False, stop=True)
                        return ins
                    P.op("pe", fpv, reads=["pTs", "vx"], writes=[por])
                    P.op("dve", lambda e, po=po, hp=hp: e.tensor_tensor(out=ya[:, hp * 256:(hp + 1) * 256].rearrange("p (h d) -> p h d", h=4),
                                                                       in0=po[:, 0:256].rearrange("p (h d) -> p h d", h=4),
                                                                       in1=rden[:].unsqueeze(2).to_broadcast([128, 4, 64]), op=ALU.mult),
                         reads=[por, "rden"], writes=["ya%d" % hp])
                rms_stats(ya[:], ["ya0", "ya1"], 512.0 ** -0.5, 3, eps1)
                P.op("dve", lambda e: e.tensor_scalar(out=mixb[:, 0:512], in0=ya[:], scalar1=rstd[:, 3:4], scalar2=None, op0=ALU.mult),
                     reads=["ya0", "ya1", "rstd3", "xn"], writes=["mixb_a"])
                dump("ya", ya[:], ["ya0", "ya1"], (not pre) and g_local == 0 and t == 0); dump("mixb", mixb[:], ["mixb_a", "mixb_s"], (not pre) and g_local == 0 and t == 0)
                pt, ptr = nextT()

                def ftm(e, pt=pt):
                    for k in range(8):
                        ins = e.transpose(out=pt[:, k * 128:(k + 1) * 128], in_=mixb[:, k * 128:(k + 1) * 128], identity=ident[:])
                    return ins
                P.op("pe", ftm, reads=["mixb_a", "mixb_s", "ident"], writes=[ptr])
                P.op("act", lambda e, pt=pt, tsl=tsl: e.activation(out=mixT[:, :, tsl], in_=pt[:, 0:1024].rearrange("p (c n) -> p c n", c=8), func=AF.Copy),
                     reads=[ptr], writes=["mixT"])

            P.op("dve", lambda e: e.tensor_copy(out=kT[:, 0:128], in_=kT[:, 256:384]), reads=["kT"], writes=["kT"])
            P.op("dve", lambda e: e.tensor_copy(out=vx[:, 0, :], in_=vx[:, 2, :]), reads=["vx"], writes=["vx"])
            if pre or STAGE in (3, 31, 32, 33, 34):
                continue

            for t in range(2):
                tsl = slice(t * 128, (t + 1) * 128)
                for nh in range(2):
                    pm, pr = nextM()

                    def fo(e, pm=pm, nh=nh, tsl=tsl):
                        for k in range(8):
                            ins = e.matmul(pm[:, 0:512], lhsT=mixT[:, k, tsl], rhs=w_o_bf[:, k, nh * 512:(nh + 1) * 512], start=(k == 0), stop=(k == 7))
                        return ins
                    P.op("pe", fo, reads=["mixT", "w_o"], writes=[pr])
                    P.op("dve", lambda e, pm=pm, nh=nh, t=t, xs=xs: e.tensor_tensor(out=xs[:, t, nh * 512:(nh + 1) * 512], in0=pm[:, 0:512],
                                                                               in1=xs[:, t, nh * 512:(nh + 1) * 512], op=ALU.add),
                         reads=[pr, xr], writes=[xr])
                norm_transpose(xs[:, t, :], xr, g2, sh2, hT, "hT", t)

            if STAGE == 4:
                continue
            dump("x1", xs[:], [xr], (not pre) and g_local == 0); dump("h2T", hT[:], ["hT"], (not pre) and g_local == 0); dump("mixT", mixT[:], ["mixT"], (not pre) and g_local == 0)
            psD = [(psM[2 + i], "psM%d" % (2 + i)) for i in range(4)]
            for j in range(22):
                sl_w = j % NSLOT
                pgi = j % 2
                pg, pgr = psM[pgi], "psM%d" % pgi

                def fgu(e, pg=pg, sl_w=sl_w):
                    for k in range(8):
                        e.matmul(pg[:, 0:256], lhsT=wgu[sl_w][:, k, 0:128], rhs=hT[:, k, :], start=(k == 0), stop=(k == 7))
                    for k in range(8):
                        ins = e.matmul(pg[:, 256:512], lhsT=wgu[sl_w][:, k, 128:256], rhs=hT[:, k, :], start=(k == 0), stop=(k == 7))
                    return ins
                P.op("pe", fgu, reads=["wgu%d" % sl_w, "hT"], writes=[pgr])
                b = pgi
                P.op("act", lambda e, pg=pg, b=b: e.activation(out=thg[b][:], in_=pg[:, 0:256], func=AF.Tanh, scale=0.5), reads=[pgr], writes=["thg%d" % b])
                P.op("dve", lambda e, pg=pg, b=b: e.scalar_tensor_tensor(out=t2[b][:], in0=thg[b][:], scalar=1.0, in1=pg[:, 0:256], op0=ALU.add, op1=ALU.mult),
                     reads=["thg%d" % b, pgr], writes=["t2_%d" % b])
                P.op("dve", lambda e, pg=pg, b=b: e.tensor_tensor(out=act2[b][:], in0=t2[b][:], in1=pg[:, 256:512], op=ALU.mult),
                     reads=["t2_%d" % b, pgr], writes=["act2_%d" % b])

                def fdn(e, sl_w=sl_w, j=j, b=b):
                    for t in range(2):
                        for nh in range(2):
                            ins = e.matmul(psD[t * 2 + nh][0][:, 0:512], lhsT=act2[b][:, t * 128:(t + 1) * 128],
                                           rhs=wdn[sl_w][:, nh * 512:(nh + 1) * 512], start=(j == 0), stop=(j == 21))
                    return ins
                P.op("pe", fdn, reads=["act2_%d" % b, "wdn%d" % sl_w], writes=[r for _, r in psD])
                if j + NSLOT < 22:
                    load_w(j + NSLOT)
            rr["m"] = 0
            for t in range(2):
                for nh in range(2):
                    pm, pr = psD[t * 2 + nh]
                    P.op("dve", lambda e, pm=pm, nh=nh, t=t, xs=xs: e.tensor_tensor(out=xs[:, t, nh * 512:(nh + 1) * 512], in0=pm[:, 0:512],
                                                                               in1=xs[:, t, nh * 512:(nh + 1) * 512], op=ALU.add),
                         reads=[pr, xr], writes=[xr])
                rms_stats(xs[:, t, :], xr, 1.0 / 32.0, 4, eps1)
                P.op("dve", lambda e, t=t, xs=xs: e.scalar_tensor_tensor(out=xs[:, t, :], in0=xs[:, t, :], scalar=rstd[:, 4:5], in1=fnorm[:], op0=ALU.mult, op1=ALU.mult),
                     reads=[xr, "rstd4", "fnorm"], writes=[xr])
            P.dma("sp", [lambda e, g_local=g_local, xs=xs: e.dma_start(out=out[g_local * 256:(g_local + 1) * 256, :].rearrange("(t p) d -> p t d", p=128), in_=xs[:])],
                  osem[sl], reads=[xr])

        P.finish("sp")
        P.emit_all()
    return nc


def _t5_buckets(dist):
    n = np.maximum(dist, 0)
    max_exact = 16
    large = max_exact + (np.log(np.maximum(n, 1) / max_exact) / np.log(128 / max_exact) * (32 - max_exact)).astype(np.int32)
    large = np.minimum(large, 31)
    return np.where(n < max_exact, n, large).astype(np.int32)


def _col(v, nchunk):
    return np.ascontiguousarray(np.asarray(v, np.float32).reshape(nchunk, 128).T)


def _bc(v):
    return np.ascontiguousarray(np.broadcast_to(np.asarray(v, np.float32)[None, :], (128, len(v))))


_NC_CACHE = {}
_DBG = False
_LAST = None


def kernel(x, c, ada_w, ada_b, norm1, w_in, conv_w, conv_b, dt_bias, A_log, D_skip, sinks,
           attn_out_norm, ssm_out_norm, w_o, norm2, w_gate_up, w_down, rel_bias, final_norm):
    x = np.asarray(x, np.float32)
    B, S_, D = x.shape
    ntok = S_ // 2
    f = lambda a: np.asarray(a, np.float32)
    ada_b0 = f(ada_b)[0]
    dist = np.arange(128)[:, None] + 128 - np.arange(256)[None, :]
    valid = (dist >= 0) & (dist < 128)
    gathered = f(rel_bias)[_t5_buckets(dist)]
    biasf = np.where(valid[:, :, None], gathered, np.float32(NEG)).astype(np.float32)
    biasf = np.ascontiguousarray(np.transpose(biasf, (0, 2, 1)))
    perm = []
    for cch in range(4):
        perm += list(range(cch * 64, cch * 64 + 64)) + list(range((cch + 4) * 64, (cch + 4) * 64 + 64))
    w_in0 = f(w_in)[0]
    w_in_p = np.ascontiguousarray(np.concatenate([w_in0[:, perm], w_in0[:, 512:]], axis=1))
    shared = {
        "ada_w": np.ascontiguousarray(f(ada_w)[0]),
        "adab_col": _col(ada_b0, 48),
        "adab_g1": _bc(ada_b0[2048:3072]),
        "adab_g2": _bc(ada_b0[5120:6144]),
        "norm1_col": _col(f(norm1)[0], 8),
        "norm2_col": _col(f(norm2)[0], 8),
        "mixnorm_col": _col(np.concatenate([f(attn_out_norm)[0], f(ssm_out_norm)[0]]), 8),
        "w_in": w_in_p,
        "convw_col": np.ascontiguousarray(np.transpose(f(conv_w)[0].reshape(4, 8, 128), (2, 1, 0))),
        "convb_col": _col(f(conv_b)[0], 8),
        "dtb_b": _bc(f(dt_bias)[0]),
        "alog_b": _bc(f(A_log)[0]),
        "dskip_b": _bc(f(D_skip)[0]),
        "sinks_b": _bc(f(sinks)[0]),
        "w_o": np.ascontiguousarray(f(w_o)[0]),
        "w_gu": np.ascontiguousarray(f(w_gate_up)[0]),
        "w_dn": np.ascontiguousarray(f(w_down)[0]),
        "biasf": biasf,
        "fnorm_b": _bc(f(final_norm)),
    }
    in_maps = []
    for core in range(8):
        b, half = core // 2, core % 2
        m = dict(shared)
        m["x_main"] = np.ascontiguousarray(x[b, half * ntok:(half + 1) * ntok])
        m["x_pre"] = np.ascontiguousarray(x[b, 0:ntok]) if half == 1 else np.zeros((ntok, D), np.float32)
        m["c_col"] = _col(f(c)[b], 8)
        m["flag"] = np.full((128, 1), float(half), np.float32)
        in_maps.append(m)
    if ntok not in _NC_CACHE:
        _NC_CACHE[ntok] = build_nc(ntok, dbg=_DBG)
    res = run_bass_kernel_spmd(_NC_CACHE[ntok], in_maps, core_ids=list(range(8)))
    if _DBG:
        global _LAST
        _LAST = res.results
    outp = np.empty((B, S_, D), np.float32)
    for core in range(8):
        b, half = core // 2, core % 2
        outp[b, half * ntok:(half + 1) * ntok] = res.results[core]["out"]
    return outp
```

```python
import contextlib
import numpy as np
import concourse.bass as bass
import concourse.mybir as mybir
from concourse.bass_utils import run_bass_kernel_spmd

F32 = mybir.dt.float32
BF16 = mybir.dt.bfloat16
AF = mybir.ActivationFunctionType
ALU = mybir.AluOpType
AX = mybir.AxisListType

ENGS = ("pe", "act", "dve", "pool", "sp")

D_MODEL = 1024
IN_WIDTH = 2312
D_FF = 2816
NEG = -30000.0
NSLOT = 4
NDSLOT = 6
DMA_RATE = 250e3


class DmaSem:
    def __init__(self, handle):
        self.h = handle
        self.count = 0


class _FakeIns:
    def then_inc(self, *a, **k):
        return self


def _numel(ap):
    n = 1
    for d in ap.shape[1:]:
        n *= int(d)
    return n


class _FakeEng:
    def __init__(self, eng):
        self.eng = eng
        self.cost = 0.0
        self.bytes = 0

    def __getattr__(self, name):
        def call(*args, **kw):
            out = kw.get("out", args[0] if args else None)
            n = _numel(out) if out is not None and hasattr(out, "shape") else 1
            if name == "matmul":
                lhsT = kw.get("lhsT", args[1] if len(args) > 1 else None)
                mult = 4.0 if (lhsT is not None and lhsT.dtype == F32) else 1.0
                self.cost += mult * (n / 2400.0 + 0.035)
            elif name == "transpose":
                self.cost += n / 2400.0 + 0.035
            elif name == "dma_start":
                self.cost += 0.06
                self.bytes += n * int(out.shape[0]) * (2 if out.dtype == BF16 else 4)
            elif name == "wait_ge":
                pass
            elif self.eng == "act":
                self.cost += n / 1200.0 + 0.22
            elif self.eng == "dve":
                self.cost += n / 960.0 + 0.17
            else:
                self.cost += n / 500.0 + 0.3
            return _FakeIns()
        return call


class _Op:
    __slots__ = ("i", "eng", "fn", "fns", "dsem", "preds", "cost", "lat", "start", "done", "idx", "val", "succs", "npred", "lab", "crit", "tag")


class Prog:
    LOOKAHEAD = 700

    def __init__(self, nc, stack):
        self.nc = nc
        self.stack = stack
        self.esem = {e: stack.enter_context(nc.semaphore("es_" + e)) for e in ENGS if e != "sp"}
        self.ops = []
        self.lastw = {}
        self.readers = {}
        self.nsem = 0
        self.dma_sems = []
        self.last_on_dsem = {}
        self.q = {e: [] for e in ENGS}

    def dma_sem(self, name=None):
        self.nsem += 1
        s = DmaSem(self.stack.enter_context(self.nc.semaphore(name or ("ds%d" % self.nsem))))
        self.dma_sems.append(s)
        return s

    def _deps(self, reads, writes, eng=None):
        preds = {}

        def need(i, raw):
            if i is None:
                return
            preds[i] = preds.get(i, False) or raw
        for r in reads:
            need(self.lastw.get(r), True)
            if r.startswith("ps"):
                for i in self.readers.get(r, ()):
                    if self.ops[i].eng != eng:
                        need(i, True)
        for w in writes:
            need(self.lastw.get(w), False)
            for i in self.readers.get(w, ()):
                need(i, False)
        return preds

    def _record(self, i, reads, writes):
        for r in reads:
            self.readers.setdefault(r, []).append(i)
        for w in writes:
            self.lastw[w] = i
            self.readers[w] = []

    def _new(self, eng, reads, writes):
        o = _Op()
        o.i = len(self.ops)
        o.eng = eng
        o.preds = self._deps(reads, writes, eng)
        o.fn = None
        o.fns = None
        o.dsem = None
        o.lab = (tuple(reads), tuple(writes))
        o.tag = getattr(self, "tag", None)
        self.ops.append(o)
        self._record(o.i, reads, writes)
        return o

    def op(self, eng, fn, reads=(), writes=()):
        o = self._new(eng, reads, writes)
        o.fn = fn
        fk = _FakeEng(eng)
        fn(fk)
        o.cost = fk.cost
        o.lat = 0.0

    def dma(self, eng, fns, dsem, reads=(), writes=()):
        o = self._new(eng, reads, writes)
        o.fns = fns
        o.dsem = dsem
        prev = self.last_on_dsem.get(id(dsem))
        if prev is not None and prev not in o.preds:
            o.preds[prev] = False
        self.last_on_dsem[id(dsem)] = o.i
        fk = _FakeEng(eng)
        for f in fns:
            f(fk)
        o.cost = fk.cost
        o.lat = 2.2 + fk.bytes / DMA_RATE

    def finish(self, eng="sp"):
        ops = self.ops
        n = len(ops)
        for o in ops:
            o.succs = []
            o.npred = len(o.preds)
            o.start = None
        for o in ops:
            for p in o.preds:
                ops[p].succs.append(o.i)
        ready = [o.i for o in ops if o.npred == 0]
        eng_free = {e: 0.0 for e in ENGS}
        order = {e: [] for e in ENGS}
        oldest = 0
        scheduled = 0
        while scheduled < n:
            while oldest < n and ops[oldest].start is not None:
                oldest += 1
            best = None
            for i in ready:
                if i > oldest + self.LOOKAHEAD:
                    continue
                o = ops[i]
                t = eng_free[o.eng]
                o.crit = -1
                for p in o.preds:
                    po = ops[p]
                    d = po.done + (0.0 if po.eng == o.eng and po.dsem is None else 0.08)
                    if d > t:
                        t = d
                        o.crit = p
                key = (t, i)
                if best is None or key < best[0]:
                    best = (key, i)
            (t, _), i = best
            o = ops[i]
            o.start = t
            eng_free[o.eng] = t + o.cost
            o.done = t + o.cost + o.lat
            if o.crit == -1 and order[o.eng]:
                o.crit = -2 - order[o.eng][-1]
            order[o.eng].append(i)
            ready.remove(i)
            scheduled += 1
            for sidx in o.succs:
                so = ops[sidx]
                so.npred -= 1
                if so.npred == 0:
                    ready.append(sidx)
        self.makespan = max(o.done for o in ops)
        cnt = {e: 0 for e in ENGS}
        for e in ENGS:
            for i in order[e]:
                o = ops[i]
                if o.dsem is None:
                    cnt[e] += 1
                    o.val = (self.esem[e], cnt[e])
                else:
                    o.dsem.count += 16 * len(o.fns)
                    o.val = (o.dsem.h, o.dsem.count)
        for e in ENGS:
            waited = {}
            for i in order[e]:
                o = ops[i]
                waits = {}
                for p, raw in o.preds.items():
                    po = ops[p]
                    if po.dsem is None and po.eng == e:
                        if e == "pe" or not raw:
                            continue
                    sem, val = po.val
                    k = id(sem)
                    if waited.get(k, 0) >= val:
                        continue
                    if k not in waits or waits[k][1] < val:
                        waits[k] = (sem, val)
                for k, (sem, val) in waits.items():
                    waited[k] = val
                wl = list(waits.values())
                if o.dsem is None:
                    def emit(en, wl=wl, fn=o.fn, sem=o.val[0]):
                        for (s, v) in wl:
                            en.wait_ge(s, v)
                        fn(en).then_inc(sem, 1)
                else:
                    def emit(en, wl=wl, fns=o.fns, h=o.dsem.h):
                        for (s, v) in wl:
                            en.wait_ge(s, v)
                        for f in fns:
                            f(en).then_inc(h, 16)
                self.q[e].append(emit)
        fw = []
        for e in ENGS:
            if e != "sp" and cnt[e] > 0:
                fw.append((self.esem[e], cnt[e]))
        for s in self.dma_sems:
            if s.count > 0:
                fw.append((s.h, s.count))

        def emit_fin(en, fw=fw):
            for (s, v) in fw:
                en.wait_ge(s, v)
        self.q[eng].append(emit_fin)

    def emit_all(self):
        with self.nc.Block() as block:
            @block.tensor
            def _(e):
                for f in self.q["pe"]:
                    f(e)

            @block.scalar
            def _(e):
                for f in self.q["act"]:
                    f(e)

            @block.vector
            def _(e):
                for f in self.q["dve"]:
                    f(e)

            @block.gpsimd
            def _(e):
                for f in self.q["pool"]:
                    f(e)

            @block.sync
            def _(e):
                for f in self.q["sp"]:
                    f(e)


def build_nc(ntok, dbg=False):
    NG = ntok // 256
    DBG = {}
    nc = bass.Bass("TRN2", target_bir_lowering=False)

    def din(name, shape, dt=F32):
        return nc.dram_tensor(name, list(shape), dt, kind="ExternalInput").ap()

    x_main = din("x_main", [ntok, 1024])
    x_pre = din("x_pre", [ntok, 1024])
    c_col = din("c_col", [128, 8])
    ada_w = din("ada_w", [1024, 6144])
    adab_col = din("adab_col", [128, 48])
    adab_g1 = din("adab_g1", [128, 1024])
    adab_g2 = din("adab_g2", [128, 1024])
    norm1_col = din("norm1_col", [128, 8])
    norm2_col = din("norm2_col", [128, 8])
    mixnorm_col = din("mixnorm_col", [128, 8])
    w_in = din("w_in", [1024, IN_WIDTH])
    convw_col = din("convw_col", [128, 8, 4])
    convb_col = din("convb_col", [128, 8])
    dtb_b = din("dtb_b", [128, 8])
    alog_b = din("alog_b", [128, 8])
    dskip_b = din("dskip_b", [128, 8])
    sinks_b = din("sinks_b", [128, 8])
    w_o = din("w_o", [1024, 1024])
    w_gu = din("w_gu", [1024, 2 * D_FF])
    w_dn = din("w_dn", [D_FF, 1024])
    biasf_in = din("biasf", [128, 8, 256])
    fnorm_in = din("fnorm_b", [128, 1024])
    flag_in = din("flag", [128, 1])
    out = nc.dram_tensor("out", [ntok, 1024], F32, kind="ExternalOutput").ap()
    wgu_s = nc.dram_tensor("wgu_s", [22, 128, 2048], BF16).ap()
    wdn_s = nc.dram_tensor("wdn_s", [2, 22, 128, 512], BF16).ap()

    with contextlib.ExitStack() as st:
        P = Prog(nc, st)

        def T(name, shape, dt=F32):
            return st.enter_context(nc.sbuf_tensor(name, list(shape), dt))

        def PSUM(name, shape, dt=F32):
            return st.enter_context(nc.psum_tensor(name, list(shape), dt))

        w_in_bf = T("w_in_bf", [128, 8, IN_WIDTH], BF16)
        w_o_bf = T("w_o_bf", [128, 8, 1024], BF16)
        wgu = [T("wgu%d" % i, [128, 8, 256], BF16) for i in range(NSLOT)]
        wdn = [T("wdn%d" % i, [128, 512], BF16) for i in range(NDSLOT)]
        diagW = T("diagW", [128, 8, 4, 128], BF16)
        actT = T("actT", [128, 22, 256], BF16)
        hT2 = T("hT2", [128, 8, 256], BF16)
        biasf = T("biasf_sb", [128, 8, 256])
        fnorm = T("fnorm_sb", [128, 1024])
        Dg = T("Dg", [128, 8, 128], BF16)
        S = T("S", [128, 512])
        Sbf = T("Sbf", [128, 512], BF16)
        ident = T("ident", [128, 128], BF16)
        identf = T("identf", [128, 128])
        tri = T("tri", [128, 128])
        ones = T("ones", [128, 128])
        maskTf = T("maskTf", [128, 128])
        maskT = T("maskT", [128, 128], BF16)
        g1 = T("g1", [128, 8]); sh1 = T("sh1", [128, 8]); g2 = T("g2", [128, 8]); sh2 = T("sh2", [128, 8])
        modc = T("modc", [128, 48])
        adabc = T("adabc", [128, 48])
        n1c = T("n1c", [128, 8]); n2c = T("n2c", [128, 8]); mnc = T("mnc", [128, 8])
        ccol = T("ccol", [128, 8]); cth = T("cth", [128, 8]); condf = T("condf", [128, 8]); condb = T("condb", [128, 8], BF16)
        convw = T("convw", [128, 8, 4]); convb = T("convb", [128, 8])
        dtb = T("dtb", [128, 8]); A_b = T("A_b", [128, 8]); dsk = T("dsk", [128, 8])
        sink = T("sink", [128, 8]); nsink = T("nsink", [128, 8])
        flag = T("flag_sb", [128, 1]); maskb = T("maskb", [128, 1])
        eps1 = T("eps1", [128, 8]); eps4 = T("eps4", [128, 8]); mhalf = T("mhalf", [128, 8])
        kT = T("kT", [128, 384], BF16)
        vx = T("vx", [128, 3, 128], BF16)
        raw = T("raw", [128, 8, 259], BF16)
        xg = [T("xg%d" % i, [128, 2, 1024]) for i in range(2)]
        hT = T("hT", [128, 8, 256], BF16)
        mixT = T("mixT", [128, 8, 256], BF16)
        qT = T("qT", [128, 4, 256], BF16)
        cvo = T("cvo", [128, 8, 256], BF16)
        gz = T("gz", [128, 2, 512], BF16)
        dtu = T("dtu", [128, 16]); dtv = T("dtv", [128, 16])
        sp_t = [T("sp_t%d" % i, [128, 16]) for i in range(8)]
        xn = T("xn", [128, 1024], BF16)
        junk = T("junk", [128, 1024], BF16)
        mixb = T("mixb", [128, 1024], BF16)
        sc = T("sc", [128, 4, 256])
        pb = T("pb", [128, 4, 256], BF16)
        pTs = T("pTs", [128, 8, 128], BF16)
        LT = T("LT", [128, 8, 128], BF16)
        WT = T("WT", [128, 8, 128], BF16)
        xdt = T("xdt", [128, 512], BF16); xdd = T("xdd", [128, 512], BF16); xsb = T("xsb", [128, 512], BF16)
        Btok = T("Btok", [128, 256], BF16)
        ya = T("ya", [128, 512]); yt = T("yt", [128, 512]); yg = T("yg", [128, 512])
        acc = [T("acc%d" % i, [128, 2, 256]) for i in range(2)]
        cth2 = [T("cth2_%d" % i, [128, 2, 256]) for i in range(2)]
        thz = T("thz", [128, 512])
        thg = [T("thg%d" % i, [128, 256]) for i in range(2)]
        t2 = [T("t2_%d" % i, [128, 256], BF16) for i in range(2)]
        ub = [T("ub_%d" % i, [128, 256], BF16) for i in range(2)]
        ms = T("ms", [128, 8]); mse = T("mse", [128, 8]); rstd = T("rstd", [128, 8])
        a_t = T("a_t", [128, 8]); e_in = T("e_in", [128, 24]); e24 = T("e24", [128, 24])
        w2 = T("w2", [128, 8]); nAcs = T("nAcs", [128, 8])
        rmax = T("rmax", [128, 4]); negm = T("negm", [128, 4]); rsum = T("rsum", [128, 4])
        stmp = T("stmp", [128, 4]); es = T("es", [128, 4]); den = T("den", [128, 4]); rden = T("rden", [128, 4])
        psT = [PSUM("psT%d" % i, [128, 1024], BF16) for i in range(2)]
        psM = [PSUM("psM%d" % i, [128, 512], F32) for i in range(6)]
        rr = {"m": 0, "t": 0}

        def nextM():
            i = 2 + rr["m"] % 4
            rr["m"] += 1
            return psM[i], "psM%d" % i

        def nextT():
            i = rr["t"] % 2
            rr["t"] += 1
            return psT[i], "psT%d" % i

        dsem_dbg = P.dma_sem("dbg") if dbg else None

        def dump(name, ap, res, cond=True):
            if not (dbg and cond) or name in DBG:
                return
            dt_ = ap.dtype
            d = nc.dram_tensor("dbg_" + name, list(ap.shape), dt_, kind="ExternalOutput").ap()
            DBG[name] = d
            P.dma("sp", [lambda e, d=d, ap=ap: e.dma_start(out=d, in_=ap)], dsem_dbg, reads=res)

        gate1_b = xg[1][:, 0, :]
        gate2h_b = xg[1][:, 1, :]
        s_small = P.dma_sem("s_small")
        small = [(ccol, c_col, "ccol"), (adabc, adab_col, "adabc"), (n1c, norm1_col, "n1c"), (n2c, norm2_col, "n2c"),
                 (mnc, mixnorm_col, "mnc"), (convw, convw_col, "convw"), (convb, convb_col, "convb"),
                 (dtb, dtb_b, "dtb"), (A_b, alog_b, "A_b"), (dsk, dskip_b, "dsk"), (sink, sinks_b, "sink"),
                 (flag, flag_in, "flag"), (biasf, biasf_in, "biasf"), (fnorm, fnorm_in, "fnorm")]
        P.dma("sp", [(lambda e, d=d, s=s: e.dma_start(out=d[:], in_=s)) for d, s, _ in small], s_small,
              writes=[r for _, _, r in small])
        s_g = P.dma_sem("s_g")
        P.dma("sp", [lambda e: e.dma_start(out=xg[0][:, 0, :], in_=adab_g1),
                     lambda e: e.dma_start(out=xg[0][:, 1, :], in_=adab_g2)], s_g, writes=["x0"])

        P.op("pool", lambda e: e.memset(ones[:], 1.0), writes=["ones"])
        P.op("pool", lambda e: e.affine_select(out=tri[:], in_=ones[:], pattern=[[1, 128]], compare_op=ALU.is_ge,
                                               fill=0.0, base=0, channel_multiplier=-1), reads=["ones"], writes=["tri"])
        P.op("pool", lambda e: e.affine_select(out=identf[:], in_=tri[:], pattern=[[-1, 128]], compare_op=ALU.is_ge,
                                               fill=0.0, base=0, channel_multiplier=1), reads=["tri"], writes=["identf"])
        P.op("pool", lambda e: e.memset(maskTf[:], NEG), writes=["maskTf"])
        P.op("pool", lambda e: e.affine_select(out=maskTf[:], in_=maskTf[:], pattern=[[-1, 128]], compare_op=ALU.is_gt,
                                               fill=0.0, base=0, channel_multiplier=1), reads=["maskTf"], writes=["maskTf"])
        P.op("dve", lambda e: e.tensor_copy(out=ident[:], in_=identf[:]), reads=["identf"], writes=["ident"])
        P.op("dve", lambda e: e.tensor_copy(out=maskT[:], in_=maskTf[:]), reads=["maskTf"], writes=["maskT"])
        P.op("pool", lambda e: e.memset(eps1[:], 1e-6), writes=["eps1"])
        P.op("pool", lambda e: e.memset(eps4[:], 4e-6), writes=["eps4"])
        P.op("pool", lambda e: e.memset(mhalf[:], -0.5), writes=["mhalf"])
        P.op("pool", lambda e: e.memset(S[:], 0.0), writes=["S"])
        P.op("pool", lambda e: e.memset(Sbf[:], 0.0), writes=["Sbf"])
        P.op("pool", lambda e: e.memset(raw[:], 0.0), writes=["raw"])
        P.op("pool", lambda e: e.memset(kT[:], 0.0), writes=["kT"])
        P.op("pool", lambda e: e.memset(vx[:], 0.0), writes=["vx"])

        P.op("act", lambda e: e.activation(out=cth[:], in_=ccol[:], func=AF.Tanh, scale=0.5), reads=["ccol"], writes=["cth"])
        P.op("dve", lambda e: e.scalar_tensor_tensor(out=condf[:], in0=cth[:], scalar=1.0, in1=ccol[:], op0=ALU.add, op1=ALU.mult),
             reads=["cth", "ccol"], writes=["condf"])
        P.op("dve", lambda e: e.tensor_scalar(out=condb[:], in0=condf[:], scalar1=0.5, scalar2=None, op0=ALU.mult),
             reads=["condf"], writes=["condb"])

        s_ada = [P.dma_sem("s_ada%d" % i) for i in range(NSLOT)]
        modps, modps_r = psM[5], "psM5"
        for pc in range(24):
            sl = pc % NSLOT
            P.dma("pool", [lambda e, pc=pc, sl=sl: e.dma_start(
                out=wgu[sl][:], in_=ada_w[:, pc * 256:(pc + 1) * 256].rearrange("(k p) n -> p k n", p=128))],
                s_ada[sl], writes=["wgu%d" % sl])
            vec = pc // 4
            if vec in (2, 5):
                pi = (pc % 4) + (0 if vec == 2 else 4)
                pm, pr = psM[pi % 4], "psM%d" % (pi % 4)

                def f(e, sl=sl, pm=pm):
                    for k in range(8):
                        ins = e.matmul(pm[:, 0:256], lhsT=condb[:, k:k + 1].to_broadcast([128, 128]), rhs=wgu[sl][:, k, :],
                                       start=(k == 0), stop=(k == 7))
                    return ins
                P.op("pe", f, reads=["wgu%d" % sl, "condb"], writes=[pr])
                qd = pc % 4
                dst = gate1_b if vec == 2 else gate2h_b
                srcb = xg[0][:, 0, :] if vec == 2 else xg[0][:, 1, :]
                P.op("dve", lambda e, pm=pm, qd=qd, dst=dst, srcb=srcb: e.tensor_tensor(
                    out=dst[:, qd * 256:(qd + 1) * 256], in0=pm[:, 0:256], in1=srcb[:, qd * 256:(qd + 1) * 256], op=ALU.add),
                    reads=[pr, "x0"], writes=[("gate1_b%d" if vec == 2 else "gate2h_b%d") % qd])
            else:
                def f(e, sl=sl, pc=pc):
                    for jj in range(2):
                        j = pc * 2 + jj
                        for k in range(8):
                            ins = e.matmul(modps[:, j:j + 1], lhsT=wgu[sl][:, k, jj * 128:(jj + 1) * 128], rhs=condb[:, k:k + 1],
                                           start=(k == 0), stop=(k == 7))
                    return ins
                P.op("pe", f, reads=["wgu%d" % sl, "condb"], writes=[modps_r])
        P.op("dve", lambda e: e.tensor_scalar(out=gate2h_b, in0=gate2h_b, scalar1=0.5, scalar2=None, op0=ALU.mult),
             reads=["gate2h_b%d" % i for i in range(4)], writes=["gate2h_b"])
        def fmodc(e):
            e.tensor_tensor(out=modc[:, 0:16], in0=modps[:, 0:16], in1=adabc[:, 0:16], op=ALU.add)
            return e.tensor_tensor(out=modc[:, 24:40], in0=modps[:, 24:40], in1=adabc[:, 24:40], op=ALU.add)
        P.op("dve", fmodc, reads=[modps_r, "adabc"], writes=["modc"])
        P.op("dve", lambda e: e.scalar_tensor_tensor(out=g1[:], in0=modc[:, 8:16], scalar=1.0, in1=n1c[:], op0=ALU.add, op1=ALU.mult),
             reads=["modc", "n1c"], writes=["g1"])
        P.op("dve", lambda e: e.tensor_copy(out=sh1[:], in_=modc[:, 0:8]), reads=["modc"], writes=["sh1"])
        P.op("dve", lambda e: e.scalar_tensor_tensor(out=g2[:], in0=modc[:, 32:40], scalar=1.0, in1=n2c[:], op0=ALU.add, op1=ALU.mult),
             reads=["modc", "n2c"], writes=["g2"])
        P.op("dve", lambda e: e.tensor_copy(out=sh2[:], in_=modc[:, 24:32]), reads=["modc"], writes=["sh2"])

        P.op("dve", lambda e: e.tensor_scalar(out=convw[:], in0=convw[:], scalar1=0.5, scalar2=None, op0=ALU.mult), reads=["convw"], writes=["convw"])
        P.op("dve", lambda e: e.tensor_scalar(out=convb[:], in0=convb[:], scalar1=0.5, scalar2=None, op0=ALU.mult), reads=["convb"], writes=["convb"])
        def fdw(e):
            for c in range(8):
                for k in range(4):
                    ins = e.tensor_scalar(out=diagW[:, c, k, :], in0=identf[:], scalar1=convw[:, c, k:k + 1], scalar2=None, op0=ALU.mult)
            return ins
        P.op("dve", fdw, reads=["identf", "convw"], writes=["diagW"])
        P.op("act", lambda e: e.activation(out=A_b[:], in_=A_b[:], func=AF.Exp), reads=["A_b"], writes=["A_b"])
        P.op("dve", lambda e: e.tensor_scalar(out=A_b[:], in0=A_b[:], scalar1=-1.0, scalar2=None, op0=ALU.mult), reads=["A_b"], writes=["A_b"])
        P.op("dve", lambda e: e.tensor_scalar(out=nsink[:], in0=sink[:], scalar1=-1.0, scalar2=None, op0=ALU.mult), reads=["sink"], writes=["nsink"])
        P.op("dve", lambda e: e.tensor_scalar(out=maskb[:], in0=flag[:], scalar1=-NEG, scalar2=NEG, op0=ALU.mult, op1=ALU.add),
             reads=["flag"], writes=["maskb"])

        def fdg(e):
            for h in range(8):
                ins = e.tensor_scalar(out=Dg[:, h, :], in0=identf[:], scalar1=dsk[:, h:h + 1], scalar2=None, op0=ALU.mult)
            return ins
        P.op("dve", fdg, reads=["identf", "dsk"], writes=["Dg"])

        s_win = P.dma_sem("s_win")
        P.dma("pool", [(lambda e, kk=kk: e.dma_start(out=w_in_bf[:, 2 * kk:2 * kk + 2, :],
                                                     in_=w_in[kk * 256:(kk + 1) * 256, :].rearrange("(k p) n -> p k n", p=128)))
                       for kk in range(4)], s_win, writes=["w_in"])
        s_wo = P.dma_sem("s_wo")
        P.dma("pool", [(lambda e, kk=kk: e.dma_start(out=w_o_bf[:, 4 * kk:4 * kk + 4, :],
                                                     in_=w_o[kk * 512:(kk + 1) * 512, :].rearrange("(k p) n -> p k n", p=128)))
                       for kk in range(2)], s_wo, writes=["w_o_raw"])

        def fwo(e):
            for k in range(8):
                e.tensor_scalar(out=w_o_bf[:, k, :], in0=w_o_bf[:, k, :], scalar1=mnc[:, k:k + 1], scalar2=None, op0=ALU.mult)
            for k in range(8):
                ins = e.tensor_tensor(out=w_o_bf[:, k, :], in0=w_o_bf[:, k, :], in1=gate1_b, op=ALU.mult)
            return ins
        P.op("dve", fwo, reads=["w_o_raw", "mnc"] + ["gate1_b%d" % i for i in range(4)], writes=["w_o", "x1"])
        s_scr = P.dma_sem("s_scr")
        P.dma("pool", [(lambda e, j=j, u=u: e.dma_start(out=wgu_s[j].rearrange("p (k n) -> p k n", k=8)[:, :, u * 128:(u + 1) * 128],
                                                        in_=w_gu[:, u * D_FF + j * 128:u * D_FF + (j + 1) * 128].rearrange("(k p) n -> p k n", p=128)))
                       for j in range(22) for u in range(2)], s_scr, writes=["scr_gu"])
        s_dnl = P.dma_sem("s_dnl")
        s_dns = P.dma_sem("s_dns")
        for j in range(22):
            P.dma("pool", [lambda e, j=j: e.dma_start(out=xn[:], in_=w_dn[j * 128:(j + 1) * 128, :])], s_dnl, writes=["xn"])
            P.op("dve", lambda e: e.tensor_tensor(out=xn[:], in0=xn[:], in1=gate2h_b, op=ALU.mult),
                 reads=["xn", "gate2h_b"], writes=["xn", "x1"])
            P.dma("sp", [lambda e, j=j, nh=nh: e.dma_start(out=wdn_s[nh, j], in_=xn[:, nh * 512:(nh + 1) * 512]) for nh in range(2)],
                  s_dns, reads=["xn"], writes=["scr_dn%d" % j])

        xsem = [P.dma_sem("xs%d" % i) for i in range(2)]
        osem = [P.dma_sem("os%d" % i) for i in range(2)]
        wsem = [P.dma_sem("ws%d" % i) for i in range(NSLOT)]

        def load_x(gi):
            pre = gi < NG
            src = x_pre if pre else x_main
            g = gi if pre else gi - NG
            sl = gi % 2
            P.dma("sp", [lambda e, src=src, g=g, sl=sl: e.dma_start(
                out=xg[sl][:], in_=src[g * 256:(g + 1) * 256, :].rearrange("(t p) d -> p t d", p=128))],
                xsem[sl], writes=["x%d" % sl])

        dsem_w = [P.dma_sem("wd%d" % i) for i in range(NDSLOT)]

        def load_w(j):
            sl = j % NSLOT
            P.dma("sp", [lambda e, j=j, sl=sl: e.dma_start(out=wgu[sl][:], in_=wgu_s[j].rearrange("p (k n) -> p k n", k=8))],
                  wsem[sl], reads=["scr_gu"], writes=["wgu%d" % sl])

        def load_wd(idx):
            nh, j = idx // 22, idx % 22
            sl = idx % NDSLOT
            P.dma("sp", [lambda e, j=j, nh=nh, sl=sl: e.dma_start(out=wdn[sl][:], in_=wdn_s[nh, j])],
                  dsem_w[sl], reads=["scr_dn%d" % j], writes=["wdn%d" % sl])

        def rms_stats(src_ap, src_res, scale, col, eps_t):
            P.op("act", lambda e: e.activation(out=junk[:, 0:src_ap.shape[-1]], in_=src_ap, func=AF.Square, scale=scale,
                                               accum_out=ms[:, col:col + 1]),
                 reads=list(src_res) if isinstance(src_res, (list, tuple)) else [src_res], writes=["ms%d" % col])
            P.op("pool", lambda e: e.tensor_tensor(out=mse[:, col:col + 1], in0=ms[:, col:col + 1], in1=eps_t[:, 0:1], op=ALU.add),
                 reads=["ms%d" % col], writes=["mse%d" % col])
            P.op("pool", lambda e: e.tensor_tensor(out=rstd[:, col:col + 1], in0=mse[:, col:col + 1], in1=mhalf[:, 0:1], op=ALU.pow),
                 reads=["mse%d" % col], writes=["rstd%d" % col])

        def norm_transpose(xs_ap, xres, gvec, svec, dstT, dres, t):
            rms_stats(xs_ap, xres, 1.0 / 32.0, 0, eps1)
            P.op("dve", lambda e: e.tensor_scalar(out=xn[:], in0=xs_ap, scalar1=rstd[:, 0:1], scalar2=None, op0=ALU.mult),
                 reads=[xres, "rstd0"], writes=["xn"])
            pt, ptr = nextT()

            def ftr(e):
                for k in range(8):
                    ins = e.transpose(out=pt[:, k * 128:(k + 1) * 128], in_=xn[:, k * 128:(k + 1) * 128], identity=ident[:])
                return ins
            P.op("pe", ftr, reads=["xn", "ident"], writes=[ptr])

            def fev(e):
                for k in range(8):
                    ins = e.activation(out=dstT[:, k, t * 128:(t + 1) * 128], in_=pt[:, k * 128:(k + 1) * 128], func=AF.Identity,
                                       scale=gvec[:, k:k + 1], bias=svec[:, k:k + 1])
                return ins
            P.op("act", fev, reads=[ptr, "g1", "sh1", "g2", "sh2"], writes=[dres])

        import os
        STAGE = int(os.environ.get("KSTAGE", "9"))
        if STAGE >= 2:
            load_x(0)
        for gi in range(2 * NG):
            pre = gi < NG
            P.tag = gi
            if STAGE < 2 or (STAGE == 2 and not pre):
                break
            g_local = gi if pre else gi - NG
            sl = gi % 2
            xr = "x%d" % sl
            xs = xg[sl]
            if gi + 1 < 2 * NG:
                load_x(gi + 1)
            if gi == NG:
                P.op("dve", lambda e: e.tensor_scalar(out=S[:], in0=S[:], scalar1=flag[:, 0:1], scalar2=None, op0=ALU.mult),
                     reads=["S", "flag"], writes=["S"])
                P.op("act", lambda e: e.activation(out=Sbf[:], in_=S[:], func=AF.Copy), reads=["S"], writes=["Sbf"])
                P.op("dve", lambda e: e.tensor_scalar(out=raw[:, :, 0:3], in0=raw[:, :, 0:3], scalar1=flag[:, 0:1], scalar2=None, op0=ALU.mult),
                     reads=["raw", "flag"], writes=["raw"])
            if not pre:
                for j in range(NSLOT):
                    load_w(j)
                for j in range(NDSLOT):
                    load_wd(j)

            for t in range(2):
                norm_transpose(xs[:, t, :], xr, g1, sh1, hT, "hT", t)

            dump("hT", hT[:], ["hT"], (not pre) and g_local == 0)
            dump("g1", g1[:], ["g1"], (not pre) and g_local == 0); dump("sh1", sh1[:], ["sh1"], (not pre) and g_local == 0)
            dump("S0", S[:], ["S"], (not pre) and g_local == 0)
            def fm_chunks(cols_list, pm, pr):
                def f(e):
                    for ci, c0 in enumerate(cols_list):
                        for k in range(8):
                            ins = e.matmul(pm[:, ci * 256:(ci + 1) * 256], lhsT=w_in_bf[:, k, c0:c0 + 128], rhs=hT[:, k, :],
                                           start=(k == 0), stop=(k == 7))
                    return ins
                P.op("pe", f, reads=["w_in", "hT"], writes=[pr])

            if not pre:
                for i in range(2):
                    pm, pr = nextM()
                    fm_chunks([(2 * i) * 128, (2 * i + 1) * 128], pm, pr)
                    P.op("act", lambda e, pm=pm, i=i: e.activation(out=qT[:, 2 * i:2 * i + 2, :],
                                                                    in_=pm[:, 0:512].rearrange("p (c n) -> p c n", c=2), func=AF.Copy),
                         reads=[pr], writes=["qT"])
            pm, pr = nextM()
            fm_chunks([512], pm, pr)
            P.op("act", lambda e, pm=pm: e.activation(out=kT[:, 128:384], in_=pm[:, 0:256], func=AF.Copy), reads=[pr], writes=["kT"])
            for i in range(4):
                pm, pr = nextM()
                fm_chunks([1280 + (2 * i) * 128, 1280 + (2 * i + 1) * 128], pm, pr)
                P.op("act", lambda e, pm=pm, i=i: e.activation(out=raw[:, 2 * i:2 * i + 2, 3:259],
                                                                in_=pm[:, 0:512].rearrange("p (c n) -> p c n", c=2), func=AF.Copy),
                     reads=[pr], writes=["raw"])
            for i in range(4):
                b = i % 2
                pm, pr = nextM()

                def fcv(e, pm=pm, i=i):
                    for ci in range(2):
                        c = 2 * i + ci
                        for k in range(4):
                            ins = e.matmul(pm[:, ci * 256:(ci + 1) * 256], lhsT=diagW[:, c, k, :], rhs=raw[:, c, k:k + 256],
                                           start=(k == 0), stop=(k == 3))
                    return ins
                P.op("pe", fcv, reads=["diagW", "raw"], writes=[pr])

                def fcu(e, pm=pm, i=i, b=b):
                    for ci in range(2):
                        ins = e.activation(out=acc[b][:, ci, :], in_=pm[:, ci * 256:(ci + 1) * 256], func=AF.Identity,
                                           bias=convb[:, 2 * i + ci:2 * i + ci + 1], scale=1.0)
                    return ins
                P.op("act", fcu, reads=[pr, "convb"], writes=["acc%d" % b])
                P.op("act", lambda e, b=b: e.activation(out=cth2[b][:], in_=acc[b][:], func=AF.Tanh), reads=["acc%d" % b], writes=["cth2_%d" % b])
                P.op("dve", lambda e, i=i, b=b: e.scalar_tensor_tensor(out=cvo[:, 2 * i:2 * i + 2, :], in0=cth2[b][:], scalar=1.0, in1=acc[b][:],
                                                                       op0=ALU.add, op1=ALU.mult),
                     reads=["cth2_%d" % b, "acc%d" % b], writes=["cvo"])
            P.op("dve", lambda e: e.tensor_copy(out=raw[:, :, 0:3], in_=raw[:, :, 256:259]), reads=["raw"], writes=["raw"])

            dump("qT", qT[:], ["qT"], (not pre) and g_local == 0); dump("kT", kT[:], ["kT"], (not pre) and g_local == 0); dump("cvo", cvo[:], ["cvo"], (not pre) and g_local == 0)
            for t in range(2):
                tsl = slice(t * 128, (t + 1) * 128)
                pa, par = nextM()
                if pre:
                    def f(e, pa=pa, tsl=tsl):
                        for k in range(8):
                            e.matmul(pa[:, 0:128], lhsT=hT[:, k, tsl], rhs=w_in_bf[:, k, 640:768], start=(k == 0), stop=(k == 7))
                        for k in range(8):
                            ins = e.matmul(pa[:, 128:136], lhsT=hT[:, k, tsl], rhs=w_in_bf[:, k, 2304:2312], start=(k == 0), stop=(k == 7))
                        return ins
                    P.op("pe", f, reads=["hT", "w_in"], writes=[par])
                    P.op("act", lambda e, pa=pa, t=t: e.activation(out=vx[:, 1 + t, :], in_=pa[:, 0:128], func=AF.Copy), reads=[par], writes=["vx"])
                    P.op("dve", lambda e, pa=pa, t=t: e.tensor_tensor(out=dtu[:, t * 8:(t + 1) * 8], in0=pa[:, 128:136], in1=dtb[:], op=ALU.add),
                         reads=[par, "dtb"], writes=["dtu"])
                else:
                    pb2, pbr = nextM()

                    def f(e, pa=pa, pb2=pb2, tsl=tsl):
                        for k in range(8):
                            e.matmul(pa[:, 0:512], lhsT=hT[:, k, tsl], rhs=w_in_bf[:, k, 640:1152], start=(k == 0), stop=(k == 7))
                        for k in range(8):
                            e.matmul(pb2[:, 0:128], lhsT=hT[:, k, tsl], rhs=w_in_bf[:, k, 1152:1280], start=(k == 0), stop=(k == 7))
                        for k in range(8):
                            ins = e.matmul(pb2[:, 128:136], lhsT=hT[:, k, tsl], rhs=w_in_bf[:, k, 2304:2312], start=(k == 0), stop=(k == 7))
                        return ins
                    P.op("pe", f, reads=["hT", "w_in"], writes=[par, pbr])
                    P.op("act", lambda e, pa=pa, t=t: e.activation(out=vx[:, 1 + t, :], in_=pa[:, 0:128], func=AF.Copy), reads=[par], writes=["vx"])

                    def fth(e, pa=pa, pb2=pb2):
                        e.activation(out=thz[:, 0:384], in_=pa[:, 128:512], func=AF.Tanh, scale=0.5)
                        return e.activation(out=thz[:, 384:512], in_=pb2[:, 0:128], func=AF.Tanh, scale=0.5)
                    P.op("act", fth, reads=[par, pbr], writes=["thz"])

                    def fgz(e, pa=pa, pb2=pb2, t=t):
                        e.scalar_tensor_tensor(out=gz[:, t, 0:384], in0=thz[:, 0:384], scalar=1.0, in1=pa[:, 128:512], op0=ALU.add, op1=ALU.mult)
                        return e.scalar_tensor_tensor(out=gz[:, t, 384:512], in0=thz[:, 384:512], scalar=1.0, in1=pb2[:, 0:128],
                                                      op0=ALU.add, op1=ALU.mult)
                    P.op("dve", fgz, reads=["thz", par, pbr], writes=["gz"])
                    P.op("dve", lambda e, pb2=pb2, t=t: e.tensor_tensor(out=dtu[:, t * 8:(t + 1) * 8], in0=pb2[:, 128:136], in1=dtb[:], op=ALU.add),
                         reads=[pbr, "dtb"], writes=["dtu"])

            au, tt, dd, ww, w2s, rr_, lnp, relu = sp_t
            P.op("act", lambda e: e.activation(out=au[:], in_=dtu[:], func=AF.Abs), reads=["dtu"], writes=["sp_au"])
            P.op("act", lambda e: e.activation(out=tt[:], in_=au[:], func=AF.Exp, scale=-1.0), reads=["sp_au"], writes=["sp_tt"])
            P.op("dve", lambda e: e.tensor_scalar(out=dd[:], in0=tt[:], scalar1=2.0, scalar2=None, op0=ALU.add), reads=["sp_tt"], writes=["sp_dd"])
            P.op("dve", lambda e: e.reciprocal(out=dd[:], in_=dd[:]), reads=["sp_dd"], writes=["sp_dd"])
            P.op("dve", lambda e: e.tensor_tensor(out=ww[:], in0=tt[:], in1=dd[:], op=ALU.mult), reads=["sp_tt", "sp_dd"], writes=["sp_ww"])
            P.op("dve", lambda e: e.tensor_tensor(out=w2s[:], in0=ww[:], in1=ww[:], op=ALU.mult), reads=["sp_ww"], writes=["sp_w2"])
            P.op("dve", lambda e: e.tensor_scalar(out=rr_[:], in0=w2s[:], scalar1=1.0 / 13.0, scalar2=None, op0=ALU.mult), reads=["sp_w2"], writes=["sp_rr"])
            for cst in (1.0 / 11.0, 1.0 / 9.0, 1.0 / 7.0, 1.0 / 5.0, 1.0 / 3.0):
                P.op("dve", lambda e, cst=cst: e.scalar_tensor_tensor(out=rr_[:], in0=rr_[:], scalar=cst, in1=w2s[:], op0=ALU.add, op1=ALU.mult),
                     reads=["sp_rr", "sp_w2"], writes=["sp_rr"])
            P.op("dve", lambda e: e.scalar_tensor_tensor(out=lnp[:], in0=rr_[:], scalar=1.0, in1=ww[:], op0=ALU.add, op1=ALU.mult),
                 reads=["sp_rr", "sp_ww"], writes=["sp_ln"])
            P.op("dve", lambda e: e.tensor_scalar(out=relu[:], in0=dtu[:], scalar1=0.0, scalar2=None, op0=ALU.max), reads=["dtu"], writes=["sp_relu"])
            P.op("dve", lambda e: e.scalar_tensor_tensor(out=dtv[:], in0=lnp[:], scalar=2.0, in1=relu[:], op0=ALU.mult, op1=ALU.add),
                 reads=["sp_ln", "sp_relu"], writes=["dtv"])

            dump("vx", vx[:], ["vx"], (not pre) and g_local == 0); dump("gz", gz[:], ["gz"], (not pre) and g_local == 0); dump("dtv", dtv[:], ["dtv"], (not pre) and g_local == 0)
            for t in range(2):
                tsl = slice(t * 128, (t + 1) * 128)
                gt_first = (not pre) and g_local == 0 and t == 0
                dt_ap = dtv[:, t * 8:(t + 1) * 8]
                P.op("dve", lambda e, dt_ap=dt_ap: e.tensor_tensor(out=a_t[:], in0=dt_ap, in1=A_b[:], op=ALU.mult), reads=["dtv", "A_b"], writes=["a_t"])
                pc_, pcr = nextM()

                def fcs(e, pc_=pc_):
                    e.matmul(pc_[:, 0:8], lhsT=tri[:], rhs=a_t[:], start=True, stop=True)
                    e.matmul(pc_[:, 8:16], lhsT=ones[:], rhs=a_t[:], start=True, stop=True)
                    return e.matmul(pc_[0:64, 16:24], lhsT=ident[:, 0:64], rhs=ident[:, 0:8], start=True, stop=True)
                P.op("pe", fcs, reads=["tri", "ones", "a_t"], writes=[pcr])
                P.op("dve", lambda e, pc_=pc_: e.tensor_copy(out=e_in[:, 0:16], in_=pc_[:, 0:16]), reads=[pcr], writes=["e_in"])
                P.op("dve", lambda e: e.tensor_tensor(out=e_in[:, 16:24], in0=e_in[:, 8:16], in1=e_in[:, 0:8], op=ALU.subtract),
                     reads=["e_in"], writes=["e_in2"])
                P.op("act", lambda e: e.activation(out=e24[:], in_=e_in[:], func=AF.Exp), reads=["e_in", "e_in2"], writes=["e24"])
                P.op("dve", lambda e, dt_ap=dt_ap: e.tensor_tensor(out=w2[:], in0=dt_ap, in1=e24[:, 16:24], op=ALU.mult), reads=["dtv", "e24"], writes=["w2"])
                pt, ptr = nextT()

                def ftx(e, pt=pt, tsl=tsl):
                    for c in range(6):
                        ins = e.transpose(out=pt[:, c * 128:(c + 1) * 128], in_=cvo[:, c, tsl], identity=ident[:])
                    return ins
                P.op("pe", ftx, reads=["cvo", "ident"], writes=[ptr])
                P.op("dve", lambda e, pt=pt: e.tensor_tensor(out=xdd[:].rearrange("p (h d) -> p h d", h=8),
                                                            in0=pt[:, 0:512].rearrange("p (h d) -> p h d", h=8),
                                                            in1=w2[:].unsqueeze(2).to_broadcast([128, 8, 64]), op=ALU.mult),
                     reads=[ptr, "w2"], writes=["xdd"])
                P.op("act", lambda e, pt=pt: e.activation(out=Btok[:], in_=pt[:, 512:768], func=AF.Copy), reads=[ptr], writes=["Btok"])
                if (not pre) and STAGE not in (31, 33):
                    P.op("dve", lambda e, pt=pt, dt_ap=dt_ap: e.tensor_tensor(out=xdt[:].rearrange("p (h d) -> p h d", h=8),
                                                                              in0=pt[:, 0:512].rearrange("p (h d) -> p h d", h=8),
                                                                              in1=dt_ap.unsqueeze(2).to_broadcast([128, 8, 64]), op=ALU.mult),
                         reads=[ptr, "dtv"], writes=["xdt"])
                    P.op("act", lambda e, pt=pt: e.activation(out=xsb[:], in_=pt[:, 0:512], func=AF.Copy), reads=[ptr, "xdt", "xdd"], writes=["xsb"])
                    P.op("dve", lambda e: e.tensor_scalar(out=nAcs[:], in0=e_in[:, 0:8], scalar1=-1.0, scalar2=None, op0=ALU.mult),
                         reads=["e_in"], writes=["nAcs"])
                    pcb, pcbr = nextM()

                    def fcb(e, pcb=pcb, tsl=tsl):
                        for g in range(2):
                            ins = e.matmul(pcb[:, g * 128:(g + 1) * 128], lhsT=cvo[:, 4 + g, tsl], rhs=cvo[:, 6 + g, tsl], start=True, stop=True)
                        return ins
                    P.op("pe", fcb, reads=["cvo"], writes=[pcbr])
                    pl = [nextM(), nextM()]

                    def fl(e, pl=pl, pcb=pcb):
                        for h in range(8):
                            o = pl[h // 4][0][:, (h % 4) * 128:(h % 4 + 1) * 128]
                            ins = e.matmul(o, lhsT=a_t[:, h:h + 1].to_broadcast([128, 128]), rhs=tri[:], start=True, stop=True)
                        return e.matmul(pcb[0:64, 256:264], lhsT=ident[:, 0:64], rhs=ident[:, 0:8], start=True, stop=True)
                    P.op("pe", fl, reads=["a_t", "tri", "ident"], writes=[pl[0][1], pl[1][1]])
                    scv = sc[:].rearrange("p a b -> p (a b)").rearrange("p (h l) -> p h l", h=8)

                    def fl2(e, pl=pl, scv=scv):
                        for h in range(8):
                            ins = e.scalar_tensor_tensor(out=scv[:, h, :], in0=pl[h // 4][0][:, (h % 4) * 128:(h % 4 + 1) * 128],
                                                         scalar=nAcs[:, h:h + 1], in1=maskTf[:], op0=ALU.add, op1=ALU.add)
                        return ins
                    P.op("dve", fl2, reads=[pl[0][1], pl[1][1], "nAcs", "maskTf"], writes=["sc"])
                    P.op("act", lambda e, scv=scv: e.activation(out=LT[:], in_=scv, func=AF.Exp), reads=["sc"], writes=["LT"])

                    def fwt(e, pcb=pcb):
                        for h in range(8):
                            g = h // 4
                            ins = e.tensor_tensor(out=WT[:, h, :], in0=pcb[:, g * 128:(g + 1) * 128], in1=LT[:, h, :], op=ALU.mult)
                        return ins
                    P.op("dve", fwt, reads=[pcbr, "LT"], writes=["WT"])
                    py, pyr = nextM()

                    SKIPB = (STAGE == 34)
                    if SKIPB:
                        P.op = lambda *a, **k: None

                    def fy(e, py=py):
                        for h in range(8):
                            hs = slice(h * 64, (h + 1) * 64)
                            e.matmul(py[:, hs], lhsT=WT[:, h, :], rhs=xdt[:, hs], start=True, stop=False)
                            ins = e.matmul(py[:, hs], lhsT=Dg[:, h, :], rhs=xsb[:, hs], start=False, stop=True)
                        return ins
                    P.op("pe", fy, reads=["WT", "xdt", "Dg", "xsb"], writes=[pyr])
                    pyo, pyor = nextM()

                    def fyo(e, pyo=pyo, tsl=tsl):
                        for g in range(2):
                            ins = e.matmul(pyo[:, g * 256:(g + 1) * 256], lhsT=cvo[:, 6 + g, tsl], rhs=Sbf[:, g * 256:(g + 1) * 256], start=True, stop=True)
                        return ins
                    P.op("pe", fyo, reads=["cvo", "Sbf"], writes=[pyor])
                    P.op("dve", lambda e, pyo=pyo: e.tensor_tensor(out=yt[:].rearrange("p (h d) -> p h d", h=8),
                                                                  in0=pyo[:, 0:512].rearrange("p (h d) -> p h d", h=8),
                                                                  in1=e24[:, 0:8].unsqueeze(2).to_broadcast([128, 8, 64]), op=ALU.mult),
                         reads=[pyor, "e24"], writes=["yt"])
                    P.op("dve", lambda e, py=py: e.tensor_tensor(out=yt[:], in0=py[:, 0:512], in1=yt[:], op=ALU.add), reads=[pyr, "yt"], writes=["yt"])
                    P.op("dve", lambda e, t=t: e.tensor_tensor(out=yg[:], in0=yt[:], in1=gz[:, t, :], op=ALU.mult), reads=["yt", "gz"], writes=["yg"])
                    for g in range(2):
                        rms_stats(yg[:, g * 256:(g + 1) * 256], "yg", 1.0 / 16.0, 1 + g, eps4)

                    def fmx(e):
                        for g in range(2):
                            ins = e.tensor_scalar(out=mixb[:, 512 + g * 256:512 + (g + 1) * 256], in0=yg[:, g * 256:(g + 1) * 256],
                                                  scalar1=rstd[:, 1 + g:2 + g], scalar2=None, op0=ALU.mult)
                        return ins
                    P.op("dve", fmx, reads=["yg", "rstd1", "rstd2"], writes=["mixb_s"])
                    if SKIPB:
                        del P.op
                if (not pre) and STAGE not in (31, 33):
                    dump("e24", e24[:], ["e24"], (not pre) and g_local == 0 and t == 0); dump("LT", LT[:], ["LT"], (not pre) and g_local == 0 and t == 0)
                    dump("WT", WT[:], ["WT"], (not pre) and g_local == 0 and t == 0); dump("yt", yt[:], ["yt"], (not pre) and g_local == 0 and t == 0)
                    dump("yg", yg[:], ["yg"], (not pre) and g_local == 0 and t == 0)
                pst, pstr = nextM()

                def fst(e, pst=pst):
                    for g in range(2):
                        ins = e.matmul(pst[:, g * 256:(g + 1) * 256], lhsT=Btok[:, g * 128:(g + 1) * 128], rhs=xdd[:, g * 256:(g + 1) * 256], start=True, stop=True)
                    return ins
                P.op("pe", fst, reads=["Btok", "xdd"], writes=[pstr])
                P.op("pool", lambda e: e.tensor_tensor(out=S[:].rearrange("p (h d) -> p h d", h=8), in0=S[:].rearrange("p (h d) -> p h d", h=8),
                                                       in1=e24[:, 8:16].unsqueeze(2).to_broadcast([128, 8, 64]), op=ALU.mult),
                     reads=["S", "e24"], writes=["S"])
                P.op("dve", lambda e, pst=pst: e.tensor_tensor(out=S[:], in0=pst[:, 0:512], in1=S[:], op=ALU.add), reads=[pstr, "S"], writes=["S"])
                P.op("act", lambda e: e.activation(out=Sbf[:], in_=S[:], func=AF.Copy), reads=["S"], writes=["Sbf"])

                if pre or STAGE in (31, 32, 34):
                    continue
                for hp in range(2):
                    ksl = slice(hp * 64, (hp + 1) * 64)
                    pS = [nextM(), nextM()]

                    def fsc(e, pS=pS, ksl=ksl, t=t, tsl=tsl):
                        for c in range(4):
                            ins = e.matmul(pS[c // 2][0][:, (c % 2) * 256:(c % 2 + 1) * 256], lhsT=qT[ksl, c, tsl],
                                           rhs=kT[ksl, t * 128:t * 128 + 256], start=True, stop=True)
                        return ins
                    P.op("pe", fsc, reads=["qT", "kT"], writes=[pS[0][1], pS[1][1]])

                    def fsb(e, pS=pS, hp=hp):
                        for i in range(2):
                            ins = e.scalar_tensor_tensor(out=sc[:, 2 * i:2 * i + 2, :], in0=pS[i][0][:, 0:512].rearrange("p (c n) -> p c n", c=2),
                                                         scalar=0.125, in1=biasf[:, 4 * hp + 2 * i:4 * hp + 2 * i + 2, :], op0=ALU.mult, op1=ALU.add)
                        return ins
                    P.op("dve", fsb, reads=[pS[0][1], pS[1][1], "biasf"], writes=["sc"])
                    if gt_first:
                        P.op("dve", lambda e: e.tensor_scalar(out=sc[:, :, 0:128], in0=sc[:, :, 0:128], scalar1=maskb[:, 0:1], scalar2=None, op0=ALU.add),
                             reads=["sc", "maskb"], writes=["sc"])
                    P.op("dve", lambda e: e.tensor_reduce(out=rmax[:], in_=sc[:], axis=AX.X, op=ALU.max), reads=["sc"], writes=["rmax"])
                    P.op("dve", lambda e, hp=hp: e.scalar_tensor_tensor(out=negm[:], in0=rmax[:], scalar=-1.0, in1=nsink[:, 4 * hp:4 * hp + 4],
                                                                       op0=ALU.mult, op1=ALU.min),
                         reads=["rmax", "nsink"], writes=["negm"])

                    def fex(e):
                        for c in range(4):
                            ins = e.activation(out=pb[:, c, :], in_=sc[:, c, :], func=AF.Exp, bias=negm[:, c:c + 1], scale=1.0,
                                               accum_out=rsum[:, c:c + 1])
                        return ins
                    P.op("act", fex, reads=["sc", "negm"], writes=["pb", "rsum"])
                    P.op("dve", lambda e, hp=hp: e.tensor_tensor(out=stmp[:], in0=sink[:, 4 * hp:4 * hp + 4], in1=negm[:], op=ALU.add),
                         reads=["sink", "negm"], writes=["stmp"])
                    P.op("act", lambda e: e.activation(out=es[:], in_=stmp[:], func=AF.Exp), reads=["stmp"], writes=["es"])
                    P.op("dve", lambda e: e.tensor_tensor(out=den[:], in0=rsum[:], in1=es[:], op=ALU.add), reads=["rsum", "es"], writes=["den"])
                    P.op("dve", lambda e: e.reciprocal(out=rden[:], in_=den[:]), reads=["den"], writes=["rden"])
                    pt, ptr = nextT()

                    def ftp(e, pt=pt):
                        for c in range(4):
                            for j in range(2):
                                ins = e.transpose(out=pt[:, (2 * c + j) * 128:(2 * c + j + 1) * 128], in_=pb[:, c, j * 128:(j + 1) * 128], identity=ident[:])
                        return ins
                    P.op("pe", ftp, reads=["pb", "ident"], writes=[ptr])
                    P.op("act", lambda e, pt=pt: e.activation(out=pTs[:], in_=pt[:, 0:1024].rearrange("p (c n) -> p c n", c=8), func=AF.Copy),
                         reads=[ptr], writes=["pTs"])
                    po, por = nextM()

                    def fpv(e, po=po, hp=hp, t=t):
                        for c in range(4):
                            e.matmul(po[:, c * 64:(c + 1) * 64], lhsT=pTs[:, 2 * c, :], rhs=vx[:, t, hp * 64:(hp + 1) * 64], start=True, stop=False)
                            ins = e.matmul(po[:, c * 64:(c + 1) * 64], lhsT=pTs[:, 2 * c + 1, :], rhs=vx[:, t + 1, hp * 64:(hp + 1) * 64], start=False, stop=True)
                        return ins
                    P.op("pe", fpv, reads=["pTs", "vx"], writes=[por])
                    P.op("dve", lambda e, po=po, hp=hp: e.tensor_tensor(out=ya[:, hp * 256:(hp + 1) * 256].rearrange("p (h d) -> p h d", h=4),
                                                                       in0=po[:, 0:256].rearrange("p (h d) -> p h d", h=4),
                                                                       in1=rden[:].unsqueeze(2).to_broadcast([128, 4, 64]), op=ALU.mult),
                         reads=[por, "rden"], writes=["ya%d" % hp])
                rms_stats(ya[:], ["ya0", "ya1"], 512.0 ** -0.5, 3, eps1)
                P.op("dve", lambda e: e.tensor_scalar(out=mixb[:, 0:512], in0=ya[:], scalar1=rstd[:, 3:4], scalar2=None, op0=ALU.mult),
                     reads=["ya0", "ya1", "rstd3", "xn"], writes=["mixb_a"])
                dump("ya", ya[:], ["ya0", "ya1"], (not pre) and g_local == 0 and t == 0); dump("mixb", mixb[:], ["mixb_a", "mixb_s"], (not pre) and g_local == 0 and t == 0)
                pt, ptr = nextT()

                def ftm(e, pt=pt):
                    for k in range(8):
                        ins = e.transpose(out=pt[:, k * 128:(k + 1) * 128], in_=mixb[:, k * 128:(k + 1) * 128], identity=ident[:])
                    return ins
                P.op("pe", ftm, reads=["mixb_a", "mixb_s", "ident"], writes=[ptr])
                P.op("act", lambda e, pt=pt, tsl=tsl: e.activation(out=mixT[:, :, tsl], in_=pt[:, 0:1024].rearrange("p (c n) -> p c n", c=8), func=AF.Copy),
                     reads=[ptr], writes=["mixT"])

            P.op("dve", lambda e: e.tensor_copy(out=kT[:, 0:128], in_=kT[:, 256:384]), reads=["kT"], writes=["kT"])
            P.op("dve", lambda e: e.tensor_copy(out=vx[:, 0, :], in_=vx[:, 2, :]), reads=["vx"], writes=["vx"])
            if pre or STAGE in (3, 31, 32, 33, 34):
                continue

            for t in range(2):
                tsl = slice(t * 128, (t + 1) * 128)
                for nh in range(2):
                    pm, pr = nextM()

                    def fo(e, pm=pm, nh=nh, tsl=tsl):
                        for k in range(8):
                            ins = e.matmul(pm[:, 0:512], lhsT=mixT[:, k, tsl], rhs=w_o_bf[:, k, nh * 512:(nh + 1) * 512], start=(k == 0), stop=(k == 7))
                        return ins
                    P.op("pe", fo, reads=["mixT", "w_o"], writes=[pr])
                    P.op("dve", lambda e, pm=pm, nh=nh, t=t, xs=xs: e.tensor_tensor(out=xs[:, t, nh * 512:(nh + 1) * 512], in0=pm[:, 0:512],
                                                                               in1=xs[:, t, nh * 512:(nh + 1) * 512], op=ALU.add),
                         reads=[pr, xr], writes=[xr])
                norm_transpose(xs[:, t, :], xr, g2, sh2, hT2, "hT2", t)

            if STAGE == 4:
                continue
            dump("x1", xs[:], [xr], (not pre) and g_local == 0); dump("h2T", hT2[:], ["hT2"], (not pre) and g_local == 0); dump("mixT", mixT[:], ["mixT"], (not pre) and g_local == 0)
            for j in range(22):
                sl_w = j % NSLOT
                b = j % 2
                pg, pgr = psM[b], "psM%d" % b

                def fgu(e, pg=pg, sl_w=sl_w):
                    for k in range(8):
                        e.matmul(pg[:, 0:256], lhsT=wgu[sl_w][:, k, 0:128], rhs=hT2[:, k, :], start=(k == 0), stop=(k == 7))
                    for k in range(8):
                        ins = e.matmul(pg[:, 256:512], lhsT=wgu[sl_w][:, k, 128:256], rhs=hT2[:, k, :], start=(k == 0), stop=(k == 7))
                    return ins
                P.op("pe", fgu, reads=["wgu%d" % sl_w, "hT2"], writes=[pgr])
                P.op("act", lambda e, pg=pg, b=b: e.activation(out=thg[b][:], in_=pg[:, 0:256], func=AF.Tanh, scale=0.5), reads=[pgr], writes=["thg%d" % b])
                P.op("act", lambda e, pg=pg, b=b: e.activation(out=ub[b][:], in_=pg[:, 256:512], func=AF.Copy), reads=[pgr], writes=["ub_%d" % b])
                P.op("dve", lambda e, pg=pg, b=b: e.scalar_tensor_tensor(out=t2[b][:], in0=thg[b][:], scalar=1.0, in1=pg[:, 0:256], op0=ALU.add, op1=ALU.mult),
                     reads=["thg%d" % b, pgr], writes=["t2_%d" % b])
                P.op("pool", lambda e, b=b, j=j: e.tensor_tensor(out=actT[:, j, :], in0=t2[b][:], in1=ub[b][:], op=ALU.mult),
                     reads=["t2_%d" % b, "ub_%d" % b], writes=["actT%d" % j])
                if j + NSLOT < 22:
                    load_w(j + NSLOT)
            for nh in range(2):
                for j in range(22):
                    idx = nh * 22 + j
                    sl_d = idx % NDSLOT

                    def fdn(e, sl_d=sl_d, j=j):
                        for t in range(2):
                            ins = e.matmul(psM[t][:, 0:512], lhsT=actT[:, j, t * 128:(t + 1) * 128], rhs=wdn[sl_d][:],
                                           start=(j == 0), stop=(j == 21))
                        return ins
                    P.op("pe", fdn, reads=["actT%d" % j, "wdn%d" % sl_d], writes=["psM0", "psM1"])
                    if idx + NDSLOT < 44:
                        load_wd(idx + NDSLOT)
                for t in range(2):
                    P.op("dve", lambda e, nh=nh, t=t, xs=xs: e.tensor_tensor(out=xs[:, t, nh * 512:(nh + 1) * 512], in0=psM[t][:, 0:512],
                                                                            in1=xs[:, t, nh * 512:(nh + 1) * 512], op=ALU.add),
                         reads=["psM%d" % t, xr], writes=[xr])
            for t in range(2):
                rms_stats(xs[:, t, :], xr, 1.0 / 32.0, 4, eps1)
                P.op("dve", lambda e, t=t, xs=xs: e.scalar_tensor_tensor(out=xs[:, t, :], in0=xs[:, t, :], scalar=rstd[:, 4:5], in1=fnorm[:], op0=ALU.mult, op1=ALU.mult),
                     reads=[xr, "rstd4", "fnorm"], writes=[xr])
            P.dma("sp", [lambda e, g_local=g_local, xs=xs: e.dma_start(out=out[g_local * 256:(g_local + 1) * 256, :].rearrange("(t p) d -> p t d", p=128), in_=xs[:])],
                  osem[sl], reads=[xr])

        P.finish("sp")
        P.emit_all()
    return nc


def _t5_buckets(dist):
    n = np.maximum(dist, 0)
    max_exact = 16
    large = max_exact + (np.log(np.maximum(n, 1) / max_exact) / np.log(128 / max_exact) * (32 - max_exact)).astype(np.int32)
    large = np.minimum(large, 31)
    return np.where(n < max_exact, n, large).astype(np.int32)


def _col(v, nchunk):
    return np.ascontiguousarray(np.asarray(v, np.float32).reshape(nchunk, 128).T)


def _bc(v):
    return np.ascontiguousarray(np.broadcast_to(np.asarray(v, np.float32)[None, :], (128, len(v))))


_NC_CACHE = {}
_DBG = False
_LAST = None


def kernel(x, c, ada_w, ada_b, norm1, w_in, conv_w, conv_b, dt_bias, A_log, D_skip, sinks,
           attn_out_norm, ssm_out_norm, w_o, norm2, w_gate_up, w_down, rel_bias, final_norm):
    x = np.asarray(x, np.float32)
    B, S_, D = x.shape
    ntok = S_ // 2
    f = lambda a: np.asarray(a, np.float32)
    ada_b0 = f(ada_b)[0]
    dist = np.arange(128)[:, None] + 128 - np.arange(256)[None, :]
    valid = (dist >= 0) & (dist < 128)
    gathered = f(rel_bias)[_t5_buckets(dist)]
    biasf = np.where(valid[:, :, None], gathered, np.float32(NEG)).astype(np.float32)
    biasf = np.ascontiguousarray(np.transpose(biasf, (0, 2, 1)))
    perm = []
    for cch in range(4):
        perm += list(range(cch * 64, cch * 64 + 64)) + list(range((cch + 4) * 64, (cch + 4) * 64 + 64))
    w_in0 = f(w_in)[0]
    w_in_p = np.ascontiguousarray(np.concatenate([w_in0[:, perm], w_in0[:, 512:]], axis=1))
    shared = {
        "ada_w": np.ascontiguousarray(f(ada_w)[0]),
        "adab_col": _col(ada_b0, 48),
        "adab_g1": _bc(ada_b0[2048:3072]),
        "adab_g2": _bc(ada_b0[5120:6144]),
        "norm1_col": _col(f(norm1)[0], 8),
        "norm2_col": _col(f(norm2)[0], 8),
        "mixnorm_col": _col(np.concatenate([f(attn_out_norm)[0], f(ssm_out_norm)[0]]), 8),
        "w_in": w_in_p,
        "convw_col": np.ascontiguousarray(np.transpose(f(conv_w)[0].reshape(4, 8, 128), (2, 1, 0))),
        "convb_col": _col(f(conv_b)[0], 8),
        "dtb_b": _bc(f(dt_bias)[0]),
        "alog_b": _bc(f(A_log)[0]),
        "dskip_b": _bc(f(D_skip)[0]),
        "sinks_b": _bc(f(sinks)[0]),
        "w_o": np.ascontiguousarray(f(w_o)[0]),
        "w_gu": np.ascontiguousarray(f(w_gate_up)[0]),
        "w_dn": np.ascontiguousarray(f(w_down)[0]),
        "biasf": biasf,
        "fnorm_b": _bc(f(final_norm)),
    }
    in_maps = []
    for core in range(8):
        b, half = core // 2, core % 2
        m = dict(shared)
        m["x_main"] = np.ascontiguousarray(x[b, half * ntok:(half + 1) * ntok])
        m["x_pre"] = np.ascontiguousarray(x[b, 0:ntok]) if half == 1 else np.zeros((ntok, D), np.float32)
        m["c_col"] = _col(f(c)[b], 8)
        m["flag"] = np.full((128, 1), float(half), np.float32)
        in_maps.append(m)
    if ntok not in _NC_CACHE:
        _NC_CACHE[ntok] = build_nc(ntok, dbg=_DBG)
    res = run_bass_kernel_spmd(_NC_CACHE[ntok], in_maps, core_ids=list(range(8)))
    if _DBG:
        global _LAST
        _LAST = res.results
    outp = np.empty((B, S_, D), np.float32)
    for core in range(8):
        b, half = core // 2, core % 2
        outp[b, half * ntok:(half + 1) * ntok] = res.results[core]["out"]
    return outp
```

```python
import contextlib
import numpy as np
import concourse.bass as bass
import concourse.mybir as mybir
from concourse.bass_utils import run_bass_kernel_spmd

F32 = mybir.dt.float32
BF16 = mybir.dt.bfloat16
AF = mybir.ActivationFunctionType
ALU = mybir.AluOpType
AX = mybir.AxisListType

ENGS = ("pe", "act", "dve", "pool", "sp")

D_MODEL = 1024
IN_WIDTH = 2312
D_FF = 2816
NEG = -30000.0
NSLOT = 4
NXS = 3
NDSLOT = 6
DMA_RATE = 250e3


class DmaSem:
    def __init__(self, handle):
        self.h = handle
        self.count = 0


class _FakeIns:
    def then_inc(self, *a, **k):
        return self


def _numel(ap):
    n = 1
    for d in ap.shape[1:]:
        n *= int(d)
    return n


class _FakeEng:
    def __init__(self, eng):
        self.eng = eng
        self.cost = 0.0
        self.bytes = 0

    def __getattr__(self, name):
        def call(*args, **kw):
            out = kw.get("out", args[0] if args else None)
            n = _numel(out) if out is not None and hasattr(out, "shape") else 1
            if name == "matmul":
                lhsT = kw.get("lhsT", args[1] if len(args) > 1 else None)
                mult = 4.0 if (lhsT is not None and lhsT.dtype == F32) else 1.0
                self.cost += mult * (n / 2400.0 + 0.035)
            elif name == "transpose":
                self.cost += n / 2400.0 + 0.035
            elif name == "dma_start":
                self.cost += 0.06
                self.bytes += n * int(out.shape[0]) * (2 if out.dtype == BF16 else 4)
            elif name == "wait_ge":
                pass
            elif self.eng == "act":
                self.cost += n / 1200.0 + 0.22
            elif self.eng == "dve":
                self.cost += n / 960.0 + 0.17
            else:
                self.cost += n / 500.0 + 0.3
            return _FakeIns()
        return call


class _Op:
    __slots__ = ("i", "eng", "fn", "fns", "dsem", "preds", "cost", "lat", "start", "done", "idx", "val", "succs", "npred", "lab", "crit", "tag", "prio")


class Prog:
    LOOKAHEAD = 700
    BIAS = 1.5
    QUANT = 0.1

    def __init__(self, nc, stack):
        self.nc = nc
        self.stack = stack
        self.esem = {e: stack.enter_context(nc.semaphore("es_" + e)) for e in ENGS if e != "sp"}
        self.ops = []
        self.lastw = {}
        self.readers = {}
        self.nsem = 0
        self.dma_sems = []
        self.last_on_dsem = {}
        self.q = {e: [] for e in ENGS}

    def dma_sem(self, name=None):
        self.nsem += 1
        s = DmaSem(self.stack.enter_context(self.nc.semaphore(name or ("ds%d" % self.nsem))))
        self.dma_sems.append(s)
        return s

    def _deps(self, reads, writes, eng=None):
        preds = {}

        def need(i, raw):
            if i is None:
                return
            preds[i] = preds.get(i, False) or raw
        for r in reads:
            need(self.lastw.get(r), True)
            if r.startswith("ps"):
                for i in self.readers.get(r, ()):
                    if self.ops[i].eng != eng:
                        need(i, True)
        for w in writes:
            need(self.lastw.get(w), False)
            for i in self.readers.get(w, ()):
                need(i, False)
        return preds

    def _record(self, i, reads, writes):
        for r in reads:
            self.readers.setdefault(r, []).append(i)
        for w in writes:
            self.lastw[w] = i
            self.readers[w] = []

    def _new(self, eng, reads, writes):
        o = _Op()
        o.i = len(self.ops)
        o.eng = eng
        o.preds = self._deps(reads, writes, eng)
        o.fn = None
        o.fns = None
        o.dsem = None
        o.lab = (tuple(reads), tuple(writes))
        o.tag = getattr(self, "tag", None)
        o.prio = getattr(self, "prio", 0)
        self.ops.append(o)
        self._record(o.i, reads, writes)
        return o

    def op(self, eng, fn, reads=(), writes=()):
        o = self._new(eng, reads, writes)
        o.fn = fn
        fk = _FakeEng(eng)
        fn(fk)
        o.cost = fk.cost
        o.lat = 0.0

    def dma(self, eng, fns, dsem, reads=(), writes=()):
        o = self._new(eng, reads, writes)
        o.fns = fns
        o.dsem = dsem
        prev = self.last_on_dsem.get(id(dsem))
        if prev is not None and prev not in o.preds:
            o.preds[prev] = False
        self.last_on_dsem[id(dsem)] = o.i
        fk = _FakeEng(eng)
        for f in fns:
            f(fk)
        o.cost = fk.cost
        o.lat = 2.2 + fk.bytes / DMA_RATE

    def finish(self, eng="sp"):
        ops = self.ops
        n = len(ops)
        for o in ops:
            o.succs = []
            o.npred = len(o.preds)
            o.start = None
        for o in ops:
            for p in o.preds:
                ops[p].succs.append(o.i)
        bl = [0.0] * n
        for o in reversed(ops):
            m = 0.0
            for sidx in o.succs:
                if bl[sidx] > m:
                    m = bl[sidx]
            bl[o.i] = m + o.cost + o.lat
        ready = [o.i for o in ops if o.npred == 0]
        eng_free = {e: 0.0 for e in ENGS}
        order = {e: [] for e in ENGS}
        oldest = 0
        scheduled = 0
        while scheduled < n:
            while oldest < n and ops[oldest].start is not None:
                oldest += 1
            best = None
            for i in ready:
                if i > oldest + self.LOOKAHEAD:
                    continue
                o = ops[i]
                t = eng_free[o.eng]
                o.crit = -1
                for p in o.preds:
                    po = ops[p]
                    d = po.done + (0.0 if po.eng == o.eng and po.dsem is None else 0.08)
                    if d > t:
                        t = d
                        o.crit = p
                key = (int(t / self.QUANT), -bl[i], i) if self.QUANT > 0 else (t + self.BIAS * o.prio, i)
                if best is None or key < best[0]:
                    best = (key, i, t)
            _, i, t = best
            o = ops[i]
            o.start = t
            eng_free[o.eng] = t + o.cost
            o.done = t + o.cost + o.lat
            if o.crit == -1 and order[o.eng]:
                o.crit = -2 - order[o.eng][-1]
            order[o.eng].append(i)
            ready.remove(i)
            scheduled += 1
            for sidx in o.succs:
                so = ops[sidx]
                so.npred -= 1
                if so.npred == 0:
                    ready.append(sidx)
        self.makespan = max(o.done for o in ops)
        cnt = {e: 0 for e in ENGS}
        for e in ENGS:
            for i in order[e]:
                o = ops[i]
                if o.dsem is None:
                    cnt[e] += 1
                    o.val = (self.esem[e], cnt[e])
                else:
                    o.dsem.count += 16 * len(o.fns)
                    o.val = (o.dsem.h, o.dsem.count)
        for e in ENGS:
            waited = {}
            for i in order[e]:
                o = ops[i]
                waits = {}
                for p, raw in o.preds.items():
                    po = ops[p]
                    if po.dsem is None and po.eng == e:
                        if e == "pe" or not raw:
                            continue
                    sem, val = po.val
                    k = id(sem)
                    if waited.get(k, 0) >= val:
                        continue
                    if k not in waits or waits[k][1] < val:
                        waits[k] = (sem, val)
                for k, (sem, val) in waits.items():
                    waited[k] = val
                wl = list(waits.values())
                if o.dsem is None:
                    def emit(en, wl=wl, fn=o.fn, sem=o.val[0]):
                        for (s, v) in wl:
                            en.wait_ge(s, v)
                        fn(en).then_inc(sem, 1)
                else:
                    def emit(en, wl=wl, fns=o.fns, h=o.dsem.h):
                        for (s, v) in wl:
                            en.wait_ge(s, v)
                        for f in fns:
                            f(en).then_inc(h, 16)
                self.q[e].append(emit)
        fw = []
        for e in ENGS:
            if e != "sp" and cnt[e] > 0:
                fw.append((self.esem[e], cnt[e]))
        for s in self.dma_sems:
            if s.count > 0:
                fw.append((s.h, s.count))

        def emit_fin(en, fw=fw):
            for (s, v) in fw:
                en.wait_ge(s, v)
        self.q[eng].append(emit_fin)

    def emit_all(self):
        with self.nc.Block() as block:
            @block.tensor
            def _(e):
                for f in self.q["pe"]:
                    f(e)

            @block.scalar
            def _(e):
                for f in self.q["act"]:
                    f(e)

            @block.vector
            def _(e):
                for f in self.q["dve"]:
                    f(e)

            @block.gpsimd
            def _(e):
                for f in self.q["pool"]:
                    f(e)

            @block.sync
            def _(e):
                for f in self.q["sp"]:
                    f(e)


def build_nc(ntok, dbg=False):
    NG = ntok // 256
    DBG = {}
    nc = bass.Bass("TRN2", target_bir_lowering=False)

    def din(name, shape, dt=F32):
        return nc.dram_tensor(name, list(shape), dt, kind="ExternalInput").ap()

    x_main = din("x_main", [ntok, 1024])
    x_pre = din("x_pre", [ntok, 1024])
    c_col = din("c_col", [128, 8])
    ada_w = din("ada_w", [1024, 6144])
    adab_col = din("adab_col", [128, 48])
    adab_g1 = din("adab_g1", [128, 1024])
    adab_g2 = din("adab_g2", [128, 1024])
    norm1_col = din("norm1_col", [128, 8])
    norm2_col = din("norm2_col", [128, 8])
    mixnorm_col = din("mixnorm_col", [128, 8])
    w_in = din("w_in", [1024, IN_WIDTH])
    convw_col = din("convw_col", [128, 8, 4])
    convb_col = din("convb_col", [128, 8])
    dtb_b = din("dtb_b", [128, 8])
    alog_b = din("alog_b", [128, 8])
    dskip_b = din("dskip_b", [128, 8])
    sinks_b = din("sinks_b", [128, 8])
    w_o = din("w_o", [1024, 1024])
    w_gu = din("w_gu", [1024, 2 * D_FF])
    w_dn = din("w_dn", [D_FF, 1024])
    biasf_in = din("biasf", [128, 8, 256])
    fnorm_in = din("fnorm_b", [128, 1024])
    flag_in = din("flag", [128, 1])
    out = nc.dram_tensor("out", [ntok, 1024], F32, kind="ExternalOutput").ap()
    wgu_s = nc.dram_tensor("wgu_s", [22, 128, 2048], BF16).ap()
    wdn_s = nc.dram_tensor("wdn_s", [2, 22, 128, 512], BF16).ap()

    with contextlib.ExitStack() as st:
        P = Prog(nc, st)

        def T(name, shape, dt=F32):
            return st.enter_context(nc.sbuf_tensor(name, list(shape), dt))

        def PSUM(name, shape, dt=F32):
            return st.enter_context(nc.psum_tensor(name, list(shape), dt))

        w_in_bf = T("w_in_bf", [128, 8, IN_WIDTH], BF16)
        w_o_bf = T("w_o_bf", [128, 8, 1024], BF16)
        wgu = [T("wgu%d" % i, [128, 8, 256], BF16) for i in range(NSLOT)]
        wdn = [T("wdn%d" % i, [128, 512], BF16) for i in range(NDSLOT)]
        diagW = T("diagW", [128, 8, 4, 128], BF16)
        actT = T("actT", [128, 22, 256], BF16)
        hT2 = T("hT2", [128, 8, 256], BF16)
        biasf = T("biasf_sb", [128, 8, 256])
        fnorm = T("fnorm_sb", [128, 1024])
        Dg = T("Dg", [128, 8, 128], BF16)
        S = T("S", [128, 512])
        Sbf = T("Sbf", [128, 512], BF16)
        ident = T("ident", [128, 128], BF16)
        identf = T("identf", [128, 128])
        tri = T("tri", [128, 128])
        ones = T("ones", [128, 128])
        maskTf = T("maskTf", [128, 128])
        maskT = T("maskT", [128, 128], BF16)
        g1 = T("g1", [128, 8]); sh1 = T("sh1", [128, 8]); g2 = T("g2", [128, 8]); sh2 = T("sh2", [128, 8])
        modc = T("modc", [128, 48])
        adabc = T("adabc", [128, 48])
        n1c = T("n1c", [128, 8]); n2c = T("n2c", [128, 8]); mnc = T("mnc", [128, 8])
        ccol = T("ccol", [128, 8]); cth = T("cth", [128, 8]); condf = T("condf", [128, 8]); condb = T("condb", [128, 8], BF16)
        convw = T("convw", [128, 8, 4]); convb = T("convb", [128, 8])
        dtb = T("dtb", [128, 8]); A_b = T("A_b", [128, 8]); dsk = T("dsk", [128, 8])
        sink = T("sink", [128, 8]); nsink = T("nsink", [128, 8])
        flag = T("flag_sb", [128, 1]); maskb = T("maskb", [128, 1])
        eps1 = T("eps1", [128, 8]); eps4 = T("eps4", [128, 8]); mhalf = T("mhalf", [128, 8])
        kT = T("kT", [128, 384], BF16)
        vx = T("vx", [128, 3, 128], BF16)
        raw = T("raw", [128, 8, 259], BF16)
        xg = [T("xg%d" % i, [128, 2, 1024]) for i in range(NXS)]
        hT = T("hT", [128, 8, 256], BF16)
        qT = T("qT", [128, 4, 256], BF16)
        cvo = T("cvo", [128, 8, 256], BF16)
        gz = T("gz", [128, 2, 512], BF16)
        dtu = T("dtu", [128, 16]); dtv = T("dtv", [128, 16])
        sp_t = [T("sp_t%d" % i, [128, 16]) for i in range(8)]
        xn = T("xn", [128, 1024], BF16)
        junk = T("junk", [128, 1024], BF16)
        mixb = T("mixb", [128, 1024], BF16)
        NB = 1
        sc_l = [T("sc%d" % i, [128, 4, 256]) for i in range(NB)]
        pb_l = [T("pb%d" % i, [128, 4, 256], BF16) for i in range(NB)]
        pTs_l = [T("pTs%d" % i, [128, 8, 128], BF16) for i in range(NB)]
        LT_l = [T("LT%d" % i, [128, 8, 128], BF16) for i in range(NB)]
        WT_l = [T("WT%d" % i, [128, 8, 128], BF16) for i in range(NB)]
        xdt_l = [T("xdt%d" % i, [128, 512], BF16) for i in range(NB)]
        xdd_l = [T("xdd%d" % i, [128, 512], BF16) for i in range(NB)]
        xsb_l = [T("xsb%d" % i, [128, 512], BF16) for i in range(NB)]
        Btok_l = [T("Btok%d" % i, [128, 256], BF16) for i in range(NB)]
        ya = T("ya", [128, 512]); yt = T("yt", [128, 512]); yg = T("yg", [128, 512])
        acc = [T("acc%d" % i, [128, 2, 256]) for i in range(2)]
        cth2 = [T("cth2_%d" % i, [128, 2, 256]) for i in range(2)]
        thg = [T("thg%d" % i, [128, 256]) for i in range(2)]
        t2 = [T("t2_%d" % i, [128, 256], BF16) for i in range(2)]
        ub = [T("ub_%d" % i, [128, 256], BF16) for i in range(2)]
        ms = T("ms", [128, 8]); mse = T("mse", [128, 8]); rstd = T("rstd", [128, 8])
        a_t = T("a_t", [128, 8]); e_in = T("e_in", [128, 24]); e24 = T("e24", [128, 24])
        w2 = T("w2", [128, 8]); nAcs = T("nAcs", [128, 8])
        rmax = T("rmax", [128, 4]); negm = T("negm", [128, 4]); rsum = T("rsum", [128, 4])
        stmp = T("stmp", [128, 4]); es = T("es", [128, 4]); den = T("den", [128, 4]); rden = T("rden", [128, 4])
        psT = [PSUM("psT%d" % i, [128, 1024], BF16) for i in range(2)]
        psM = [PSUM("psM%d" % i, [128, 512], F32) for i in range(6)]
        rr = {"m": 0, "t": 0}

        def nextM():
            i = 2 + rr["m"] % 4
            rr["m"] += 1
            return psM[i], "psM%d" % i

        def nextT():
            i = rr["t"] % 2
            rr["t"] += 1
            return psT[i], "psT%d" % i

        dsem_dbg = P.dma_sem("dbg") if dbg else None

        def dump(name, ap, res, cond=True):
            if not (dbg and cond) or name in DBG:
                return
            dt_ = ap.dtype
            d = nc.dram_tensor("dbg_" + name, list(ap.shape), dt_, kind="ExternalOutput").ap()
            DBG[name] = d
            P.dma("sp", [lambda e, d=d, ap=ap: e.dma_start(out=d, in_=ap)], dsem_dbg, reads=res)

        gate1_b = xg[1][:, 0, :]
        gate2h_b = xg[1][:, 1, :]
        s_small = P.dma_sem("s_small")
        small = [(ccol, c_col, "ccol"), (adabc, adab_col, "adabc"), (n1c, norm1_col, "n1c"), (n2c, norm2_col, "n2c"),
                 (mnc, mixnorm_col, "mnc"), (convw, convw_col, "convw"), (convb, convb_col, "convb"),
                 (dtb, dtb_b, "dtb"), (A_b, alog_b, "A_b"), (dsk, dskip_b, "dsk"), (sink, sinks_b, "sink"),
                 (flag, flag_in, "flag"), (biasf, biasf_in, "biasf"), (fnorm, fnorm_in, "fnorm")]
        P.dma("sp", [(lambda e, d=d, s=s: e.dma_start(out=d[:], in_=s)) for d, s, _ in small], s_small,
              writes=[r for _, _, r in small])
        s_g = P.dma_sem("s_g")
        P.dma("sp", [lambda e: e.dma_start(out=xg[0][:, 0, :], in_=adab_g1),
                     lambda e: e.dma_start(out=xg[0][:, 1, :], in_=adab_g2)], s_g, writes=["x0"])

        P.op("pool", lambda e: e.memset(ones[:], 1.0), writes=["ones"])
        P.op("pool", lambda e: e.affine_select(out=tri[:], in_=ones[:], pattern=[[1, 128]], compare_op=ALU.is_ge,
                                               fill=0.0, base=0, channel_multiplier=-1), reads=["ones"], writes=["tri"])
        P.op("pool", lambda e: e.affine_select(out=identf[:], in_=tri[:], pattern=[[-1, 128]], compare_op=ALU.is_ge,
                                               fill=0.0, base=0, channel_multiplier=1), reads=["tri"], writes=["identf"])
        P.op("pool", lambda e: e.memset(maskTf[:], NEG), writes=["maskTf"])
        P.op("pool", lambda e: e.affine_select(out=maskTf[:], in_=maskTf[:], pattern=[[-1, 128]], compare_op=ALU.is_gt,
                                               fill=0.0, base=0, channel_multiplier=1), reads=["maskTf"], writes=["maskTf"])
        P.op("dve", lambda e: e.tensor_copy(out=ident[:], in_=identf[:]), reads=["identf"], writes=["ident"])
        P.op("dve", lambda e: e.tensor_copy(out=maskT[:], in_=maskTf[:]), reads=["maskTf"], writes=["maskT"])
        P.op("pool", lambda e: e.memset(eps1[:], 1e-6), writes=["eps1"])
        P.op("pool", lambda e: e.memset(eps4[:], 4e-6), writes=["eps4"])
        P.op("pool", lambda e: e.memset(mhalf[:], -0.5), writes=["mhalf"])
        P.op("pool", lambda e: e.memset(S[:], 0.0), writes=["S"])
        P.op("pool", lambda e: e.memset(Sbf[:], 0.0), writes=["Sbf"])
        P.op("pool", lambda e: e.memset(raw[:], 0.0), writes=["raw"])
        P.op("pool", lambda e: e.memset(kT[:], 0.0), writes=["kT"])
        P.op("pool", lambda e: e.memset(vx[:], 0.0), writes=["vx"])

        P.op("act", lambda e: e.activation(out=cth[:], in_=ccol[:], func=AF.Tanh, scale=0.5), reads=["ccol"], writes=["cth"])
        P.op("dve", lambda e: e.scalar_tensor_tensor(out=condf[:], in0=cth[:], scalar=1.0, in1=ccol[:], op0=ALU.add, op1=ALU.mult),
             reads=["cth", "ccol"], writes=["condf"])
        P.op("dve", lambda e: e.tensor_scalar(out=condb[:], in0=condf[:], scalar1=0.5, scalar2=None, op0=ALU.mult),
             reads=["condf"], writes=["condb"])

        s_ada = [P.dma_sem("s_ada%d" % i) for i in range(NSLOT)]
        modps, modps_r = psM[5], "psM5"
        for pc in range(24):
            sl = pc % NSLOT
            P.dma("pool", [lambda e, pc=pc, sl=sl: e.dma_start(
                out=wgu[sl][:], in_=ada_w[:, pc * 256:(pc + 1) * 256].rearrange("(k p) n -> p k n", p=128))],
                s_ada[sl], writes=["wgu%d" % sl])
            vec = pc // 4
            if vec in (2, 5):
                pi = (pc % 4) + (0 if vec == 2 else 4)
                pm, pr = psM[pi % 4], "psM%d" % (pi % 4)

                def f(e, sl=sl, pm=pm):
                    for k in range(8):
                        ins = e.matmul(pm[:, 0:256], lhsT=condb[:, k:k + 1].to_broadcast([128, 128]), rhs=wgu[sl][:, k, :],
                                       start=(k == 0), stop=(k == 7))
                    return ins
                P.op("pe", f, reads=["wgu%d" % sl, "condb"], writes=[pr])
                qd = pc % 4
                dst = gate1_b if vec == 2 else gate2h_b
                srcb = xg[0][:, 0, :] if vec == 2 else xg[0][:, 1, :]
                P.op("dve", lambda e, pm=pm, qd=qd, dst=dst, srcb=srcb: e.tensor_tensor(
                    out=dst[:, qd * 256:(qd + 1) * 256], in0=pm[:, 0:256], in1=srcb[:, qd * 256:(qd + 1) * 256], op=ALU.add),
                    reads=[pr, "x0"], writes=[("gate1_b%d" if vec == 2 else "gate2h_b%d") % qd])
            else:
                def f(e, sl=sl, pc=pc):
                    for jj in range(2):
                        j = pc * 2 + jj
                        for k in range(8):
                            ins = e.matmul(modps[:, j:j + 1], lhsT=wgu[sl][:, k, jj * 128:(jj + 1) * 128], rhs=condb[:, k:k + 1],
                                           start=(k == 0), stop=(k == 7))
                    return ins
                P.op("pe", f, reads=["wgu%d" % sl, "condb"], writes=[modps_r])
        P.op("dve", lambda e: e.tensor_scalar(out=gate2h_b, in0=gate2h_b, scalar1=0.5, scalar2=None, op0=ALU.mult),
             reads=["gate2h_b%d" % i for i in range(4)], writes=["gate2h_b"])
        def fmodc(e):
            e.tensor_tensor(out=modc[:, 0:16], in0=modps[:, 0:16], in1=adabc[:, 0:16], op=ALU.add)
            return e.tensor_tensor(out=modc[:, 24:40], in0=modps[:, 24:40], in1=adabc[:, 24:40], op=ALU.add)
        P.op("dve", fmodc, reads=[modps_r, "adabc"], writes=["modc"])
        P.op("dve", lambda e: e.scalar_tensor_tensor(out=g1[:], in0=modc[:, 8:16], scalar=1.0, in1=n1c[:], op0=ALU.add, op1=ALU.mult),
             reads=["modc", "n1c"], writes=["g1"])
        P.op("dve", lambda e: e.tensor_copy(out=sh1[:], in_=modc[:, 0:8]), reads=["modc"], writes=["sh1"])
        P.op("dve", lambda e: e.scalar_tensor_tensor(out=g2[:], in0=modc[:, 32:40], scalar=1.0, in1=n2c[:], op0=ALU.add, op1=ALU.mult),
             reads=["modc", "n2c"], writes=["g2"])
        P.op("dve", lambda e: e.tensor_copy(out=sh2[:], in_=modc[:, 24:32]), reads=["modc"], writes=["sh2"])

        P.op("dve", lambda e: e.tensor_scalar(out=convw[:], in0=convw[:], scalar1=0.5, scalar2=None, op0=ALU.mult), reads=["convw"], writes=["convw"])
        P.op("dve", lambda e: e.tensor_scalar(out=convb[:], in0=convb[:], scalar1=0.5, scalar2=None, op0=ALU.mult), reads=["convb"], writes=["convb"])
        def fdw(e):
            for c in range(8):
                for k in range(4):
                    ins = e.tensor_scalar(out=diagW[:, c, k, :], in0=identf[:], scalar1=convw[:, c, k:k + 1], scalar2=None, op0=ALU.mult)
            return ins
        P.op("dve", fdw, reads=["identf", "convw"], writes=["diagW"])
        P.op("act", lambda e: e.activation(out=A_b[:], in_=A_b[:], func=AF.Exp), reads=["A_b"], writes=["A_b"])
        P.op("dve", lambda e: e.tensor_scalar(out=A_b[:], in0=A_b[:], scalar1=-1.0, scalar2=None, op0=ALU.mult), reads=["A_b"], writes=["A_b"])
        P.op("dve", lambda e: e.tensor_scalar(out=nsink[:], in0=sink[:], scalar1=-1.0, scalar2=None, op0=ALU.mult), reads=["sink"], writes=["nsink"])
        P.op("dve", lambda e: e.tensor_scalar(out=maskb[:], in0=flag[:], scalar1=-NEG, scalar2=NEG, op0=ALU.mult, op1=ALU.add),
             reads=["flag"], writes=["maskb"])

        def fdg(e):
            for h in range(8):
                ins = e.tensor_scalar(out=Dg[:, h, :], in0=identf[:], scalar1=dsk[:, h:h + 1], scalar2=None, op0=ALU.mult)
            return ins
        P.op("dve", fdg, reads=["identf", "dsk"], writes=["Dg"])

        s_win = P.dma_sem("s_win")
        P.dma("pool", [(lambda e, kk=kk: e.dma_start(out=w_in_bf[:, 2 * kk:2 * kk + 2, :],
                                                     in_=w_in[kk * 256:(kk + 1) * 256, :].rearrange("(k p) n -> p k n", p=128)))
                       for kk in range(4)], s_win, writes=["w_in"])
        s_wo = P.dma_sem("s_wo")
        P.dma("pool", [(lambda e, kk=kk: e.dma_start(out=w_o_bf[:, 4 * kk:4 * kk + 4, :],
                                                     in_=w_o[kk * 512:(kk + 1) * 512, :].rearrange("(k p) n -> p k n", p=128)))
                       for kk in range(2)], s_wo, writes=["w_o_raw"])

        def fwo(e):
            for k in range(8):
                e.tensor_scalar(out=w_o_bf[:, k, :], in0=w_o_bf[:, k, :], scalar1=mnc[:, k:k + 1], scalar2=None, op0=ALU.mult)
            for k in range(8):
                ins = e.tensor_tensor(out=w_o_bf[:, k, :], in0=w_o_bf[:, k, :], in1=gate1_b, op=ALU.mult)
            return ins
        P.op("dve", fwo, reads=["w_o_raw", "mnc"] + ["gate1_b%d" % i for i in range(4)], writes=["w_o", "x1"])
        s_scr = P.dma_sem("s_scr")
        P.dma("pool", [(lambda e, j=j, u=u: e.dma_start(out=wgu_s[j].rearrange("p (k n) -> p k n", k=8)[:, :, u * 128:(u + 1) * 128],
                                                        in_=w_gu[:, u * D_FF + j * 128:u * D_FF + (j + 1) * 128].rearrange("(k p) n -> p k n", p=128)))
                       for j in range(22) for u in range(2)], s_scr, writes=["scr_gu"])
        s_dnl = P.dma_sem("s_dnl")
        s_dns = P.dma_sem("s_dns")
        for j in range(22):
            P.dma("pool", [lambda e, j=j: e.dma_start(out=xn[:], in_=w_dn[j * 128:(j + 1) * 128, :])], s_dnl, writes=["xn"])
            P.op("dve", lambda e: e.tensor_tensor(out=xn[:], in0=xn[:], in1=gate2h_b, op=ALU.mult),
                 reads=["xn", "gate2h_b"], writes=["xn", "x1"])
            P.dma("sp", [lambda e, j=j, nh=nh: e.dma_start(out=wdn_s[nh, j], in_=xn[:, nh * 512:(nh + 1) * 512]) for nh in range(2)],
                  s_dns, reads=["xn"], writes=["scr_dn%d" % j])

        xsem = [P.dma_sem("xs%d" % i) for i in range(NXS)]
        osem = [P.dma_sem("os%d" % i) for i in range(NXS)]
        wsem = [P.dma_sem("ws%d" % i) for i in range(NSLOT)]

        def load_x(gi):
            pre = gi < NG
            src = x_pre if pre else x_main
            g = gi if pre else gi - NG
            sl = gi % NXS
            P.dma("sp", [lambda e, src=src, g=g, sl=sl: e.dma_start(
                out=xg[sl][:], in_=src[g * 256:(g + 1) * 256, :].rearrange("(t p) d -> p t d", p=128))],
                xsem[sl], writes=["x%d" % sl])

        dsem_w = [P.dma_sem("wd%d" % i) for i in range(NDSLOT)]

        def load_w(j):
            sl = j % NSLOT
            P.dma("sp", [lambda e, j=j, sl=sl: e.dma_start(out=wgu[sl][:], in_=wgu_s[j].rearrange("p (k n) -> p k n", k=8))],
                  wsem[sl], reads=["scr_gu"], writes=["wgu%d" % sl])

        def load_wd(idx):
            nh, j = idx // 22, idx % 22
            sl = idx % NDSLOT
            P.dma("sp", [lambda e, j=j, nh=nh, sl=sl: e.dma_start(out=wdn[sl][:], in_=wdn_s[nh, j])],
                  dsem_w[sl], reads=["scr_dn%d" % j], writes=["wdn%d" % sl])

        def rms_stats(src_ap, src_res, scale, col, eps_t):
            P.op("act", lambda e: e.activation(out=junk[:, 0:src_ap.shape[-1]], in_=src_ap, func=AF.Square, scale=scale,
                                               accum_out=ms[:, col:col + 1]),
                 reads=list(src_res) if isinstance(src_res, (list, tuple)) else [src_res], writes=["ms%d" % col])
            P.op("pool", lambda e: e.tensor_tensor(out=mse[:, col:col + 1], in0=ms[:, col:col + 1], in1=eps_t[:, 0:1], op=ALU.add),
                 reads=["ms%d" % col], writes=["mse%d" % col])
            P.op("pool", lambda e: e.tensor_tensor(out=rstd[:, col:col + 1], in0=mse[:, col:col + 1], in1=mhalf[:, 0:1], op=ALU.pow),
                 reads=["mse%d" % col], writes=["rstd%d" % col])

        def norm_transpose(xs_ap, xres, gvec, svec, dstT, dres, t):
            rms_stats(xs_ap, xres, 1.0 / 32.0, 0, eps1)
            P.op("dve", lambda e: e.tensor_scalar(out=xn[:], in0=xs_ap, scalar1=rstd[:, 0:1], scalar2=None, op0=ALU.mult),
                 reads=[xres, "rstd0"], writes=["xn"])
            pt, ptr = nextT()

            def ftr(e):
                for k in range(8):
                    ins = e.transpose(out=pt[:, k * 128:(k + 1) * 128], in_=xn[:, k * 128:(k + 1) * 128], identity=ident[:])
                return ins
            P.op("pe", ftr, reads=["xn", "ident"], writes=[ptr])

            def fev(e):
                for k in range(8):
                    ins = e.activation(out=dstT[:, k, t * 128:(t + 1) * 128], in_=pt[:, k * 128:(k + 1) * 128], func=AF.Identity,
                                       scale=gvec[:, k:k + 1], bias=svec[:, k:k + 1])
                return ins
            P.op("act", fev, reads=[ptr, "g1", "sh1", "g2", "sh2"], writes=[dres])

        import os
        STAGE = int(os.environ.get("KSTAGE", "9"))
        if STAGE >= 2:
            load_x(0)
            load_x(1)
        for gi in range(2 * NG):
            pre = gi < NG
            P.tag = gi
            if STAGE < 2 or (STAGE == 2 and not pre):
                break
            g_local = gi if pre else gi - NG
            sl = gi % NXS
            xr = "x%d" % sl
            xs = xg[sl]
            if gi + 2 < 2 * NG:
                load_x(gi + 2)
            if gi == NG:
                P.op("dve", lambda e: e.tensor_scalar(out=S[:], in0=S[:], scalar1=flag[:, 0:1], scalar2=None, op0=ALU.mult),
                     reads=["S", "flag"], writes=["S"])
                P.op("act", lambda e: e.activation(out=Sbf[:], in_=S[:], func=AF.Copy), reads=["S"], writes=["Sbf"])
                P.op("dve", lambda e: e.tensor_scalar(out=raw[:, :, 0:3], in0=raw[:, :, 0:3], scalar1=flag[:, 0:1], scalar2=None, op0=ALU.mult),
                     reads=["raw", "flag"], writes=["raw"])
            if not pre:
                for j in range(NSLOT):
                    load_w(j)
                for j in range(NDSLOT):
                    load_wd(j)

            for t in range(2):
                norm_transpose(xs[:, t, :], xr, g1, sh1, hT, "hT", t)

            dump("hT", hT[:], ["hT"], (not pre) and g_local == 0)
            dump("g1", g1[:], ["g1"], (not pre) and g_local == 0); dump("sh1", sh1[:], ["sh1"], (not pre) and g_local == 0)
            dump("S0", S[:], ["S"], (not pre) and g_local == 0)
            def fm_chunks(cols_list, pm, pr):
                def f(e):
                    for ci, c0 in enumerate(cols_list):
                        for k in range(8):
                            ins = e.matmul(pm[:, ci * 256:(ci + 1) * 256], lhsT=w_in_bf[:, k, c0:c0 + 128], rhs=hT[:, k, :],
                                           start=(k == 0), stop=(k == 7))
                    return ins
                P.op("pe", f, reads=["w_in", "hT"], writes=[pr])

            if not pre:
                for i in range(2):
                    pm, pr = nextM()
                    fm_chunks([(2 * i) * 128, (2 * i + 1) * 128], pm, pr)
                    P.op("act", lambda e, pm=pm, i=i: e.activation(out=qT[:, 2 * i:2 * i + 2, :],
                                                                    in_=pm[:, 0:512].rearrange("p (c n) -> p c n", c=2), func=AF.Copy),
                         reads=[pr], writes=["qT"])
            pm, pr = nextM()
            fm_chunks([512], pm, pr)
            P.op("act", lambda e, pm=pm: e.activation(out=kT[:, 128:384], in_=pm[:, 0:256], func=AF.Copy), reads=[pr], writes=["kT"])
            for i in range(4):
                pm, pr = nextM()
                fm_chunks([1280 + (2 * i) * 128, 1280 + (2 * i + 1) * 128], pm, pr)
                P.op("act", lambda e, pm=pm, i=i: e.activation(out=raw[:, 2 * i:2 * i + 2, 3:259],
                                                                in_=pm[:, 0:512].rearrange("p (c n) -> p c n", c=2), func=AF.Copy),
                     reads=[pr], writes=["raw"])
            for i in range(4):
                b = i % 2
                pm, pr = nextM()

                def fcv(e, pm=pm, i=i):
                    for ci in range(2):
                        c = 2 * i + ci
                        for k in range(4):
                            ins = e.matmul(pm[:, ci * 256:(ci + 1) * 256], lhsT=diagW[:, c, k, :], rhs=raw[:, c, k:k + 256],
                                           start=(k == 0), stop=(k == 3))
                    return ins
                P.op("pe", fcv, reads=["diagW", "raw"], writes=[pr])

                def fcu(e, pm=pm, i=i, b=b):
                    for ci in range(2):
                        ins = e.activation(out=acc[b][:, ci, :], in_=pm[:, ci * 256:(ci + 1) * 256], func=AF.Identity,
                                           bias=convb[:, 2 * i + ci:2 * i + ci + 1], scale=1.0)
                    return ins
                P.op("act", fcu, reads=[pr, "convb"], writes=["acc%d" % b])
                P.op("act", lambda e, b=b: e.activation(out=cth2[b][:], in_=acc[b][:], func=AF.Tanh), reads=["acc%d" % b], writes=["cth2_%d" % b])
                P.op("dve", lambda e, i=i, b=b: e.scalar_tensor_tensor(out=cvo[:, 2 * i:2 * i + 2, :], in0=cth2[b][:], scalar=1.0, in1=acc[b][:],
                                                                       op0=ALU.add, op1=ALU.mult),
                     reads=["cth2_%d" % b, "acc%d" % b], writes=["cvo"])
            P.op("dve", lambda e: e.tensor_copy(out=raw[:, :, 0:3], in_=raw[:, :, 256:259]), reads=["raw"], writes=["raw"])

            dump("qT", qT[:], ["qT"], (not pre) and g_local == 0); dump("kT", kT[:], ["kT"], (not pre) and g_local == 0); dump("cvo", cvo[:], ["cvo"], (not pre) and g_local == 0)
            for t in range(2):
                tsl = slice(t * 128, (t + 1) * 128)
                pa, par = nextM()
                if pre:
                    def f(e, pa=pa, tsl=tsl):
                        for k in range(8):
                            e.matmul(pa[:, 0:128], lhsT=hT[:, k, tsl], rhs=w_in_bf[:, k, 640:768], start=(k == 0), stop=(k == 7))
                        for k in range(8):
                            ins = e.matmul(pa[:, 128:136], lhsT=hT[:, k, tsl], rhs=w_in_bf[:, k, 2304:2312], start=(k == 0), stop=(k == 7))
                        return ins
                    P.op("pe", f, reads=["hT", "w_in"], writes=[par])
                    P.op("act", lambda e, pa=pa, t=t: e.activation(out=vx[:, 1 + t, :], in_=pa[:, 0:128], func=AF.Copy), reads=[par], writes=["vx"])
                    P.op("dve", lambda e, pa=pa, t=t: e.tensor_tensor(out=dtu[:, t * 8:(t + 1) * 8], in0=pa[:, 128:136], in1=dtb[:], op=ALU.add),
                         reads=[par, "dtb"], writes=["dtu"])
                else:
                    pb2, pbr = nextM()

                    def f(e, pa=pa, pb2=pb2, tsl=tsl):
                        for k in range(8):
                            e.matmul(pa[:, 0:512], lhsT=hT[:, k, tsl], rhs=w_in_bf[:, k, 640:1152], start=(k == 0), stop=(k == 7))
                        for k in range(8):
                            e.matmul(pb2[:, 0:128], lhsT=hT[:, k, tsl], rhs=w_in_bf[:, k, 1152:1280], start=(k == 0), stop=(k == 7))
                        for k in range(8):
                            ins = e.matmul(pb2[:, 128:136], lhsT=hT[:, k, tsl], rhs=w_in_bf[:, k, 2304:2312], start=(k == 0), stop=(k == 7))
                        return ins
                    P.op("pe", f, reads=["hT", "w_in"], writes=[par, pbr])
                    P.op("act", lambda e, pa=pa, t=t: e.activation(out=vx[:, 1 + t, :], in_=pa[:, 0:128], func=AF.Copy), reads=[par], writes=["vx"])

                    def fth(e, pa=pa, pb2=pb2):
                        e.activation(out=yt[:, 0:384], in_=pa[:, 128:512], func=AF.Tanh, scale=0.5)
                        return e.activation(out=yt[:, 384:512], in_=pb2[:, 0:128], func=AF.Tanh, scale=0.5)
                    P.op("act", fth, reads=[par, pbr], writes=["yt"])

                    def fgz(e, pa=pa, pb2=pb2, t=t):
                        e.scalar_tensor_tensor(out=gz[:, t, 0:384], in0=yt[:, 0:384], scalar=1.0, in1=pa[:, 128:512], op0=ALU.add, op1=ALU.mult)
                        return e.scalar_tensor_tensor(out=gz[:, t, 384:512], in0=yt[:, 384:512], scalar=1.0, in1=pb2[:, 0:128],
                                                      op0=ALU.add, op1=ALU.mult)
                    P.op("dve", fgz, reads=["yt", par, pbr], writes=["gz"])
                    P.op("dve", lambda e, pb2=pb2, t=t: e.tensor_tensor(out=dtu[:, t * 8:(t + 1) * 8], in0=pb2[:, 128:136], in1=dtb[:], op=ALU.add),
                         reads=[pbr, "dtb"], writes=["dtu"])

            au, tt, dd, ww, w2s, rr_, lnp, relu = sp_t
            P.op("act", lambda e: e.activation(out=au[:], in_=dtu[:], func=AF.Abs), reads=["dtu"], writes=["sp_au"])
            P.op("act", lambda e: e.activation(out=tt[:], in_=au[:], func=AF.Exp, scale=-1.0), reads=["sp_au"], writes=["sp_tt"])
            P.op("dve", lambda e: e.tensor_scalar(out=dd[:], in0=tt[:], scalar1=2.0, scalar2=None, op0=ALU.add), reads=["sp_tt"], writes=["sp_dd"])
            P.op("dve", lambda e: e.reciprocal(out=dd[:], in_=dd[:]), reads=["sp_dd"], writes=["sp_dd"])
            P.op("dve", lambda e: e.tensor_tensor(out=ww[:], in0=tt[:], in1=dd[:], op=ALU.mult), reads=["sp_tt", "sp_dd"], writes=["sp_ww"])
            P.op("dve", lambda e: e.tensor_tensor(out=w2s[:], in0=ww[:], in1=ww[:], op=ALU.mult), reads=["sp_ww"], writes=["sp_w2"])
            P.op("dve", lambda e: e.tensor_scalar(out=rr_[:], in0=w2s[:], scalar1=1.0 / 13.0, scalar2=None, op0=ALU.mult), reads=["sp_w2"], writes=["sp_rr"])
            for cst in (1.0 / 11.0, 1.0 / 9.0, 1.0 / 7.0, 1.0 / 5.0, 1.0 / 3.0):
                P.op("dve", lambda e, cst=cst: e.scalar_tensor_tensor(out=rr_[:], in0=rr_[:], scalar=cst, in1=w2s[:], op0=ALU.add, op1=ALU.mult),
                     reads=["sp_rr", "sp_w2"], writes=["sp_rr"])
            P.op("dve", lambda e: e.scalar_tensor_tensor(out=lnp[:], in0=rr_[:], scalar=1.0, in1=ww[:], op0=ALU.add, op1=ALU.mult),
                 reads=["sp_rr", "sp_ww"], writes=["sp_ln"])
            P.op("dve", lambda e: e.tensor_scalar(out=relu[:], in0=dtu[:], scalar1=0.0, scalar2=None, op0=ALU.max), reads=["dtu"], writes=["sp_relu"])
            P.op("dve", lambda e: e.scalar_tensor_tensor(out=dtv[:], in0=lnp[:], scalar=2.0, in1=relu[:], op0=ALU.mult, op1=ALU.add),
                 reads=["sp_ln", "sp_relu"], writes=["dtv"])

            dump("vx", vx[:], ["vx"], (not pre) and g_local == 0); dump("gz", gz[:], ["gz"], (not pre) and g_local == 0); dump("dtv", dtv[:], ["dtv"], (not pre) and g_local == 0)
            def phaseC(t):
                par = t % NB
                LT, WT, xdt, xdd, xsb, Btok, sc = LT_l[par], WT_l[par], xdt_l[par], xdd_l[par], xsb_l[par], Btok_l[par], sc_l[par]
                rLT, rWT, rxdt, rxdd, rxsb, rBtok, rsc = ["%s%d" % (nm, par) for nm in ("LT", "WT", "xdt", "xdd", "xsb", "Btok", "sc")]
                tsl = slice(t * 128, (t + 1) * 128)
                gt_first = (not pre) and g_local == 0 and t == 0
                dt_ap = dtv[:, t * 8:(t + 1) * 8]
                P.op("dve", lambda e, dt_ap=dt_ap: e.tensor_tensor(out=a_t[:], in0=dt_ap, in1=A_b[:], op=ALU.mult), reads=["dtv", "A_b"], writes=["a_t"])
                pc_, pcr = nextM()

                def fcs(e, pc_=pc_):
                    e.matmul(pc_[:, 0:8], lhsT=tri[:], rhs=a_t[:], start=True, stop=True)
                    return e.matmul(pc_[:, 8:16], lhsT=ones[:], rhs=a_t[:], start=True, stop=True)
                P.op("pe", fcs, reads=["tri", "ones", "a_t"], writes=[pcr])
                P.op("dve", lambda e, pc_=pc_: e.tensor_copy(out=e_in[:, 0:16], in_=pc_[:, 0:16]), reads=[pcr], writes=["e_in"])
                P.op("dve", lambda e: e.tensor_tensor(out=e_in[:, 16:24], in0=e_in[:, 8:16], in1=e_in[:, 0:8], op=ALU.subtract),
                     reads=["e_in"], writes=["e_in2"])
                P.op("act", lambda e: e.activation(out=e24[:], in_=e_in[:], func=AF.Exp), reads=["e_in", "e_in2"], writes=["e24"])
                P.op("dve", lambda e, dt_ap=dt_ap: e.tensor_tensor(out=w2[:], in0=dt_ap, in1=e24[:, 16:24], op=ALU.mult), reads=["dtv", "e24"], writes=["w2"])
                pt, ptr = nextT()

                def ftx(e, pt=pt, tsl=tsl):
                    for c in range(6):
                        ins = e.transpose(out=pt[:, c * 128:(c + 1) * 128], in_=cvo[:, c, tsl], identity=ident[:])
                    return ins
                P.op("pe", ftx, reads=["cvo", "ident"], writes=[ptr])
                P.op("dve", lambda e, pt=pt: e.tensor_tensor(out=xdd[:].rearrange("p (h d) -> p h d", h=8),
                                                            in0=pt[:, 0:512].rearrange("p (h d) -> p h d", h=8),
                                                            in1=w2[:].unsqueeze(2).to_broadcast([128, 8, 64]), op=ALU.mult),
                     reads=[ptr, "w2"], writes=[rxdd])
                P.op("act", lambda e, pt=pt: e.activation(out=Btok[:], in_=pt[:, 512:768], func=AF.Copy), reads=[ptr], writes=[rBtok])
                if (not pre) and STAGE not in (31, 33):
                    P.op("dve", lambda e, pt=pt, dt_ap=dt_ap: e.tensor_tensor(out=xdt[:].rearrange("p (h d) -> p h d", h=8),
                                                                              in0=pt[:, 0:512].rearrange("p (h d) -> p h d", h=8),
                                                                              in1=dt_ap.unsqueeze(2).to_broadcast([128, 8, 64]), op=ALU.mult),
                         reads=[ptr, "dtv"], writes=[rxdt])
                    P.op("act", lambda e, pt=pt: e.activation(out=xsb[:], in_=pt[:, 0:512], func=AF.Copy), reads=[ptr, rxdt, rxdd], writes=[rxsb])
                    P.op("dve", lambda e: e.tensor_scalar(out=nAcs[:], in0=e_in[:, 0:8], scalar1=-1.0, scalar2=None, op0=ALU.mult),
                         reads=["e_in"], writes=["nAcs"])
                    pcb, pcbr = nextM()

                    def fcb(e, pcb=pcb, tsl=tsl):
                        for g in range(2):
                            ins = e.matmul(pcb[:, g * 128:(g + 1) * 128], lhsT=cvo[:, 4 + g, tsl], rhs=cvo[:, 6 + g, tsl], start=True, stop=True)
                        return ins
                    P.op("pe", fcb, reads=["cvo"], writes=[pcbr])
                    pl = [nextM(), nextM()]

                    def fl(e, pl=pl, pcb=pcb):
                        for h in range(8):
                            o = pl[h // 4][0][:, (h % 4) * 128:(h % 4 + 1) * 128]
                            ins = e.matmul(o, lhsT=a_t[:, h:h + 1].to_broadcast([128, 128]), rhs=tri[:], start=True, stop=True)
                        return ins
                    P.op("pe", fl, reads=["a_t", "tri", "ident"], writes=[pl[0][1], pl[1][1]])
                    scv = sc[:].rearrange("p a b -> p (a b)").rearrange("p (h l) -> p h l", h=8)

                    def fl2(e, pl=pl, scv=scv):
                        for h in range(8):
                            ins = e.scalar_tensor_tensor(out=scv[:, h, :], in0=pl[h // 4][0][:, (h % 4) * 128:(h % 4 + 1) * 128],
                                                         scalar=nAcs[:, h:h + 1], in1=maskTf[:], op0=ALU.add, op1=ALU.add)
                        return ins
                    P.op("dve", fl2, reads=[pl[0][1], pl[1][1], "nAcs", "maskTf"], writes=[rsc])
                    P.op("act", lambda e, scv=scv: e.activation(out=LT[:], in_=scv, func=AF.Exp), reads=[rsc], writes=[rLT])

                    def fwt(e, pcb=pcb):
                        for h in range(8):
                            g = h // 4
                            ins = e.tensor_tensor(out=WT[:, h, :], in0=pcb[:, g * 128:(g + 1) * 128], in1=LT[:, h, :], op=ALU.mult)
                        return ins
                    P.op("dve", fwt, reads=[pcbr, rLT], writes=[rWT])
                    py, pyr = nextM()

                    SKIPB = (STAGE == 34)
                    if SKIPB:
                        P.op = lambda *a, **k: None

                    def fy(e, py=py):
                        for h in range(8):
                            hs = slice(h * 64, (h + 1) * 64)
                            e.matmul(py[:, hs], lhsT=WT[:, h, :], rhs=xdt[:, hs], start=True, stop=False)
                            ins = e.matmul(py[:, hs], lhsT=Dg[:, h, :], rhs=xsb[:, hs], start=False, stop=True)
                        return ins
                    P.op("pe", fy, reads=[rWT, rxdt, "Dg", rxsb], writes=[pyr])
                    pyo, pyor = nextM()

                    def fyo(e, pyo=pyo, tsl=tsl):
                        for g in range(2):
                            ins = e.matmul(pyo[:, g * 256:(g + 1) * 256], lhsT=cvo[:, 6 + g, tsl], rhs=Sbf[:, g * 256:(g + 1) * 256], start=True, stop=True)
                        return ins
                    P.op("pe", fyo, reads=["cvo", "Sbf"], writes=[pyor])
                    P.op("dve", lambda e, pyo=pyo: e.tensor_tensor(out=yt[:].rearrange("p (h d) -> p h d", h=8),
                                                                  in0=pyo[:, 0:512].rearrange("p (h d) -> p h d", h=8),
                                                                  in1=e24[:, 0:8].unsqueeze(2).to_broadcast([128, 8, 64]), op=ALU.mult),
                         reads=[pyor, "e24"], writes=["yt"])
                    P.op("dve", lambda e, py=py: e.tensor_tensor(out=yt[:], in0=py[:, 0:512], in1=yt[:], op=ALU.add), reads=[pyr, "yt"], writes=["yt"])
                    P.op("dve", lambda e, t=t: e.tensor_tensor(out=yg[:], in0=yt[:], in1=gz[:, t, :], op=ALU.mult), reads=["yt", "gz"], writes=["yg"])
                    for g in range(2):
                        rms_stats(yg[:, g * 256:(g + 1) * 256], "yg", 1.0 / 16.0, 1 + g, eps4)

                    def fmx(e):
                        for g in range(2):
                            ins = e.tensor_scalar(out=mixb[:, 512 + g * 256:512 + (g + 1) * 256], in0=yg[:, g * 256:(g + 1) * 256],
                                                  scalar1=rstd[:, 1 + g:2 + g], scalar2=None, op0=ALU.mult)
                        return ins
                    P.op("dve", fmx, reads=["yg", "rstd1", "rstd2"], writes=["mixb_s"])
                    if SKIPB:
                        del P.op
                if (not pre) and STAGE not in (31, 33):
                    dump("e24", e24[:], ["e24"], (not pre) and g_local == 0 and t == 0); dump(rLT, LT[:], [rLT], (not pre) and g_local == 0 and t == 0)
                    dump(rWT, WT[:], [rWT], (not pre) and g_local == 0 and t == 0); dump("yt", yt[:], ["yt"], (not pre) and g_local == 0 and t == 0)
                    dump("yg", yg[:], ["yg"], (not pre) and g_local == 0 and t == 0)
                pst, pstr = nextM()

                def fst(e, pst=pst):
                    for g in range(2):
                        ins = e.matmul(pst[:, g * 256:(g + 1) * 256], lhsT=Btok[:, g * 128:(g + 1) * 128], rhs=xdd[:, g * 256:(g + 1) * 256], start=True, stop=True)
                    return ins
                P.op("pe", fst, reads=[rBtok, rxdd], writes=[pstr])
                P.op("pool", lambda e: e.tensor_tensor(out=S[:].rearrange("p (h d) -> p h d", h=8), in0=S[:].rearrange("p (h d) -> p h d", h=8),
                                                       in1=e24[:, 8:16].unsqueeze(2).to_broadcast([128, 8, 64]), op=ALU.mult),
                     reads=["S", "e24"], writes=["S"])
                P.op("dve", lambda e, pst=pst: e.tensor_tensor(out=S[:], in0=pst[:, 0:512], in1=S[:], op=ALU.add), reads=[pstr, "S"], writes=["S"])
                P.op("act", lambda e: e.activation(out=Sbf[:], in_=S[:], func=AF.Copy), reads=["S"], writes=["Sbf"])

                if pre or STAGE in (31, 32, 34):
                    return
                def attn(hp):
                    sc, pb, pTs = sc_l[hp % NB], pb_l[hp % NB], pTs_l[hp % NB]
                    rsc, rpb, rpTs = ["%s%d" % (nm, hp % NB) for nm in ("sc", "pb", "pTs")]
                    ksl = slice(hp * 64, (hp + 1) * 64)
                    pS = [nextM(), nextM()]

                    def fsc(e, pS=pS, ksl=ksl, t=t, tsl=tsl):
                        for c in range(4):
                            ins = e.matmul(pS[c // 2][0][:, (c % 2) * 256:(c % 2 + 1) * 256], lhsT=qT[ksl, c, tsl],
                                           rhs=kT[ksl, t * 128:t * 128 + 256], start=True, stop=True)
                        return ins
                    P.op("pe", fsc, reads=["qT", "kT"], writes=[pS[0][1], pS[1][1]])

                    def fsb(e, pS=pS, hp=hp):
                        for i in range(2):
                            ins = e.scalar_tensor_tensor(out=sc[:, 2 * i:2 * i + 2, :], in0=pS[i][0][:, 0:512].rearrange("p (c n) -> p c n", c=2),
                                                         scalar=0.125, in1=biasf[:, 4 * hp + 2 * i:4 * hp + 2 * i + 2, :], op0=ALU.mult, op1=ALU.add)
                        return ins
                    P.op("dve", fsb, reads=[pS[0][1], pS[1][1], "biasf"], writes=[rsc])
                    if gt_first:
                        P.op("dve", lambda e: e.tensor_scalar(out=sc[:, :, 0:128], in0=sc[:, :, 0:128], scalar1=maskb[:, 0:1], scalar2=None, op0=ALU.add),
                             reads=[rsc, "maskb"], writes=[rsc])
                    P.op("dve", lambda e: e.tensor_reduce(out=rmax[:], in_=sc[:], axis=AX.X, op=ALU.max), reads=[rsc], writes=["rmax"])
                    P.op("dve", lambda e, hp=hp: e.scalar_tensor_tensor(out=negm[:], in0=rmax[:], scalar=-1.0, in1=nsink[:, 4 * hp:4 * hp + 4],
                                                                       op0=ALU.mult, op1=ALU.min),
                         reads=["rmax", "nsink"], writes=["negm"])

                    def fex(e):
                        for c in range(4):
                            ins = e.activation(out=pb[:, c, :], in_=sc[:, c, :], func=AF.Exp, bias=negm[:, c:c + 1], scale=1.0,
                                               accum_out=rsum[:, c:c + 1])
                        return ins
                    P.op("act", fex, reads=[rsc, "negm"], writes=[rpb, "rsum"])
                    P.op("dve", lambda e, hp=hp: e.tensor_tensor(out=stmp[:], in0=sink[:, 4 * hp:4 * hp + 4], in1=negm[:], op=ALU.add),
                         reads=["sink", "negm"], writes=["stmp"])
                    P.op("act", lambda e: e.activation(out=es[:], in_=stmp[:], func=AF.Exp), reads=["stmp"], writes=["es"])
                    P.op("dve", lambda e: e.tensor_tensor(out=den[:], in0=rsum[:], in1=es[:], op=ALU.add), reads=["rsum", "es"], writes=["den"])
                    P.op("dve", lambda e: e.reciprocal(out=rden[:], in_=den[:]), reads=["den"], writes=["rden"])
                    pt, ptr = nextT()

                    def ftp(e, pt=pt):
                        for c in range(4):
                            for j in range(2):
                                ins = e.transpose(out=pt[:, (2 * c + j) * 128:(2 * c + j + 1) * 128], in_=pb[:, c, j * 128:(j + 1) * 128], identity=ident[:])
                        return ins
                    P.op("pe", ftp, reads=[rpb, "ident"], writes=[ptr])
                    P.op("act", lambda e, pt=pt: e.activation(out=pTs[:], in_=pt[:, 0:1024].rearrange("p (c n) -> p c n", c=8), func=AF.Copy),
                         reads=[ptr], writes=[rpTs])
                    po, por = nextM()

                    def fpv(e, po=po, hp=hp, t=t):
                        for c in range(4):
                            e.matmul(po[:, c * 64:(c + 1) * 64], lhsT=pTs[:, 2 * c, :], rhs=vx[:, t, hp * 64:(hp + 1) * 64], start=True, stop=False)
                            ins = e.matmul(po[:, c * 64:(c + 1) * 64], lhsT=pTs[:, 2 * c + 1, :], rhs=vx[:, t + 1, hp * 64:(hp + 1) * 64], start=False, stop=True)
                        return ins
                    P.op("pe", fpv, reads=[rpTs, "vx"], writes=[por])
                    P.op("dve", lambda e, po=po, hp=hp: e.tensor_tensor(out=ya[:, hp * 256:(hp + 1) * 256].rearrange("p (h d) -> p h d", h=4),
                                                                       in0=po[:, 0:256].rearrange("p (h d) -> p h d", h=4),
                                                                       in1=rden[:].unsqueeze(2).to_broadcast([128, 4, 64]), op=ALU.mult),
                         reads=[por, "rden"], writes=["ya%d" % hp])
                for hp in range(2):
                    attn(hp)
                rms_stats(ya[:], ["ya0", "ya1"], 512.0 ** -0.5, 3, eps1)
                P.op("dve", lambda e: e.tensor_scalar(out=mixb[:, 0:512], in0=ya[:], scalar1=rstd[:, 3:4], scalar2=None, op0=ALU.mult),
                     reads=["ya0", "ya1", "rstd3", "xn"], writes=["mixb_a"])
                dump("ya", ya[:], ["ya0", "ya1"], (not pre) and g_local == 0 and t == 0); dump("mixb", mixb[:], ["mixb_a", "mixb_s"], (not pre) and g_local == 0 and t == 0)
                pt, ptr = nextT()

                def ftm(e, pt=pt):
                    for k in range(8):
                        ins = e.transpose(out=pt[:, k * 128:(k + 1) * 128], in_=mixb[:, k * 128:(k + 1) * 128], identity=ident[:])
                    return ins
                P.op("pe", ftm, reads=["mixb_a", "mixb_s", "ident"], writes=[ptr])
                P.op("act", lambda e, pt=pt, tsl=tsl: e.activation(out=hT[:, :, tsl], in_=pt[:, 0:1024].rearrange("p (c n) -> p c n", c=8), func=AF.Copy),
                     reads=[ptr], writes=["hT"])

            for t in range(2):
                phaseC(t)
            P.op("dve", lambda e: e.tensor_copy(out=kT[:, 0:128], in_=kT[:, 256:384]), reads=["kT"], writes=["kT"])
            P.op("dve", lambda e: e.tensor_copy(out=vx[:, 0, :], in_=vx[:, 2, :]), reads=["vx"], writes=["vx"])
            if pre or STAGE in (3, 31, 32, 33, 34):
                continue

            for t in range(2):
                tsl = slice(t * 128, (t + 1) * 128)
                for nh in range(2):
                    pm, pr = nextM()

                    def fo(e, pm=pm, nh=nh, tsl=tsl):
                        for k in range(8):
                            ins = e.matmul(pm[:, 0:512], lhsT=hT[:, k, tsl], rhs=w_o_bf[:, k, nh * 512:(nh + 1) * 512], start=(k == 0), stop=(k == 7))
                        return ins
                    P.op("pe", fo, reads=["hT", "w_o"], writes=[pr])
                    P.op("dve", lambda e, pm=pm, nh=nh, t=t, xs=xs: e.tensor_tensor(out=xs[:, t, nh * 512:(nh + 1) * 512], in0=pm[:, 0:512],
                                                                               in1=xs[:, t, nh * 512:(nh + 1) * 512], op=ALU.add),
                         reads=[pr, xr], writes=[xr])
                norm_transpose(xs[:, t, :], xr, g2, sh2, hT2, "hT2", t)

            if STAGE == 4:
                continue
            dump("x1", xs[:], [xr], (not pre) and g_local == 0); dump("h2T", hT2[:], ["hT2"], (not pre) and g_local == 0); dump("hT", hT[:], ["hT"], (not pre) and g_local == 0)
            P.prio = 1
            for j in range(22):
                sl_w = j % NSLOT
                b = j % 2
                pg, pgr = psM[b], "psM%d" % b

                def fgu(e, pg=pg, sl_w=sl_w):
                    for k in range(8):
                        e.matmul(pg[:, 0:256], lhsT=wgu[sl_w][:, k, 0:128], rhs=hT2[:, k, :], start=(k == 0), stop=(k == 7))
                    for k in range(8):
                        ins = e.matmul(pg[:, 256:512], lhsT=wgu[sl_w][:, k, 128:256], rhs=hT2[:, k, :], start=(k == 0), stop=(k == 7))
                    return ins
                P.op("pe", fgu, reads=["wgu%d" % sl_w, "hT2"], writes=[pgr])
                P.op("act", lambda e, pg=pg, b=b: e.activation(out=thg[b][:], in_=pg[:, 0:256], func=AF.Tanh, scale=0.5), reads=[pgr], writes=["thg%d" % b])
                P.op("act", lambda e, pg=pg, b=b: e.activation(out=ub[b][:], in_=pg[:, 256:512], func=AF.Copy), reads=[pgr], writes=["ub_%d" % b])
                P.op("dve", lambda e, pg=pg, b=b: e.scalar_tensor_tensor(out=t2[b][:], in0=thg[b][:], scalar=1.0, in1=pg[:, 0:256], op0=ALU.add, op1=ALU.mult),
                     reads=["thg%d" % b, pgr], writes=["t2_%d" % b])
                P.op("pool", lambda e, b=b, j=j: e.tensor_tensor(out=actT[:, j, :], in0=t2[b][:], in1=ub[b][:], op=ALU.mult),
                     reads=["t2_%d" % b, "ub_%d" % b], writes=["actT%d" % j])
                if j + NSLOT < 22:
                    load_w(j + NSLOT)
            for nh in range(2):
                for j in range(22):
                    idx = nh * 22 + j
                    sl_d = idx % NDSLOT

                    def fdn(e, sl_d=sl_d, j=j):
                        for t in range(2):
                            ins = e.matmul(psM[t][:, 0:512], lhsT=actT[:, j, t * 128:(t + 1) * 128], rhs=wdn[sl_d][:],
                                           start=(j == 0), stop=(j == 21))
                        return ins
                    P.op("pe", fdn, reads=["actT%d" % j, "wdn%d" % sl_d], writes=["psM0", "psM1"])
                    if idx + NDSLOT < 44:
                        load_wd(idx + NDSLOT)
                for t in range(2):
                    P.op("dve", lambda e, nh=nh, t=t, xs=xs: e.tensor_tensor(out=xs[:, t, nh * 512:(nh + 1) * 512], in0=psM[t][:, 0:512],
                                                                            in1=xs[:, t, nh * 512:(nh + 1) * 512], op=ALU.add),
                         reads=["psM%d" % t, xr], writes=[xr])
            for t in range(2):
                rms_stats(xs[:, t, :], xr, 1.0 / 32.0, 4, eps1)
                P.op("dve", lambda e, t=t, xs=xs: e.scalar_tensor_tensor(out=xs[:, t, :], in0=xs[:, t, :], scalar=rstd[:, 4:5], in1=fnorm[:], op0=ALU.mult, op1=ALU.mult),
                     reads=[xr, "rstd4", "fnorm"], writes=[xr])
            P.dma("sp", [lambda e, g_local=g_local, xs=xs: e.dma_start(out=out[g_local * 256:(g_local + 1) * 256, :].rearrange("(t p) d -> p t d", p=128), in_=xs[:])],
                  osem[sl], reads=[xr])
            P.prio = 0

        P.finish("sp")
        P.emit_all()
    return nc


def _t5_buckets(dist):
    n = np.maximum(dist, 0)
    max_exact = 16
    large = max_exact + (np.log(np.maximum(n, 1) / max_exact) / np.log(128 / max_exact) * (32 - max_exact)).astype(np.int32)
    large = np.minimum(large, 31)
    return np.where(n < max_exact, n, large).astype(np.int32)


def _col(v, nchunk):
    return np.ascontiguousarray(np.asarray(v, np.float32).reshape(nchunk, 128).T)


def _bc(v):
    return np.ascontiguousarray(np.broadcast_to(np.asarray(v, np.float32)[None, :], (128, len(v))))


_NC_CACHE = {}
_DBG = False
_LAST = None


def kernel(x, c, ada_w, ada_b, norm1, w_in, conv_w, conv_b, dt_bias, A_log, D_skip, sinks,
           attn_out_norm, ssm_out_norm, w_o, norm2, w_gate_up, w_down, rel_bias, final_norm):
    x = np.asarray(x, np.float32)
    B, S_, D = x.shape
    ntok = S_ // 2
    f = lambda a: np.asarray(a, np.float32)
    ada_b0 = f(ada_b)[0]
    dist = np.arange(128)[:, None] + 128 - np.arange(256)[None, :]
    valid = (dist >= 0) & (dist < 128)
    gathered = f(rel_bias)[_t5_buckets(dist)]
    biasf = np.where(valid[:, :, None], gathered, np.float32(NEG)).astype(np.float32)
    biasf = np.ascontiguousarray(np.transpose(biasf, (0, 2, 1)))
    perm = []
    for cch in range(4):
        perm += list(range(cch * 64, cch * 64 + 64)) + list(range((cch + 4) * 64, (cch + 4) * 64 + 64))
    w_in0 = f(w_in)[0]
    w_in_p = np.ascontiguousarray(np.concatenate([w_in0[:, perm], w_in0[:, 512:]], axis=1))
    shared = {
        "ada_w": np.ascontiguousarray(f(ada_w)[0]),
        "adab_col": _col(ada_b0, 48),
        "adab_g1": _bc(ada_b0[2048:3072]),
        "adab_g2": _bc(ada_b0[5120:6144]),
        "norm1_col": _col(f(norm1)[0], 8),
        "norm2_col": _col(f(norm2)[0], 8),
        "mixnorm_col": _col(np.concatenate([f(attn_out_norm)[0], f(ssm_out_norm)[0]]), 8),
        "w_in": w_in_p,
        "convw_col": np.ascontiguousarray(np.transpose(f(conv_w)[0].reshape(4, 8, 128), (2, 1, 0))),
        "convb_col": _col(f(conv_b)[0], 8),
        "dtb_b": _bc(f(dt_bias)[0]),
        "alog_b": _bc(f(A_log)[0]),
        "dskip_b": _bc(f(D_skip)[0]),
        "sinks_b": _bc(f(sinks)[0]),
        "w_o": np.ascontiguousarray(f(w_o)[0]),
        "w_gu": np.ascontiguousarray(f(w_gate_up)[0]),
        "w_dn": np.ascontiguousarray(f(w_down)[0]),
        "biasf": biasf,
        "fnorm_b": _bc(f(final_norm)),
    }
    in_maps = []
    for core in range(8):
        b, half = core // 2, core % 2
        m = dict(shared)
        m["x_main"] = np.ascontiguousarray(x[b, half * ntok:(half + 1) * ntok])
        m["x_pre"] = np.ascontiguousarray(x[b, 0:ntok]) if half == 1 else np.zeros((ntok, D), np.float32)
        m["c_col"] = _col(f(c)[b], 8)
        m["flag"] = np.full((128, 1), float(half), np.float32)
        in_maps.append(m)
    if ntok not in _NC_CACHE:
        _NC_CACHE[ntok] = build_nc(ntok, dbg=_DBG)
    res = run_bass_kernel_spmd(_NC_CACHE[ntok], in_maps, core_ids=list(range(8)))
    if _DBG:
        global _LAST
        _LAST = res.results
    outp = np.empty((B, S_, D), np.float32)
    for core in range(8):
        b, half = core // 2, core % 2
        outp[b, half * ntok:(half + 1) * ntok] = res.results[core]["out"]
    return outp
```

```python
import contextlib
import numpy as np
import concourse.bass as bass
import concourse.mybir as mybir
from concourse.bass_utils import run_bass_kernel_spmd

F32 = mybir.dt.float32
BF16 = mybir.dt.bfloat16
AF = mybir.ActivationFunctionType
ALU = mybir.AluOpType
AX = mybir.AxisListType

ENGS = ("pe", "act", "dve", "pool", "sp")

D_MODEL = 1024
IN_WIDTH = 2312
D_FF = 2816
NEG = -30000.0
NSLOT = 4
NXS = 3
NDSLOT = 6
DMA_RATE = 250e3


class DmaSem:
    def __init__(self, handle):
        self.h = handle
        self.count = 0


class _FakeIns:
    def then_inc(self, *a, **k):
        return self


def _numel(ap):
    n = 1
    for d in ap.shape[1:]:
        n *= int(d)
    return n


class _FakeEng:
    def __init__(self, eng):
        self.eng = eng
        self.cost = 0.0
        self.bytes = 0

    def __getattr__(self, name):
        def call(*args, **kw):
            out = kw.get("out", args[0] if args else None)
            n = _numel(out) if out is not None and hasattr(out, "shape") else 1
            if name == "matmul":
                lhsT = kw.get("lhsT", args[1] if len(args) > 1 else None)
                mult = 4.0 if (lhsT is not None and lhsT.dtype == F32) else 1.0
                self.cost += mult * (n / 2400.0 + 0.035)
            elif name == "transpose":
                self.cost += n / 2400.0 + 0.035
            elif name == "dma_start":
                self.cost += 0.06
                self.bytes += n * int(out.shape[0]) * (2 if out.dtype == BF16 else 4)
            elif name == "wait_ge":
                pass
            elif self.eng == "act":
                self.cost += n / 1200.0 + 0.22
            elif self.eng == "dve":
                self.cost += n / 960.0 + 0.17
            else:
                self.cost += n / 500.0 + 0.3
            return _FakeIns()
        return call


class _Op:
    __slots__ = ("i", "eng", "fn", "fns", "dsem", "preds", "cost", "lat", "start", "done", "idx", "val", "succs", "npred", "lab", "crit", "tag", "prio")


class Prog:
    LOOKAHEAD = 700
    BIAS = 1.5
    QUANT = 0.1

    def __init__(self, nc, stack):
        self.nc = nc
        self.stack = stack
        self.esem = {e: stack.enter_context(nc.semaphore("es_" + e)) for e in ENGS if e != "sp"}
        self.ops = []
        self.lastw = {}
        self.readers = {}
        self.nsem = 0
        self.dma_sems = []
        self.last_on_dsem = {}
        self.q = {e: [] for e in ENGS}

    def dma_sem(self, name=None):
        self.nsem += 1
        s = DmaSem(self.stack.enter_context(self.nc.semaphore(name or ("ds%d" % self.nsem))))
        self.dma_sems.append(s)
        return s

    def _deps(self, reads, writes, eng=None):
        preds = {}

        def need(i, raw):
            if i is None:
                return
            preds[i] = preds.get(i, False) or raw
        for r in reads:
            need(self.lastw.get(r), True)
            if r.startswith("ps"):
                for i in self.readers.get(r, ()):
                    if self.ops[i].eng != eng:
                        need(i, True)
        for w in writes:
            need(self.lastw.get(w), False)
            for i in self.readers.get(w, ()):
                need(i, False)
        return preds

    def _record(self, i, reads, writes):
        for r in reads:
            self.readers.setdefault(r, []).append(i)
        for w in writes:
            self.lastw[w] = i
            self.readers[w] = []

    def _new(self, eng, reads, writes):
        o = _Op()
        o.i = len(self.ops)
        o.eng = eng
        o.preds = self._deps(reads, writes, eng)
        o.fn = None
        o.fns = None
        o.dsem = None
        o.lab = (tuple(reads), tuple(writes))
        o.tag = getattr(self, "tag", None)
        o.prio = getattr(self, "prio", 0)
        self.ops.append(o)
        self._record(o.i, reads, writes)
        return o

    def op(self, eng, fn, reads=(), writes=()):
        o = self._new(eng, reads, writes)
        o.fn = fn
        fk = _FakeEng(eng)
        fn(fk)
        o.cost = fk.cost
        o.lat = 0.0

    def dma(self, eng, fns, dsem, reads=(), writes=()):
        o = self._new(eng, reads, writes)
        o.fns = fns
        o.dsem = dsem
        prev = self.last_on_dsem.get(id(dsem))
        if prev is not None and prev not in o.preds:
            o.preds[prev] = False
        self.last_on_dsem[id(dsem)] = o.i
        fk = _FakeEng(eng)
        for f in fns:
            f(fk)
        o.cost = fk.cost
        o.lat = 2.2 + fk.bytes / DMA_RATE

    def finish(self, eng="sp"):
        ops = self.ops
        n = len(ops)
        for o in ops:
            o.succs = []
            o.npred = len(o.preds)
            o.start = None
        for o in ops:
            for p in o.preds:
                ops[p].succs.append(o.i)
        bl = [0.0] * n
        for o in reversed(ops):
            m = 0.0
            for sidx in o.succs:
                if bl[sidx] > m:
                    m = bl[sidx]
            bl[o.i] = m + o.cost + o.lat
        ready = [o.i for o in ops if o.npred == 0]
        eng_free = {e: 0.0 for e in ENGS}
        order = {e: [] for e in ENGS}
        oldest = 0
        scheduled = 0
        while scheduled < n:
            while oldest < n and ops[oldest].start is not None:
                oldest += 1
            best = None
            for i in ready:
                if i > oldest + self.LOOKAHEAD:
                    continue
                o = ops[i]
                t = eng_free[o.eng]
                o.crit = -1
                for p in o.preds:
                    po = ops[p]
                    d = po.done + (0.0 if po.eng == o.eng and po.dsem is None else 0.08)
                    if d > t:
                        t = d
                        o.crit = p
                key = (int(t / self.QUANT), -bl[i], i) if self.QUANT > 0 else (t + self.BIAS * o.prio, i)
                if best is None or key < best[0]:
                    best = (key, i, t)
            _, i, t = best
            o = ops[i]
            o.start = t
            eng_free[o.eng] = t + o.cost
            o.done = t + o.cost + o.lat
            if o.crit == -1 and order[o.eng]:
                o.crit = -2 - order[o.eng][-1]
            order[o.eng].append(i)
            ready.remove(i)
            scheduled += 1
            for sidx in o.succs:
                so = ops[sidx]
                so.npred -= 1
                if so.npred == 0:
                    ready.append(sidx)
        self.makespan = max(o.done for o in ops)
        cnt = {e: 0 for e in ENGS}
        for e in ENGS:
            for i in order[e]:
                o = ops[i]
                if o.dsem is None:
                    cnt[e] += 1
                    o.val = (self.esem[e], cnt[e])
                else:
                    o.dsem.count += 16 * len(o.fns)
                    o.val = (o.dsem.h, o.dsem.count)
        for e in ENGS:
            waited = {}
            for i in order[e]:
                o = ops[i]
                waits = {}
                for p, raw in o.preds.items():
                    po = ops[p]
                    if po.dsem is None and po.eng == e:
                        if e == "pe" or not raw:
                            continue
                    sem, val = po.val
                    k = id(sem)
                    if waited.get(k, 0) >= val:
                        continue
                    if k not in waits or waits[k][1] < val:
                        waits[k] = (sem, val)
                for k, (sem, val) in waits.items():
                    waited[k] = val
                wl = list(waits.values())
                if o.dsem is None:
                    def emit(en, wl=wl, fn=o.fn, sem=o.val[0]):
                        for (s, v) in wl:
                            en.wait_ge(s, v)
                        fn(en).then_inc(sem, 1)
                else:
                    def emit(en, wl=wl, fns=o.fns, h=o.dsem.h):
                        for (s, v) in wl:
                            en.wait_ge(s, v)
                        for f in fns:
                            f(en).then_inc(h, 16)
                self.q[e].append(emit)
        fw = []
        for e in ENGS:
            if e != "sp" and cnt[e] > 0:
                fw.append((self.esem[e], cnt[e]))
        for s in self.dma_sems:
            if s.count > 0:
                fw.append((s.h, s.count))

        def emit_fin(en, fw=fw):
            for (s, v) in fw:
                en.wait_ge(s, v)
        self.q[eng].append(emit_fin)

    def emit_all(self):
        with self.nc.Block() as block:
            @block.tensor
            def _(e):
                for f in self.q["pe"]:
                    f(e)

            @block.scalar
            def _(e):
                for f in self.q["act"]:
                    f(e)

            @block.vector
            def _(e):
                for f in self.q["dve"]:
                    f(e)

            @block.gpsimd
            def _(e):
                for f in self.q["pool"]:
                    f(e)

            @block.sync
            def _(e):
                for f in self.q["sp"]:
                    f(e)


def build_nc(ntok, dbg=False):
    NG = ntok // 256
    DBG = {}
    nc = bass.Bass("TRN2", target_bir_lowering=False)

    def din(name, shape, dt=F32):
        return nc.dram_tensor(name, list(shape), dt, kind="ExternalInput").ap()

    x_main = din("x_main", [ntok, 1024])
    x_pre = din("x_pre", [ntok, 1024])
    c_col = din("c_col", [128, 8])
    ada_w = din("ada_w", [1024, 6144])
    adab_col = din("adab_col", [128, 48])
    adab_g1 = din("adab_g1", [128, 1024])
    adab_g2 = din("adab_g2", [128, 1024])
    norm1_col = din("norm1_col", [128, 8])
    norm2_col = din("norm2_col", [128, 8])
    mixnorm_col = din("mixnorm_col", [128, 8])
    w_in = din("w_in", [1024, IN_WIDTH])
    convw_col = din("convw_col", [128, 8, 4])
    convb_col = din("convb_col", [128, 8])
    dtb_b = din("dtb_b", [128, 8])
    alog_b = din("alog_b", [128, 8])
    dskip_b = din("dskip_b", [128, 8])
    sinks_b = din("sinks_b", [128, 8])
    w_o = din("w_o", [1024, 1024])
    w_gu = din("w_gu", [1024, 2 * D_FF])
    w_dn = din("w_dn", [D_FF, 1024])
    biasf_in = din("biasf", [128, 8, 256])
    fnorm_in = din("fnorm_b", [128, 1024])
    flag_in = din("flag", [128, 1])
    out = nc.dram_tensor("out", [ntok, 1024], F32, kind="ExternalOutput").ap()
    wgu_s = nc.dram_tensor("wgu_s", [22, 128, 2048], BF16).ap()
    wdn_s = nc.dram_tensor("wdn_s", [2, 22, 128, 512], BF16).ap()

    with contextlib.ExitStack() as st:
        P = Prog(nc, st)

        def T(name, shape, dt=F32):
            return st.enter_context(nc.sbuf_tensor(name, list(shape), dt))

        def PSUM(name, shape, dt=F32):
            return st.enter_context(nc.psum_tensor(name, list(shape), dt))

        w_in_bf = T("w_in_bf", [128, 8, IN_WIDTH], BF16)
        w_o_bf = T("w_o_bf", [128, 8, 1024], BF16)
        wgu = [T("wgu%d" % i, [128, 8, 256], BF16) for i in range(NSLOT)]
        wdn = [T("wdn%d" % i, [128, 512], BF16) for i in range(NDSLOT)]
        diagW = T("diagW", [128, 8, 4, 128], BF16)
        actT = T("actT", [128, 22, 256], BF16)
        hT2 = T("hT2", [128, 8, 256], BF16)
        biasf = T("biasf_sb", [128, 8, 256])
        fnorm = T("fnorm_sb", [128, 1024])
        Dg = T("Dg", [128, 8, 128], BF16)
        S = T("S", [128, 512])
        Sbf = T("Sbf", [128, 512], BF16)
        ident = T("ident", [128, 128], BF16)
        identf = T("identf", [128, 128])
        tri = T("tri", [128, 128])
        ones = T("ones", [128, 128])
        maskTf = T("maskTf", [128, 128])
        maskT = T("maskT", [128, 128], BF16)
        g1 = T("g1", [128, 8]); sh1 = T("sh1", [128, 8]); g2 = T("g2", [128, 8]); sh2 = T("sh2", [128, 8])
        modc = T("modc", [128, 48])
        adabc = T("adabc", [128, 48])
        n1c = T("n1c", [128, 8]); n2c = T("n2c", [128, 8]); mnc = T("mnc", [128, 8])
        ccol = T("ccol", [128, 8]); cth = T("cth", [128, 8]); condf = T("condf", [128, 8]); condb = T("condb", [128, 8], BF16)
        convw = T("convw", [128, 8, 4]); convb = T("convb", [128, 8])
        dtb = T("dtb", [128, 8]); A_b = T("A_b", [128, 8]); dsk = T("dsk", [128, 8])
        sink = T("sink", [128, 8]); nsink = T("nsink", [128, 8])
        flag = T("flag_sb", [128, 1]); maskb = T("maskb", [128, 1])
        eps1 = T("eps1", [128, 8]); eps4 = T("eps4", [128, 8]); mhalf = T("mhalf", [128, 8])
        kT = T("kT", [128, 384], BF16)
        vx = T("vx", [128, 3, 128], BF16)
        raw = T("raw", [128, 8, 259], BF16)
        xg = [T("xg%d" % i, [128, 2, 1024]) for i in range(NXS)]
        hT = T("hT", [128, 8, 256], BF16)
        qT = T("qT", [128, 4, 256], BF16)
        cvo = T("cvo", [128, 8, 256], BF16)
        gz = T("gz", [128, 2, 512], BF16)
        dtu = T("dtu", [128, 16]); dtv = T("dtv", [128, 16])
        sp_t = [T("sp_t%d" % i, [128, 16]) for i in range(8)]
        xn = T("xn", [128, 1024], BF16)
        junk = T("junk", [128, 1024], BF16)
        mixb = T("mixb", [128, 1024], BF16)
        NB = 1
        sc_l = [T("sc%d" % i, [128, 4, 256]) for i in range(NB)]
        pb_l = [T("pb%d" % i, [128, 4, 256], BF16) for i in range(NB)]
        pTs_l = [T("pTs%d" % i, [128, 8, 128], BF16) for i in range(NB)]
        LT_l = [T("LT%d" % i, [128, 8, 128], BF16) for i in range(NB)]
        WT_l = [T("WT%d" % i, [128, 8, 128], BF16) for i in range(NB)]
        xdt_l = [T("xdt%d" % i, [128, 512], BF16) for i in range(NB)]
        xdd_l = [T("xdd%d" % i, [128, 512], BF16) for i in range(NB)]
        xsb_l = [T("xsb%d" % i, [128, 512], BF16) for i in range(NB)]
        Btok_l = [T("Btok%d" % i, [128, 256], BF16) for i in range(NB)]
        ya = T("ya", [128, 512]); yt = T("yt", [128, 512]); yg = T("yg", [128, 512])
        acc = [T("acc%d" % i, [128, 2, 256]) for i in range(2)]
        cth2 = [T("cth2_%d" % i, [128, 2, 256]) for i in range(2)]
        thg = [T("thg%d" % i, [128, 256]) for i in range(2)]
        t2 = [T("t2_%d" % i, [128, 256], BF16) for i in range(2)]
        ub = [T("ub_%d" % i, [128, 256], BF16) for i in range(2)]
        ms = T("ms", [128, 8]); mse = T("mse", [128, 8]); rstd = T("rstd", [128, 8])
        a_t = T("a_t", [128, 8]); e_in = T("e_in", [128, 24]); e24 = T("e24", [128, 24])
        w2 = T("w2", [128, 8]); nAcs = T("nAcs", [128, 8])
        rmax = T("rmax", [128, 4]); negm = T("negm", [128, 4]); rsum = T("rsum", [128, 4])
        stmp = T("stmp", [128, 4]); es = T("es", [128, 4]); den = T("den", [128, 4]); rden = T("rden", [128, 4])
        psT = [PSUM("psT%d" % i, [128, 1024], BF16) for i in range(2)]
        psM = [PSUM("psM%d" % i, [128, 512], F32) for i in range(6)]
        rr = {"m": 0, "t": 0}

        def nextM():
            i = 2 + rr["m"] % 4
            rr["m"] += 1
            return psM[i], "psM%d" % i

        def nextT():
            i = rr["t"] % 2
            rr["t"] += 1
            return psT[i], "psT%d" % i

        dsem_dbg = P.dma_sem("dbg") if dbg else None

        def dump(name, ap, res, cond=True):
            if not (dbg and cond) or name in DBG:
                return
            dt_ = ap.dtype
            d = nc.dram_tensor("dbg_" + name, list(ap.shape), dt_, kind="ExternalOutput").ap()
            DBG[name] = d
            P.dma("sp", [lambda e, d=d, ap=ap: e.dma_start(out=d, in_=ap)], dsem_dbg, reads=res)

        abrows = actT[:].rearrange("p j n -> p (j n)").bitcast(F32)
        ab1, ab2 = abrows[:, 0:1024], abrows[:, 1024:2048]
        gate1_b = sc_l[0][:].rearrange("p a b -> p (a b)")
        gate2h_b = hT2[:].rearrange("p k n -> p (k n)").bitcast(F32)
        fence_t = T("fence_t", [128, 1])
        s_small = P.dma_sem("s_small")
        small = [(ccol, c_col, "ccol"), (adabc, adab_col, "adabc"), (n1c, norm1_col, "n1c"), (n2c, norm2_col, "n2c"),
                 (mnc, mixnorm_col, "mnc"), (convw, convw_col, "convw"), (convb, convb_col, "convb"),
                 (dtb, dtb_b, "dtb"), (A_b, alog_b, "A_b"), (dsk, dskip_b, "dsk"), (sink, sinks_b, "sink"),
                 (flag, flag_in, "flag"), (biasf, biasf_in, "biasf"), (fnorm, fnorm_in, "fnorm")]
        P.dma("sp", [(lambda e, d=d, s=s: e.dma_start(out=d[:], in_=s)) for d, s, _ in small], s_small,
              writes=[r for _, _, r in small])
        s_g = P.dma_sem("s_g")
        P.dma("sp", [lambda e: e.dma_start(out=ab1, in_=adab_g1),
                     lambda e: e.dma_start(out=ab2, in_=adab_g2)], s_g, writes=["abrows"])

        P.op("pool", lambda e: e.memset(ones[:], 1.0), writes=["ones"])
        P.op("pool", lambda e: e.affine_select(out=tri[:], in_=ones[:], pattern=[[1, 128]], compare_op=ALU.is_ge,
                                               fill=0.0, base=0, channel_multiplier=-1), reads=["ones"], writes=["tri"])
        P.op("pool", lambda e: e.affine_select(out=identf[:], in_=tri[:], pattern=[[-1, 128]], compare_op=ALU.is_ge,
                                               fill=0.0, base=0, channel_multiplier=1), reads=["tri"], writes=["identf"])
        P.op("pool", lambda e: e.memset(maskTf[:], NEG), writes=["maskTf"])
        P.op("pool", lambda e: e.affine_select(out=maskTf[:], in_=maskTf[:], pattern=[[-1, 128]], compare_op=ALU.is_gt,
                                               fill=0.0, base=0, channel_multiplier=1), reads=["maskTf"], writes=["maskTf"])
        P.op("dve", lambda e: e.tensor_copy(out=ident[:], in_=identf[:]), reads=["identf"], writes=["ident"])
        P.op("dve", lambda e: e.tensor_copy(out=maskT[:], in_=maskTf[:]), reads=["maskTf"], writes=["maskT"])
        P.op("pool", lambda e: e.memset(eps1[:], 1e-6), writes=["eps1"])
        P.op("pool", lambda e: e.memset(eps4[:], 4e-6), writes=["eps4"])
        P.op("pool", lambda e: e.memset(mhalf[:], -0.5), writes=["mhalf"])
        P.op("pool", lambda e: e.memset(S[:], 0.0), writes=["S"])
        P.op("pool", lambda e: e.memset(Sbf[:], 0.0), writes=["Sbf"])
        P.op("pool", lambda e: e.memset(raw[:], 0.0), writes=["raw"])
        P.op("pool", lambda e: e.memset(kT[:], 0.0), writes=["kT"])
        P.op("pool", lambda e: e.memset(vx[:], 0.0), writes=["vx"])

        P.op("act", lambda e: e.activation(out=cth[:], in_=ccol[:], func=AF.Tanh, scale=0.5), reads=["ccol"], writes=["cth"])
        P.op("dve", lambda e: e.scalar_tensor_tensor(out=condf[:], in0=cth[:], scalar=1.0, in1=ccol[:], op0=ALU.add, op1=ALU.mult),
             reads=["cth", "ccol"], writes=["condf"])
        P.op("dve", lambda e: e.tensor_scalar(out=condb[:], in0=condf[:], scalar1=0.5, scalar2=None, op0=ALU.mult),
             reads=["condf"], writes=["condb"])

        s_win = P.dma_sem("s_win")

        def load_w_in():
            P.dma("pool", [(lambda e, kk=kk: e.dma_start(out=w_in_bf[:, 2 * kk:2 * kk + 2, :],
                                                         in_=w_in[kk * 256:(kk + 1) * 256, :].rearrange("(k p) n -> p k n", p=128)))
                           for kk in range(4)], s_win, writes=["w_in"])

        s_ada = [P.dma_sem("s_ada%d" % i) for i in range(NSLOT)]
        modps, modps_r = psM[5], "psM5"
        for pc in range(24):
            sl = pc % NSLOT
            P.dma("pool", [lambda e, pc=pc, sl=sl: e.dma_start(
                out=wgu[sl][:], in_=ada_w[:, pc * 256:(pc + 1) * 256].rearrange("(k p) n -> p k n", p=128))],
                s_ada[sl], writes=["wgu%d" % sl])
            vec = pc // 4
            if vec in (2, 5):
                pi = (pc % 4) + (0 if vec == 2 else 4)
                pm, pr = psM[pi % 4], "psM%d" % (pi % 4)

                def f(e, sl=sl, pm=pm):
                    for k in range(8):
                        ins = e.matmul(pm[:, 0:256], lhsT=condb[:, k:k + 1].to_broadcast([128, 128]), rhs=wgu[sl][:, k, :],
                                       start=(k == 0), stop=(k == 7))
                    return ins
                P.op("pe", f, reads=["wgu%d" % sl, "condb"], writes=[pr])
                qd = pc % 4
                dst = gate1_b if vec == 2 else gate2h_b
                srcb = ab1 if vec == 2 else ab2
                P.op("dve", lambda e, pm=pm, qd=qd, dst=dst, srcb=srcb: e.tensor_tensor(
                    out=dst[:, qd * 256:(qd + 1) * 256], in0=pm[:, 0:256], in1=srcb[:, qd * 256:(qd + 1) * 256], op=ALU.add),
                    reads=[pr, "abrows"], writes=[("gate1_b%d" if vec == 2 else "gate2h_b%d") % qd])
            else:
                def f(e, sl=sl, pc=pc):
                    for jj in range(2):
                        j = pc * 2 + jj
                        for k in range(8):
                            ins = e.matmul(modps[:, j:j + 1], lhsT=wgu[sl][:, k, jj * 128:(jj + 1) * 128], rhs=condb[:, k:k + 1],
                                           start=(k == 0), stop=(k == 7))
                    return ins
                P.op("pe", f, reads=["wgu%d" % sl, "condb"], writes=[modps_r])
            if pc == 7:
                P.op("dve", lambda e: e.tensor_tensor(out=modc[:, 0:16], in0=modps[:, 0:16], in1=adabc[:, 0:16], op=ALU.add),
                     reads=[modps_r, "adabc"], writes=["modcA"])
                P.op("dve", lambda e: e.scalar_tensor_tensor(out=g1[:], in0=modc[:, 8:16], scalar=1.0, in1=n1c[:], op0=ALU.add, op1=ALU.mult),
                     reads=["modcA", "n1c"], writes=["g1"])
                P.op("dve", lambda e: e.tensor_copy(out=sh1[:], in_=modc[:, 0:8]), reads=["modcA"], writes=["sh1"])
                load_w_in()
            if pc == 19:
                P.op("dve", lambda e: e.tensor_tensor(out=modc[:, 24:40], in0=modps[:, 24:40], in1=adabc[:, 24:40], op=ALU.add),
                     reads=[modps_r, "adabc"], writes=["modcB"])
                P.op("dve", lambda e: e.scalar_tensor_tensor(out=g2[:], in0=modc[:, 32:40], scalar=1.0, in1=n2c[:], op0=ALU.add, op1=ALU.mult),
                     reads=["modcB", "n2c"], writes=["g2"])
                P.op("dve", lambda e: e.tensor_copy(out=sh2[:], in_=modc[:, 24:32]), reads=["modcB"], writes=["sh2"])
        P.op("dve", lambda e: e.tensor_scalar(out=gate2h_b, in0=gate2h_b, scalar1=0.5, scalar2=None, op0=ALU.mult),
             reads=["gate2h_b%d" % i for i in range(4)], writes=["gate2h_b"])
        P.op("dve", lambda e: e.tensor_scalar(out=convw[:], in0=convw[:], scalar1=0.5, scalar2=None, op0=ALU.mult), reads=["convw"], writes=["convw"])
        P.op("dve", lambda e: e.tensor_scalar(out=convb[:], in0=convb[:], scalar1=0.5, scalar2=None, op0=ALU.mult), reads=["convb"], writes=["convb"])
        def fdw(e):
            for c in range(8):
                for k in range(4):
                    ins = e.tensor_scalar(out=diagW[:, c, k, :], in0=identf[:], scalar1=convw[:, c, k:k + 1], scalar2=None, op0=ALU.mult)
            return ins
        P.op("dve", fdw, reads=["identf", "convw"], writes=["diagW"])
        P.op("act", lambda e: e.activation(out=A_b[:], in_=A_b[:], func=AF.Exp), reads=["A_b"], writes=["A_b"])
        P.op("dve", lambda e: e.tensor_scalar(out=A_b[:], in0=A_b[:], scalar1=-1.0, scalar2=None, op0=ALU.mult), reads=["A_b"], writes=["A_b"])
        P.op("dve", lambda e: e.tensor_scalar(out=nsink[:], in0=sink[:], scalar1=-1.0, scalar2=None, op0=ALU.mult), reads=["sink"], writes=["nsink"])
        P.op("dve", lambda e: e.tensor_scalar(out=maskb[:], in0=flag[:], scalar1=-NEG, scalar2=NEG, op0=ALU.mult, op1=ALU.add),
             reads=["flag"], writes=["maskb"])

        def fdg(e):
            for h in range(8):
                ins = e.tensor_scalar(out=Dg[:, h, :], in0=identf[:], scalar1=dsk[:, h:h + 1], scalar2=None, op0=ALU.mult)
            return ins
        P.op("dve", fdg, reads=["identf", "dsk"], writes=["Dg"])

        s_wo = P.dma_sem("s_wo")
        P.dma("pool", [(lambda e, kk=kk: e.dma_start(out=w_o_bf[:, 4 * kk:4 * kk + 4, :],
                                                     in_=w_o[kk * 512:(kk + 1) * 512, :].rearrange("(k p) n -> p k n", p=128)))
                       for kk in range(2)], s_wo, writes=["w_o_raw"])

        def fwo(e):
            for k in range(8):
                e.tensor_scalar(out=w_o_bf[:, k, :], in0=w_o_bf[:, k, :], scalar1=mnc[:, k:k + 1], scalar2=None, op0=ALU.mult)
            for k in range(8):
                ins = e.tensor_tensor(out=w_o_bf[:, k, :], in0=w_o_bf[:, k, :], in1=gate1_b, op=ALU.mult)
            return ins
        P.op("dve", fwo, reads=["w_o_raw", "mnc"] + ["gate1_b%d" % i for i in range(4)], writes=["w_o", "sc0"])
        s_scr = P.dma_sem("s_scr")
        P.dma("pool", [(lambda e, j=j, u=u: e.dma_start(out=wgu_s[j].rearrange("p (k n) -> p k n", k=8)[:, :, u * 128:(u + 1) * 128],
                                                        in_=w_gu[:, u * D_FF + j * 128:u * D_FF + (j + 1) * 128].rearrange("(k p) n -> p k n", p=128)))
                       for j in range(22) for u in range(2)], s_scr, writes=["scr_gu"])
        s_dnl = P.dma_sem("s_dnl")
        s_dns = P.dma_sem("s_dns")
        MB = ["mixb_a", "mixb_s"]
        for j in range(22):
            P.dma("pool", [lambda e, j=j: e.dma_start(out=mixb[:], in_=w_dn[j * 128:(j + 1) * 128, :])], s_dnl, writes=MB)
            P.op("dve", lambda e: e.tensor_tensor(out=mixb[:], in0=mixb[:], in1=gate2h_b, op=ALU.mult),
                 reads=MB + ["gate2h_b"], writes=MB + ["hT2"])
            P.dma("sp", [lambda e, j=j, nh=nh: e.dma_start(out=wdn_s[nh, j], in_=mixb[:, nh * 512:(nh + 1) * 512]) for nh in range(2)],
                  s_dns, reads=MB, writes=["scr_dn%d" % j])

        xsem = [P.dma_sem("xs%d" % i) for i in range(NXS)]
        osem = [P.dma_sem("os%d" % i) for i in range(NXS)]
        wsem = [P.dma_sem("ws%d" % i) for i in range(NSLOT)]

        def load_x(gi):
            pre = gi < NG
            src = x_pre if pre else x_main
            g = gi if pre else gi - NG
            sl = gi % NXS
            P.dma("sp", [lambda e, src=src, g=g, sl=sl: e.dma_start(
                out=xg[sl][:], in_=src[g * 256:(g + 1) * 256, :].rearrange("(t p) d -> p t d", p=128))],
                xsem[sl], writes=["x%d" % sl])

        dsem_w = [P.dma_sem("wd%d" % i) for i in range(NDSLOT)]

        def load_w(j):
            sl = j % NSLOT
            P.dma("sp", [lambda e, j=j, sl=sl: e.dma_start(out=wgu[sl][:], in_=wgu_s[j].rearrange("p (k n) -> p k n", k=8))],
                  wsem[sl], reads=["scr_gu"], writes=["wgu%d" % sl])

        def load_wd(idx):
            nh, j = idx // 22, idx % 22
            sl = idx % NDSLOT
            P.dma("sp", [lambda e, j=j, nh=nh, sl=sl: e.dma_start(out=wdn[sl][:], in_=wdn_s[nh, j])],
                  dsem_w[sl], reads=["scr_dn%d" % j], writes=["wdn%d" % sl])

        def rms_stats(src_ap, src_res, scale, col, eps_t):
            P.op("act", lambda e: e.activation(out=junk[:, 0:src_ap.shape[-1]], in_=src_ap, func=AF.Square, scale=scale,
                                               accum_out=ms[:, col:col + 1]),
                 reads=list(src_res) if isinstance(src_res, (list, tuple)) else [src_res], writes=["ms%d" % col])
            P.op("pool", lambda e: e.tensor_tensor(out=mse[:, col:col + 1], in0=ms[:, col:col + 1], in1=eps_t[:, 0:1], op=ALU.add),
                 reads=["ms%d" % col, "eps1", "eps4"], writes=["mse%d" % col])
            P.op("pool", lambda e: e.tensor_tensor(out=rstd[:, col:col + 1], in0=mse[:, col:col + 1], in1=mhalf[:, 0:1], op=ALU.pow),
                 reads=["mse%d" % col, "mhalf"], writes=["rstd%d" % col])

        def norm_transpose(xs_ap, xres, gvec, svec, dstT, dres, t):
            rms_stats(xs_ap, xres, 1.0 / 32.0, 0, eps1)
            P.op("dve", lambda e: e.tensor_scalar(out=xn[:], in0=xs_ap, scalar1=rstd[:, 0:1], scalar2=None, op0=ALU.mult),
                 reads=[xres, "rstd0"], writes=["xn"])
            pt, ptr = nextT()

            def ftr(e):
                for k in range(8):
                    ins = e.transpose(out=pt[:, k * 128:(k + 1) * 128], in_=xn[:, k * 128:(k + 1) * 128], identity=ident[:])
                return ins
            P.op("pe", ftr, reads=["xn", "ident"], writes=[ptr])

            def fev(e):
                for k in range(8):
                    ins = e.activation(out=dstT[:, k, t * 128:(t + 1) * 128], in_=pt[:, k * 128:(k + 1) * 128], func=AF.Identity,
                                       scale=gvec[:, k:k + 1], bias=svec[:, k:k + 1])
                return ins
            P.op("act", fev, reads=[ptr, "g1", "sh1", "g2", "sh2"], writes=[dres])

        import os
        STAGE = int(os.environ.get("KSTAGE", "9"))
        if STAGE >= 2:
            load_x(0)
            load_x(1)
        for gi in range(2 * NG):
            pre = gi < NG
            P.tag = gi
            if STAGE < 2 or (STAGE == 2 and not pre):
                break
            g_local = gi if pre else gi - NG
            sl = gi % NXS
            xr = "x%d" % sl
            xs = xg[sl]
            if gi + 2 < 2 * NG:
                load_x(gi + 2)
            if gi == NG:
                P.op("pool", lambda e: e.memset(fence_t[:], 0.0), reads=["gate2h_b", "abrows"],
                     writes=["fence_t", "hT2", "sc0"] + ["actT%d" % j for j in range(22)])
                P.op("dve", lambda e: e.tensor_scalar(out=S[:], in0=S[:], scalar1=flag[:, 0:1], scalar2=None, op0=ALU.mult),
                     reads=["S", "flag"], writes=["S"])
                P.op("act", lambda e: e.activation(out=Sbf[:], in_=S[:], func=AF.Copy), reads=["S"], writes=["Sbf"])
                P.op("dve", lambda e: e.tensor_scalar(out=raw[:, :, 0:3], in0=raw[:, :, 0:3], scalar1=flag[:, 0:1], scalar2=None, op0=ALU.mult),
                     reads=["raw", "flag"], writes=["raw"])
            if not pre:
                for j in range(NSLOT):
                    load_w(j)
                for j in range(NDSLOT):
                    load_wd(j)

            for t in range(2):
                norm_transpose(xs[:, t, :], xr, g1, sh1, hT, "hT", t)

            dump("hT", hT[:], ["hT"], (not pre) and g_local == 0)
            dump("g1", g1[:], ["g1"], (not pre) and g_local == 0); dump("sh1", sh1[:], ["sh1"], (not pre) and g_local == 0)
            dump("S0", S[:], ["S"], (not pre) and g_local == 0)
            def fm_chunks(cols_list, pm, pr):
                def f(e):
                    for ci, c0 in enumerate(cols_list):
                        for k in range(8):
                            ins = e.matmul(pm[:, ci * 256:(ci + 1) * 256], lhsT=w_in_bf[:, k, c0:c0 + 128], rhs=hT[:, k, :],
                                           start=(k == 0), stop=(k == 7))
                    return ins
                P.op("pe", f, reads=["w_in", "hT"], writes=[pr])

            if not pre:
                for i in range(2):
                    pm, pr = nextM()
                    fm_chunks([(2 * i) * 128, (2 * i + 1) * 128], pm, pr)
                    P.op("act", lambda e, pm=pm, i=i: e.activation(out=qT[:, 2 * i:2 * i + 2, :],
                                                                    in_=pm[:, 0:512].rearrange("p (c n) -> p c n", c=2), func=AF.Copy),
                         reads=[pr], writes=["qT"])
            pm, pr = nextM()
            fm_chunks([512], pm, pr)
            P.op("act", lambda e, pm=pm: e.activation(out=kT[:, 128:384], in_=pm[:, 0:256], func=AF.Copy), reads=[pr], writes=["kT"])
            for i in range(4):
                pm, pr = nextM()
                fm_chunks([1280 + (2 * i) * 128, 1280 + (2 * i + 1) * 128], pm, pr)
                P.op("act", lambda e, pm=pm, i=i: e.activation(out=raw[:, 2 * i:2 * i + 2, 3:259],
                                                                in_=pm[:, 0:512].rearrange("p (c n) -> p c n", c=2), func=AF.Copy),
                     reads=[pr], writes=["raw"])
            for i in range(4):
                b = i % 2
                pm, pr = nextM()

                def fcv(e, pm=pm, i=i):
                    for ci in range(2):
                        c = 2 * i + ci
                        for k in range(4):
                            ins = e.matmul(pm[:, ci * 256:(ci + 1) * 256], lhsT=diagW[:, c, k, :], rhs=raw[:, c, k:k + 256],
                                           start=(k == 0), stop=(k == 3))
                    return ins
                P.op("pe", fcv, reads=["diagW", "raw"], writes=[pr])

                def fcu(e, pm=pm, i=i, b=b):
                    for ci in range(2):
                        ins = e.activation(out=acc[b][:, ci, :], in_=pm[:, ci * 256:(ci + 1) * 256], func=AF.Identity,
                                           bias=convb[:, 2 * i + ci:2 * i + ci + 1], scale=1.0)
                    return ins
                P.op("act", fcu, reads=[pr, "convb"], writes=["acc%d" % b])
                P.op("act", lambda e, b=b: e.activation(out=cth2[b][:], in_=acc[b][:], func=AF.Tanh), reads=["acc%d" % b], writes=["cth2_%d" % b])
                P.op("dve", lambda e, i=i, b=b: e.scalar_tensor_tensor(out=cvo[:, 2 * i:2 * i + 2, :], in0=cth2[b][:], scalar=1.0, in1=acc[b][:],
                                                                       op0=ALU.add, op1=ALU.mult),
                     reads=["cth2_%d" % b, "acc%d" % b], writes=["cvo"])
            P.op("dve", lambda e: e.tensor_copy(out=raw[:, :, 0:3], in_=raw[:, :, 256:259]), reads=["raw"], writes=["raw"])

            dump("qT", qT[:], ["qT"], (not pre) and g_local == 0); dump("kT", kT[:], ["kT"], (not pre) and g_local == 0); dump("cvo", cvo[:], ["cvo"], (not pre) and g_local == 0)
            for t in range(2):
                tsl = slice(t * 128, (t + 1) * 128)
                pa, par = nextM()
                if pre:
                    def f(e, pa=pa, tsl=tsl):
                        for k in range(8):
                            e.matmul(pa[:, 0:128], lhsT=hT[:, k, tsl], rhs=w_in_bf[:, k, 640:768], start=(k == 0), stop=(k == 7))
                        for k in range(8):
                            ins = e.matmul(pa[:, 128:136], lhsT=hT[:, k, tsl], rhs=w_in_bf[:, k, 2304:2312], start=(k == 0), stop=(k == 7))
                        return ins
                    P.op("pe", f, reads=["hT", "w_in"], writes=[par])
                    P.op("act", lambda e, pa=pa, t=t: e.activation(out=vx[:, 1 + t, :], in_=pa[:, 0:128], func=AF.Copy), reads=[par], writes=["vx"])
                    P.op("dve", lambda e, pa=pa, t=t: e.tensor_tensor(out=dtu[:, t * 8:(t + 1) * 8], in0=pa[:, 128:136], in1=dtb[:], op=ALU.add),
                         reads=[par, "dtb"], writes=["dtu"])
                else:
                    pb2, pbr = nextM()

                    def f(e, pa=pa, pb2=pb2, tsl=tsl):
                        for k in range(8):
                            e.matmul(pa[:, 0:512], lhsT=hT[:, k, tsl], rhs=w_in_bf[:, k, 640:1152], start=(k == 0), stop=(k == 7))
                        for k in range(8):
                            e.matmul(pb2[:, 0:128], lhsT=hT[:, k, tsl], rhs=w_in_bf[:, k, 1152:1280], start=(k == 0), stop=(k == 7))
                        for k in range(8):
                            ins = e.matmul(pb2[:, 128:136], lhsT=hT[:, k, tsl], rhs=w_in_bf[:, k, 2304:2312], start=(k == 0), stop=(k == 7))
                        return ins
                    P.op("pe", f, reads=["hT", "w_in"], writes=[par, pbr])
                    P.op("act", lambda e, pa=pa, t=t: e.activation(out=vx[:, 1 + t, :], in_=pa[:, 0:128], func=AF.Copy), reads=[par], writes=["vx"])

                    def fth(e, pa=pa, pb2=pb2):
                        e.activation(out=yt[:, 0:384], in_=pa[:, 128:512], func=AF.Tanh, scale=0.5)
                        return e.activation(out=yt[:, 384:512], in_=pb2[:, 0:128], func=AF.Tanh, scale=0.5)
                    P.op("act", fth, reads=[par, pbr], writes=["yt"])

                    def fgz(e, pa=pa, pb2=pb2, t=t):
                        e.scalar_tensor_tensor(out=gz[:, t, 0:384], in0=yt[:, 0:384], scalar=1.0, in1=pa[:, 128:512], op0=ALU.add, op1=ALU.mult)
                        return e.scalar_tensor_tensor(out=gz[:, t, 384:512], in0=yt[:, 384:512], scalar=1.0, in1=pb2[:, 0:128],
                                                      op0=ALU.add, op1=ALU.mult)
                    P.op("dve", fgz, reads=["yt", par, pbr], writes=["gz"])
                    P.op("dve", lambda e, pb2=pb2, t=t: e.tensor_tensor(out=dtu[:, t * 8:(t + 1) * 8], in0=pb2[:, 128:136], in1=dtb[:], op=ALU.add),
                         reads=[pbr, "dtb"], writes=["dtu"])

            au, tt, dd, ww, w2s, rr_, lnp, relu = sp_t
            P.op("act", lambda e: e.activation(out=au[:], in_=dtu[:], func=AF.Abs), reads=["dtu"], writes=["sp_au"])
            P.op("act", lambda e: e.activation(out=tt[:], in_=au[:], func=AF.Exp, scale=-1.0), reads=["sp_au"], writes=["sp_tt"])
            P.op("dve", lambda e: e.tensor_scalar(out=dd[:], in0=tt[:], scalar1=2.0, scalar2=None, op0=ALU.add), reads=["sp_tt"], writes=["sp_dd"])
            P.op("dve", lambda e: e.reciprocal(out=dd[:], in_=dd[:]), reads=["sp_dd"], writes=["sp_dd"])
            P.op("dve", lambda e: e.tensor_tensor(out=ww[:], in0=tt[:], in1=dd[:], op=ALU.mult), reads=["sp_tt", "sp_dd"], writes=["sp_ww"])
            P.op("dve", lambda e: e.tensor_tensor(out=w2s[:], in0=ww[:], in1=ww[:], op=ALU.mult), reads=["sp_ww"], writes=["sp_w2"])
            P.op("dve", lambda e: e.tensor_scalar(out=rr_[:], in0=w2s[:], scalar1=1.0 / 13.0, scalar2=None, op0=ALU.mult), reads=["sp_w2"], writes=["sp_rr"])
            for cst in (1.0 / 11.0, 1.0 / 9.0, 1.0 / 7.0, 1.0 / 5.0, 1.0 / 3.0):
                P.op("dve", lambda e, cst=cst: e.scalar_tensor_tensor(out=rr_[:], in0=rr_[:], scalar=cst, in1=w2s[:], op0=ALU.add, op1=ALU.mult),
                     reads=["sp_rr", "sp_w2"], writes=["sp_rr"])
            P.op("dve", lambda e: e.scalar_tensor_tensor(out=lnp[:], in0=rr_[:], scalar=1.0, in1=ww[:], op0=ALU.add, op1=ALU.mult),
                 reads=["sp_rr", "sp_ww"], writes=["sp_ln"])
            P.op("dve", lambda e: e.tensor_scalar(out=relu[:], in0=dtu[:], scalar1=0.0, scalar2=None, op0=ALU.max), reads=["dtu"], writes=["sp_relu"])
            P.op("dve", lambda e: e.scalar_tensor_tensor(out=dtv[:], in0=lnp[:], scalar=2.0, in1=relu[:], op0=ALU.mult, op1=ALU.add),
                 reads=["sp_ln", "sp_relu"], writes=["dtv"])

            dump("vx", vx[:], ["vx"], (not pre) and g_local == 0); dump("gz", gz[:], ["gz"], (not pre) and g_local == 0); dump("dtv", dtv[:], ["dtv"], (not pre) and g_local == 0)
            def phaseC(t):
                par = t % NB
                LT, WT, xdt, xdd, xsb, Btok, sc = LT_l[par], WT_l[par], xdt_l[par], xdd_l[par], xsb_l[par], Btok_l[par], sc_l[par]
                rLT, rWT, rxdt, rxdd, rxsb, rBtok, rsc = ["%s%d" % (nm, par) for nm in ("LT", "WT", "xdt", "xdd", "xsb", "Btok", "sc")]
                tsl = slice(t * 128, (t + 1) * 128)
                gt_first = (not pre) and g_local == 0 and t == 0
                dt_ap = dtv[:, t * 8:(t + 1) * 8]
                P.op("dve", lambda e, dt_ap=dt_ap: e.tensor_tensor(out=a_t[:], in0=dt_ap, in1=A_b[:], op=ALU.mult), reads=["dtv", "A_b"], writes=["a_t"])
                pc_, pcr = nextM()

                def fcs(e, pc_=pc_):
                    e.matmul(pc_[:, 0:8], lhsT=tri[:], rhs=a_t[:], start=True, stop=True)
                    return e.matmul(pc_[:, 8:16], lhsT=ones[:], rhs=a_t[:], start=True, stop=True)
                P.op("pe", fcs, reads=["tri", "ones", "a_t"], writes=[pcr])
                P.op("dve", lambda e, pc_=pc_: e.tensor_copy(out=e_in[:, 0:16], in_=pc_[:, 0:16]), reads=[pcr], writes=["e_in"])
                P.op("dve", lambda e: e.tensor_tensor(out=e_in[:, 16:24], in0=e_in[:, 8:16], in1=e_in[:, 0:8], op=ALU.subtract),
                     reads=["e_in"], writes=["e_in2"])
                P.op("act", lambda e: e.activation(out=e24[:], in_=e_in[:], func=AF.Exp), reads=["e_in", "e_in2"], writes=["e24"])
                P.op("dve", lambda e, dt_ap=dt_ap: e.tensor_tensor(out=w2[:], in0=dt_ap, in1=e24[:, 16:24], op=ALU.mult), reads=["dtv", "e24"], writes=["w2"])
                pt, ptr = nextT()

                def ftx(e, pt=pt, tsl=tsl):
                    for c in range(6):
                        ins = e.transpose(out=pt[:, c * 128:(c + 1) * 128], in_=cvo[:, c, tsl], identity=ident[:])
                    return ins
                P.op("pe", ftx, reads=["cvo", "ident"], writes=[ptr])
                P.op("dve", lambda e, pt=pt: e.tensor_tensor(out=xdd[:].rearrange("p (h d) -> p h d", h=8),
                                                            in0=pt[:, 0:512].rearrange("p (h d) -> p h d", h=8),
                                                            in1=w2[:].unsqueeze(2).to_broadcast([128, 8, 64]), op=ALU.mult),
                     reads=[ptr, "w2"], writes=[rxdd])
                P.op("act", lambda e, pt=pt: e.activation(out=Btok[:], in_=pt[:, 512:768], func=AF.Copy), reads=[ptr], writes=[rBtok])
                if (not pre) and STAGE not in (31, 33):
                    P.op("dve", lambda e, pt=pt, dt_ap=dt_ap: e.tensor_tensor(out=xdt[:].rearrange("p (h d) -> p h d", h=8),
                                                                              in0=pt[:, 0:512].rearrange("p (h d) -> p h d", h=8),
                                                                              in1=dt_ap.unsqueeze(2).to_broadcast([128, 8, 64]), op=ALU.mult),
                         reads=[ptr, "dtv"], writes=[rxdt])
                    P.op("act", lambda e, pt=pt: e.activation(out=xsb[:], in_=pt[:, 0:512], func=AF.Copy), reads=[ptr, rxdt, rxdd], writes=[rxsb])
                    P.op("dve", lambda e: e.tensor_scalar(out=nAcs[:], in0=e_in[:, 0:8], scalar1=-1.0, scalar2=None, op0=ALU.mult),
                         reads=["e_in"], writes=["nAcs"])
                    pcb, pcbr = nextM()

                    def fcb(e, pcb=pcb, tsl=tsl):
                        for g in range(2):
                            ins = e.matmul(pcb[:, g * 128:(g + 1) * 128], lhsT=cvo[:, 4 + g, tsl], rhs=cvo[:, 6 + g, tsl], start=True, stop=True)
                        return ins
                    P.op("pe", fcb, reads=["cvo"], writes=[pcbr])
                    pl = [nextM(), nextM()]

                    def fl(e, pl=pl, pcb=pcb):
                        for h in range(8):
                            o = pl[h // 4][0][:, (h % 4) * 128:(h % 4 + 1) * 128]
                            ins = e.matmul(o, lhsT=a_t[:, h:h + 1].to_broadcast([128, 128]), rhs=tri[:], start=True, stop=True)
                        return ins
                    P.op("pe", fl, reads=["a_t", "tri", "ident"], writes=[pl[0][1], pl[1][1]])
                    scv = sc[:].rearrange("p a b -> p (a b)").rearrange("p (h l) -> p h l", h=8)

                    def fl2(e, pl=pl, scv=scv):
                        for h in range(8):
                            ins = e.scalar_tensor_tensor(out=scv[:, h, :], in0=pl[h // 4][0][:, (h % 4) * 128:(h % 4 + 1) * 128],
                                                         scalar=nAcs[:, h:h + 1], in1=maskTf[:], op0=ALU.add, op1=ALU.add)
                        return ins
                    P.op("dve", fl2, reads=[pl[0][1], pl[1][1], "nAcs", "maskTf"], writes=[rsc])
                    P.op("act", lambda e, scv=scv: e.activation(out=LT[:], in_=scv, func=AF.Exp), reads=[rsc], writes=[rLT])

                    def fwt(e, pcb=pcb):
                        for h in range(8):
                            g = h // 4
                            ins = e.tensor_tensor(out=WT[:, h, :], in0=pcb[:, g * 128:(g + 1) * 128], in1=LT[:, h, :], op=ALU.mult)
                        return ins
                    P.op("dve", fwt, reads=[pcbr, rLT], writes=[rWT])
                    py, pyr = nextM()

                    SKIPB = (STAGE == 34)
                    if SKIPB:
                        P.op = lambda *a, **k: None

                    def fy(e, py=py):
                        for h in range(8):
                            hs = slice(h * 64, (h + 1) * 64)
                            e.matmul(py[:, hs], lhsT=WT[:, h, :], rhs=xdt[:, hs], start=True, stop=False)
                            ins = e.matmul(py[:, hs], lhsT=Dg[:, h, :], rhs=xsb[:, hs], start=False, stop=True)
                        return ins
                    P.op("pe", fy, reads=[rWT, rxdt, "Dg", rxsb], writes=[pyr])
                    pyo, pyor = nextM()

                    def fyo(e, pyo=pyo, tsl=tsl):
                        for g in range(2):
                            ins = e.matmul(pyo[:, g * 256:(g + 1) * 256], lhsT=cvo[:, 6 + g, tsl], rhs=Sbf[:, g * 256:(g + 1) * 256], start=True, stop=True)
                        return ins
                    P.op("pe", fyo, reads=["cvo", "Sbf"], writes=[pyor])
                    P.op("dve", lambda e, pyo=pyo: e.tensor_tensor(out=yt[:].rearrange("p (h d) -> p h d", h=8),
                                                                  in0=pyo[:, 0:512].rearrange("p (h d) -> p h d", h=8),
                                                                  in1=e24[:, 0:8].unsqueeze(2).to_broadcast([128, 8, 64]), op=ALU.mult),
                         reads=[pyor, "e24"], writes=["yt"])
                    P.op("dve", lambda e, py=py: e.tensor_tensor(out=yt[:], in0=py[:, 0:512], in1=yt[:], op=ALU.add), reads=[pyr, "yt"], writes=["yt"])
                    P.op("dve", lambda e, t=t: e.tensor_tensor(out=yg[:], in0=yt[:], in1=gz[:, t, :], op=ALU.mult), reads=["yt", "gz"], writes=["yg"])
                    for g in range(2):
                        rms_stats(yg[:, g * 256:(g + 1) * 256], "yg", 1.0 / 16.0, 1 + g, eps4)

                    def fmx(e):
                        for g in range(2):
                            ins = e.tensor_scalar(out=mixb[:, 512 + g * 256:512 + (g + 1) * 256], in0=yg[:, g * 256:(g + 1) * 256],
                                                  scalar1=rstd[:, 1 + g:2 + g], scalar2=None, op0=ALU.mult)
                        return ins
                    P.op("dve", fmx, reads=["yg", "rstd1", "rstd2"], writes=["mixb_s"])
                    if SKIPB:
                        del P.op
                if (not pre) and STAGE not in (31, 33):
                    dump("e24", e24[:], ["e24"], (not pre) and g_local == 0 and t == 0); dump(rLT, LT[:], [rLT], (not pre) and g_local == 0 and t == 0)
                    dump(rWT, WT[:], [rWT], (not pre) and g_local == 0 and t == 0); dump("yt", yt[:], ["yt"], (not pre) and g_local == 0 and t == 0)
                    dump("yg", yg[:], ["yg"], (not pre) and g_local == 0 and t == 0)
                pst, pstr = nextM()

                def fst(e, pst=pst):
                    for g in range(2):
                        ins = e.matmul(pst[:, g * 256:(g + 1) * 256], lhsT=Btok[:, g * 128:(g + 1) * 128], rhs=xdd[:, g * 256:(g + 1) * 256], start=True, stop=True)
                    return ins
                P.op("pe", fst, reads=[rBtok, rxdd], writes=[pstr])
                P.op("pool", lambda e: e.tensor_tensor(out=S[:].rearrange("p (h d) -> p h d", h=8), in0=S[:].rearrange("p (h d) -> p h d", h=8),
                                                       in1=e24[:, 8:16].unsqueeze(2).to_broadcast([128, 8, 64]), op=ALU.mult),
                     reads=["S", "e24"], writes=["S"])
                P.op("dve", lambda e, pst=pst: e.tensor_tensor(out=S[:], in0=pst[:, 0:512], in1=S[:], op=ALU.add), reads=[pstr, "S"], writes=["S"])
                P.op("act", lambda e: e.activation(out=Sbf[:], in_=S[:], func=AF.Copy), reads=["S"], writes=["Sbf"])

                if pre or STAGE in (31, 32, 34):
                    return
                def attn(hp):
                    sc, pb, pTs = sc_l[hp % NB], pb_l[hp % NB], pTs_l[hp % NB]
                    rsc, rpb, rpTs = ["%s%d" % (nm, hp % NB) for nm in ("sc", "pb", "pTs")]
                    ksl = slice(hp * 64, (hp + 1) * 64)
                    pS = [nextM(), nextM()]

                    def fsc(e, pS=pS, ksl=ksl, t=t, tsl=tsl):
                        for c in range(4):
                            ins = e.matmul(pS[c // 2][0][:, (c % 2) * 256:(c % 2 + 1) * 256], lhsT=qT[ksl, c, tsl],
                                           rhs=kT[ksl, t * 128:t * 128 + 256], start=True, stop=True)
                        return ins
                    P.op("pe", fsc, reads=["qT", "kT"], writes=[pS[0][1], pS[1][1]])

                    def fsb(e, pS=pS, hp=hp):
                        for i in range(2):
                            ins = e.scalar_tensor_tensor(out=sc[:, 2 * i:2 * i + 2, :], in0=pS[i][0][:, 0:512].rearrange("p (c n) -> p c n", c=2),
                                                         scalar=0.125, in1=biasf[:, 4 * hp + 2 * i:4 * hp + 2 * i + 2, :], op0=ALU.mult, op1=ALU.add)
                        return ins
                    P.op("dve", fsb, reads=[pS[0][1], pS[1][1], "biasf"], writes=[rsc])
                    if gt_first:
                        P.op("dve", lambda e: e.tensor_scalar(out=sc[:, :, 0:128], in0=sc[:, :, 0:128], scalar1=maskb[:, 0:1], scalar2=None, op0=ALU.add),
                             reads=[rsc, "maskb"], writes=[rsc])
                    P.op("dve", lambda e: e.tensor_reduce(out=rmax[:], in_=sc[:], axis=AX.X, op=ALU.max), reads=[rsc], writes=["rmax"])
                    P.op("dve", lambda e, hp=hp: e.scalar_tensor_tensor(out=negm[:], in0=rmax[:], scalar=-1.0, in1=nsink[:, 4 * hp:4 * hp + 4],
                                                                       op0=ALU.mult, op1=ALU.min),
                         reads=["rmax", "nsink"], writes=["negm"])

                    def fex(e):
                        for c in range(4):
                            ins = e.activation(out=pb[:, c, :], in_=sc[:, c, :], func=AF.Exp, bias=negm[:, c:c + 1], scale=1.0,
                                               accum_out=rsum[:, c:c + 1])
                        return ins
                    P.op("act", fex, reads=[rsc, "negm"], writes=[rpb, "rsum"])
                    P.op("dve", lambda e, hp=hp: e.tensor_tensor(out=stmp[:], in0=sink[:, 4 * hp:4 * hp + 4], in1=negm[:], op=ALU.add),
                         reads=["sink", "negm"], writes=["stmp"])
                    P.op("act", lambda e: e.activation(out=es[:], in_=stmp[:], func=AF.Exp), reads=["stmp"], writes=["es"])
                    P.op("dve", lambda e: e.tensor_tensor(out=den[:], in0=rsum[:], in1=es[:], op=ALU.add), reads=["rsum", "es"], writes=["den"])
                    P.op("dve", lambda e: e.reciprocal(out=rden[:], in_=den[:]), reads=["den"], writes=["rden"])
                    pt, ptr = nextT()

                    def ftp(e, pt=pt):
                        for c in range(4):
                            for j in range(2):
                                ins = e.transpose(out=pt[:, (2 * c + j) * 128:(2 * c + j + 1) * 128], in_=pb[:, c, j * 128:(j + 1) * 128], identity=ident[:])
                        return ins
                    P.op("pe", ftp, reads=[rpb, "ident"], writes=[ptr])
                    P.op("act", lambda e, pt=pt: e.activation(out=pTs[:], in_=pt[:, 0:1024].rearrange("p (c n) -> p c n", c=8), func=AF.Copy),
                         reads=[ptr], writes=[rpTs])
                    po, por = nextM()

                    def fpv(e, po=po, hp=hp, t=t):
                        for c in range(4):
                            e.matmul(po[:, c * 64:(c + 1) * 64], lhsT=pTs[:, 2 * c, :], rhs=vx[:, t, hp * 64:(hp + 1) * 64], start=True, stop=False)
                            ins = e.matmul(po[:, c * 64:(c + 1) * 64], lhsT=pTs[:, 2 * c + 1, :], rhs=vx[:, t + 1, hp * 64:(hp + 1) * 64], start=False, stop=True)
                        return ins
                    P.op("pe", fpv, reads=[rpTs, "vx"], writes=[por])
                    P.op("dve", lambda e, po=po, hp=hp: e.tensor_tensor(out=ya[:, hp * 256:(hp + 1) * 256].rearrange("p (h d) -> p h d", h=4),
                                                                       in0=po[:, 0:256].rearrange("p (h d) -> p h d", h=4),
                                                                       in1=rden[:].unsqueeze(2).to_broadcast([128, 4, 64]), op=ALU.mult),
                         reads=[por, "rden"], writes=["ya%d" % hp])
                for hp in range(2):
                    attn(hp)
                rms_stats(ya[:], ["ya0", "ya1"], 512.0 ** -0.5, 3, eps1)
                P.op("dve", lambda e: e.tensor_scalar(out=mixb[:, 0:512], in0=ya[:], scalar1=rstd[:, 3:4], scalar2=None, op0=ALU.mult),
                     reads=["ya0", "ya1", "rstd3", "xn"], writes=["mixb_a"])
                dump("ya", ya[:], ["ya0", "ya1"], (not pre) and g_local == 0 and t == 0); dump("mixb", mixb[:], ["mixb_a", "mixb_s"], (not pre) and g_local == 0 and t == 0)
                pt, ptr = nextT()

                def ftm(e, pt=pt):
                    for k in range(8):
                        ins = e.transpose(out=pt[:, k * 128:(k + 1) * 128], in_=mixb[:, k * 128:(k + 1) * 128], identity=ident[:])
                    return ins
                P.op("pe", ftm, reads=["mixb_a", "mixb_s", "ident"], writes=[ptr])
                P.op("act", lambda e, pt=pt, tsl=tsl: e.activation(out=hT[:, :, tsl], in_=pt[:, 0:1024].rearrange("p (c n) -> p c n", c=8), func=AF.Copy),
                     reads=[ptr], writes=["hT"])

            for t in range(2):
                phaseC(t)
            P.op("dve", lambda e: e.tensor_copy(out=kT[:, 0:128], in_=kT[:, 256:384]), reads=["kT"], writes=["kT"])
            P.op("dve", lambda e: e.tensor_copy(out=vx[:, 0, :], in_=vx[:, 2, :]), reads=["vx"], writes=["vx"])
            if pre or STAGE in (3, 31, 32, 33, 34):
                continue

            for t in range(2):
                tsl = slice(t * 128, (t + 1) * 128)
                for nh in range(2):
                    pm, pr = nextM()

                    def fo(e, pm=pm, nh=nh, tsl=tsl):
                        for k in range(8):
                            ins = e.matmul(pm[:, 0:512], lhsT=hT[:, k, tsl], rhs=w_o_bf[:, k, nh * 512:(nh + 1) * 512], start=(k == 0), stop=(k == 7))
                        return ins
                    P.op("pe", fo, reads=["hT", "w_o"], writes=[pr])
                    P.op("dve", lambda e, pm=pm, nh=nh, t=t, xs=xs: e.tensor_tensor(out=xs[:, t, nh * 512:(nh + 1) * 512], in0=pm[:, 0:512],
                                                                               in1=xs[:, t, nh * 512:(nh + 1) * 512], op=ALU.add),
                         reads=[pr, xr], writes=[xr])
                norm_transpose(xs[:, t, :], xr, g2, sh2, hT2, "hT2", t)

            if STAGE == 4:
                continue
            dump("x1", xs[:], [xr], (not pre) and g_local == 0); dump("h2T", hT2[:], ["hT2"], (not pre) and g_local == 0); dump("hT", hT[:], ["hT"], (not pre) and g_local == 0)
            P.prio = 1
            for j in range(22):
                sl_w = j % NSLOT
                b = j % 2
                pg, pgr = psM[b], "psM%d" % b

                def fgu(e, pg=pg, sl_w=sl_w):
                    for k in range(8):
                        e.matmul(pg[:, 0:256], lhsT=wgu[sl_w][:, k, 0:128], rhs=hT2[:, k, :], start=(k == 0), stop=(k == 7))
                    for k in range(8):
                        ins = e.matmul(pg[:, 256:512], lhsT=wgu[sl_w][:, k, 128:256], rhs=hT2[:, k, :], start=(k == 0), stop=(k == 7))
                    return ins
                P.op("pe", fgu, reads=["wgu%d" % sl_w, "hT2"], writes=[pgr])
                P.op("act", lambda e, pg=pg, b=b: e.activation(out=thg[b][:], in_=pg[:, 0:256], func=AF.Tanh, scale=0.5), reads=[pgr], writes=["thg%d" % b])
                P.op("act", lambda e, pg=pg, b=b: e.activation(out=ub[b][:], in_=pg[:, 256:512], func=AF.Copy), reads=[pgr], writes=["ub_%d" % b])
                P.op("dve", lambda e, pg=pg, b=b: e.scalar_tensor_tensor(out=t2[b][:], in0=thg[b][:], scalar=1.0, in1=pg[:, 0:256], op0=ALU.add, op1=ALU.mult),
                     reads=["thg%d" % b, pgr], writes=["t2_%d" % b])
                P.op("pool", lambda e, b=b, j=j: e.tensor_tensor(out=actT[:, j, :], in0=t2[b][:], in1=ub[b][:], op=ALU.mult),
                     reads=["t2_%d" % b, "ub_%d" % b], writes=["actT%d" % j])
                if j + NSLOT < 22:
                    load_w(j + NSLOT)
            for nh in range(2):
                for j in range(22):
                    idx = nh * 22 + j
                    sl_d = idx % NDSLOT

                    def fdn(e, sl_d=sl_d, j=j):
                        for t in range(2):
                            ins = e.matmul(psM[t][:, 0:512], lhsT=actT[:, j, t * 128:(t + 1) * 128], rhs=wdn[sl_d][:],
                                           start=(j == 0), stop=(j == 21))
                        return ins
                    P.op("pe", fdn, reads=["actT%d" % j, "wdn%d" % sl_d], writes=["psM0", "psM1"])
                    if idx + NDSLOT < 44:
                        load_wd(idx + NDSLOT)
                for t in range(2):
                    P.op("dve", lambda e, nh=nh, t=t, xs=xs: e.tensor_tensor(out=xs[:, t, nh * 512:(nh + 1) * 512], in0=psM[t][:, 0:512],
                                                                            in1=xs[:, t, nh * 512:(nh + 1) * 512], op=ALU.add),
                         reads=["psM%d" % t, xr], writes=[xr])
            for t in range(2):
                rms_stats(xs[:, t, :], xr, 1.0 / 32.0, 4, eps1)
                P.op("dve", lambda e, t=t, xs=xs: e.scalar_tensor_tensor(out=xs[:, t, :], in0=xs[:, t, :], scalar=rstd[:, 4:5], in1=fnorm[:], op0=ALU.mult, op1=ALU.mult),
                     reads=[xr, "rstd4", "fnorm"], writes=[xr])
            P.dma("sp", [lambda e, g_local=g_local, xs=xs: e.dma_start(out=out[g_local * 256:(g_local + 1) * 256, :].rearrange("(t p) d -> p t d", p=128), in_=xs[:])],
                  osem[sl], reads=[xr])
            P.prio = 0

        P.finish("sp")
        P.emit_all()
    return nc


def _t5_buckets(dist):
    n = np.maximum(dist, 0)
    max_exact = 16
    large = max_exact + (np.log(np.maximum(n, 1) / max_exact) / np.log(128 / max_exact) * (32 - max_exact)).astype(np.int32)
    large = np.minimum(large, 31)
    return np.where(n < max_exact, n, large).astype(np.int32)


def _col(v, nchunk):
    return np.ascontiguousarray(np.asarray(v, np.float32).reshape(nchunk, 128).T)


def _bc(v):
    return np.ascontiguousarray(np.broadcast_to(np.asarray(v, np.float32)[None, :], (128, len(v))))


_NC_CACHE = {}
_DBG = False
_LAST = None


def kernel(x, c, ada_w, ada_b, norm1, w_in, conv_w, conv_b, dt_bias, A_log, D_skip, sinks,
           attn_out_norm, ssm_out_norm, w_o, norm2, w_gate_up, w_down, rel_bias, final_norm):
    x = np.asarray(x, np.float32)
    B, S_, D = x.shape
    ntok = S_ // 2
    f = lambda a: np.asarray(a, np.float32)
    ada_b0 = f(ada_b)[0]
    dist = np.arange(128)[:, None] + 128 - np.arange(256)[None, :]
    valid = (dist >= 0) & (dist < 128)
    gathered = f(rel_bias)[_t5_buckets(dist)]
    biasf = np.where(valid[:, :, None], gathered, np.float32(NEG)).astype(np.float32)
    biasf = np.ascontiguousarray(np.transpose(biasf, (0, 2, 1)))
    perm = []
    for cch in range(4):
        perm += list(range(cch * 64, cch * 64 + 64)) + list(range((cch + 4) * 64, (cch + 4) * 64 + 64))
    w_in0 = f(w_in)[0]
    w_in_p = np.ascontiguousarray(np.concatenate([w_in0[:, perm], w_in0[:, 512:]], axis=1))
    shared = {
        "ada_w": np.ascontiguousarray(f(ada_w)[0]),
        "adab_col": _col(ada_b0, 48),
        "adab_g1": _bc(ada_b0[2048:3072]),
        "adab_g2": _bc(ada_b0[5120:6144]),
        "norm1_col": _col(f(norm1)[0], 8),
        "norm2_col": _col(f(norm2)[0], 8),
        "mixnorm_col": _col(np.concatenate([f(attn_out_norm)[0], f(ssm_out_norm)[0]]), 8),
        "w_in": w_in_p,
        "convw_col": np.ascontiguousarray(np.transpose(f(conv_w)[0].reshape(4, 8, 128), (2, 1, 0))),
        "convb_col": _col(f(conv_b)[0], 8),
        "dtb_b": _bc(f(dt_bias)[0]),
        "alog_b": _bc(f(A_log)[0]),
        "dskip_b": _bc(f(D_skip)[0]),
        "sinks_b": _bc(f(sinks)[0]),
        "w_o": np.ascontiguousarray(f(w_o)[0]),
        "w_gu": np.ascontiguousarray(f(w_gate_up)[0]),
        "w_dn": np.ascontiguousarray(f(w_down)[0]),
        "biasf": biasf,
        "fnorm_b": _bc(f(final_norm)),
    }
    in_maps = []
    for core in range(8):
        b, half = core // 2, core % 2
        m = dict(shared)
        m["x_main"] = np.ascontiguousarray(x[b, half * ntok:(half + 1) * ntok])
        m["x_pre"] = np.ascontiguousarray(x[b, 0:ntok]) if half == 1 else np.zeros((ntok, D), np.float32)
        m["c_col"] = _col(f(c)[b], 8)
        m["flag"] = np.full((128, 1), float(half), np.float32)
        in_maps.append(m)
    if ntok not in _NC_CACHE:
        _NC_CACHE[ntok] = build_nc(ntok, dbg=_DBG)
    res = run_bass_kernel_spmd(_NC_CACHE[ntok], in_maps, core_ids=list(range(8)))
    if _DBG:
        global _LAST
        _LAST = res.results
    outp = np.empty((B, S_, D), np.float32)
    for core in range(8):
        b, half = core // 2, core % 2
        outp[b, half * ntok:(half + 1) * ntok] = res.results[core]["out"]
    return outp
```
